# Optimizing a Trainium2 kernel written in Bass

```python
import math
import jax, jax.numpy as jnp
from jax import lax
import numpy as np

D_MODEL = 2048
BATCH = 4
SEQ = 8192
DEPTH = 1

CHUNK = 64
Q_BLOCK = 128
CONV_WIDTH = D_MODEL // 2
CONV_KERNEL = 31
HEAD_DIM = 128
V_HEAD_DIM = 2 * HEAD_DIM
N_HEADS = D_MODEL // (4 * HEAD_DIM)
ATTN_QK_WIDTH = N_HEADS * 2 * HEAD_DIM
ATTN_V_WIDTH = N_HEADS * V_HEAD_DIM
N_BRANCHES = 2
D_FF = 4 * D_MODEL
ROPE_THETA = 10000.0
EPS = 1e-6
IN_WIDTH = 2 * CONV_WIDTH + 2 * ATTN_QK_WIDTH + ATTN_V_WIDTH + N_BRANCHES * D_MODEL

kernel_name = "hybrid_conformer_diffattn_gated_block"


def rmsnorm(t, g):
    tf = t.astype(jnp.float32)
    y = tf * lax.rsqrt(jnp.mean(tf * tf, axis=-1, keepdims=True) + EPS)
    return (y * g.astype(jnp.float32)).astype(t.dtype)


def rope(t, cos, sin):
    half = t.shape[-1] // 2
    tf = t.astype(jnp.float32)
    t1, t2 = tf[..., :half], tf[..., half:]
    return jnp.concatenate([t1 * cos - t2 * sin, t2 * cos + t1 * sin], axis=-1).astype(t.dtype)


def lambda_init_fn(layer_idx):
    return 0.8 - 0.6 * math.exp(-0.3 * layer_idx)


def conformer_conv_branch(conv_a, conv_b_in, conv_w, conv_b, conv_norm_g, w_conv_out):
    glu = conv_a * jax.nn.sigmoid(conv_b_in)
    y = lax.conv_general_dilated(
        glu, conv_w[:, None, :].astype(glu.dtype), window_strides=(1,),
        padding=[(CONV_KERNEL - 1, 0)],
        dimension_numbers=("NWC", "WIO", "NWC"),
        feature_group_count=CONV_WIDTH) + conv_b
    y = jax.nn.silu(rmsnorm(y, conv_norm_g))
    return y @ w_conv_out


def diff_attention_branch(q, k, v, cos, sin, q_norm_g, k_norm_g,
                          lambda_q1, lambda_k1, lambda_q2, lambda_k2, subln_g, w_attn_out, lam_init):
    B, S, _ = q.shape
    nb = S // Q_BLOCK
    q = q.reshape(B, S, N_HEADS, 2, HEAD_DIM)
    k = k.reshape(B, S, N_HEADS, 2, HEAD_DIM)
    q = rope(rmsnorm(q, q_norm_g), cos, sin)
    k = rope(rmsnorm(k, k_norm_g), cos, sin)
    q = jnp.transpose(q, (0, 2, 3, 1, 4))
    k = jnp.transpose(k, (0, 2, 3, 1, 4))
    v = jnp.transpose(v.reshape(B, S, N_HEADS, V_HEAD_DIM), (0, 2, 1, 3))
    lam = (jnp.exp(jnp.sum(lambda_q1.astype(jnp.float32) * lambda_k1.astype(jnp.float32)))
           - jnp.exp(jnp.sum(lambda_q2.astype(jnp.float32) * lambda_k2.astype(jnp.float32)))
           + lam_init)
    scale = 1.0 / math.sqrt(HEAD_DIM)
    key_chunk = jnp.arange(S) // CHUNK
    q_blocks = jnp.moveaxis(q.reshape(B, N_HEADS, 2, nb, Q_BLOCK, HEAD_DIM), 3, 0)

    def attend(args):
        q_blk, blk = args
        s = jnp.einsum("bhmqd,bhmkd->bhmqk", q_blk, k).astype(jnp.float32) * scale
        q_chunk = (blk * Q_BLOCK + jnp.arange(Q_BLOCK)) // CHUNK
        mask = key_chunk[None, :] <= q_chunk[:, None]
        s = jnp.where(mask, s, jnp.finfo(jnp.float32).min)
        p = jax.nn.softmax(s, axis=-1)
        p_diff = p[:, :, 0] - lam * p[:, :, 1]
        return jnp.einsum("bhqk,bhkv->bhqv", p_diff.astype(v.dtype), v)

    out = lax.map(attend, (q_blocks, jnp.arange(nb)))
    out = jnp.moveaxis(out, 0, 2).reshape(B, N_HEADS, S, V_HEAD_DIM)
    out = rmsnorm(out, subln_g) * (1.0 - lam_init)
    out = jnp.transpose(out, (0, 2, 1, 3)).reshape(B, S, ATTN_V_WIDTH)
    return out @ w_attn_out


def setup_inputs(seed: int = 0) -> dict:
    key = jax.random.key(seed)
    ks = jax.random.split(key, 24)
    f32 = jnp.float32

    def nrm(k, shape, s):
        return jax.random.normal(k, shape, f32) * s

    def gain(k, shape):
        return 1.0 + 0.05 * jax.random.normal(k, shape, f32)

    L, D = DEPTH, D_MODEL
    pos_offset = jax.random.randint(ks[2], (BATCH, 1), 0, 1000, dtype=jnp.int32) * CHUNK
    return {
        "x": nrm(ks[0], (BATCH, SEQ, D), 1.0),
        "c": nrm(ks[1], (BATCH, D), 1.0),
        "pos": (pos_offset + jnp.arange(SEQ, dtype=jnp.int32)[None, :]).astype(jnp.int32),
        "ada_w": nrm(ks[3], (L, D, 6 * D), 0.5 * D ** -0.5),
        "ada_b": nrm(ks[4], (L, 6 * D), 0.01),
        "norm_mix_g": gain(ks[5], (L, D)),
        "w_in": nrm(ks[6], (L, D, IN_WIDTH), D ** -0.5),
        "conv_w": nrm(ks[7], (L, CONV_KERNEL, CONV_WIDTH), CONV_KERNEL ** -0.5),
        "conv_b": nrm(ks[8], (L, CONV_WIDTH), 0.01),
        "conv_norm_g": gain(ks[9], (L, CONV_WIDTH)),
        "w_conv_out": nrm(ks[10], (L, CONV_WIDTH, D), CONV_WIDTH ** -0.5),
        "q_norm_g": gain(ks[11], (L, HEAD_DIM)),
        "k_norm_g": gain(ks[12], (L, HEAD_DIM)),
        "lambda_q1": nrm(ks[13], (L, HEAD_DIM), 0.1),
        "lambda_k1": nrm(ks[14], (L, HEAD_DIM), 0.1),
        "lambda_q2": nrm(ks[15], (L, HEAD_DIM), 0.1),
        "lambda_k2": nrm(ks[16], (L, HEAD_DIM), 0.1),
        "subln_g": gain(ks[17], (L, V_HEAD_DIM)),
        "w_attn_out": nrm(ks[18], (L, ATTN_V_WIDTH, D), ATTN_V_WIDTH ** -0.5),
        "gate_b": nrm(ks[19], (L, N_BRANCHES * D), 0.01),
        "w_out": nrm(ks[20], (L, D, D), D ** -0.5),
        "norm_mlp_g": gain(ks[21], (L, D)),
        "w_mlp_in": nrm(ks[22], (L, D, D_FF), D ** -0.5),
        "w_mlp_out": nrm(ks[23], (L, D_FF, D), D_FF ** -0.5),
    }


def reference(x, c, pos, ada_w, ada_b, norm_mix_g, w_in, conv_w, conv_b, conv_norm_g, w_conv_out,
              q_norm_g, k_norm_g, lambda_q1, lambda_k1, lambda_q2, lambda_k2, subln_g, w_attn_out,
              gate_b, w_out, norm_mlp_g, w_mlp_in, w_mlp_out):
    B, S, D = x.shape
    inv_freq = ROPE_THETA ** (-jnp.arange(0, HEAD_DIM, 2, dtype=jnp.float32) / HEAD_DIM)
    ang = pos.astype(jnp.float32)[:, :, None] * inv_freq[None, None, :]
    cos = jnp.cos(ang)[:, :, None, None, :]
    sin = jnp.sin(ang)[:, :, None, None, :]
    c_act = jax.nn.silu(c)
    split_idx = [CONV_WIDTH, 2 * CONV_WIDTH, 2 * CONV_WIDTH + ATTN_QK_WIDTH,
                 2 * CONV_WIDTH + 2 * ATTN_QK_WIDTH, 2 * CONV_WIDTH + 2 * ATTN_QK_WIDTH + ATTN_V_WIDTH]

    for l in range(DEPTH):
        lam_init = lambda_init_fn(l)
        ada = (c_act @ ada_w[l] + ada_b[l])[:, None, :]
        shift_m, scale_m, gate_m, shift_f, scale_f, gate_f = jnp.split(ada, 6, axis=-1)

        h = rmsnorm(x, norm_mix_g[l]) * (1.0 + scale_m) + shift_m
        u = h @ w_in[l]
        conv_a, conv_g, q, k, v, gate_logits = jnp.split(u, split_idx, axis=-1)
        y_conv = conformer_conv_branch(conv_a, conv_g, conv_w[l], conv_b[l], conv_norm_g[l], w_conv_out[l])
        y_attn = diff_attention_branch(q, k, v, cos, sin, q_norm_g[l], k_norm_g[l],
                                       lambda_q1[l], lambda_k1[l], lambda_q2[l], lambda_k2[l],
                                       subln_g[l], w_attn_out[l], lam_init)
        gates = jax.nn.sigmoid(gate_logits + gate_b[l]).reshape(B, S, N_BRANCHES, D)
        merged = gates[:, :, 0] * y_conv + gates[:, :, 1] * y_attn
        x = x + gate_m * (merged @ w_out[l])

        h = rmsnorm(x, norm_mlp_g[l]) * (1.0 + scale_f) + shift_f
        x = x + gate_f * (jnp.square(jax.nn.relu(h @ w_mlp_in[l])) @ w_mlp_out[l])
    return x
```

```python
import math
import numpy as np
import concourse.bass as bass
import concourse.mybir as mybir
from concourse.bass_utils import run_bass_kernel_spmd

F32 = mybir.dt.float32
BF16 = mybir.dt.bfloat16
I32 = mybir.dt.int32
AF = mybir.ActivationFunctionType
ALU = mybir.AluOpType
AX = mybir.AxisListType

D = 2048
HD = 128
NH = 4
VD = 256
CW = 1024
CK = 31
DFF = 8192
EPS = 1e-6
LAM_INIT = 0.8 - 0.6 * math.exp(0.0)
HALO = 32
C1 = 6.28125
C2 = 0.001934051513671875
C3 = float(2 * math.pi - C1 - C2)

ENGS = ("pe", "act", "dve", "pool", "sp")
EPOCH = 8000

OPT_NORM_ACT = False
OPT_P1_ACT = False


class Res:
    __slots__ = ("name", "writers", "readers", "pw", "pr")

    def __init__(self, name=""):
        self.name = name
        self.writers = []
        self.readers = []
        self.pw = []
        self.pr = []


class DmaSem:
    def __init__(self, handle):
        self.handle = handle
        self.count = 0


class Op:
    __slots__ = ("eng", "fn", "deps", "signal", "ev", "waits", "dsem", "snap")

    def __init__(self, eng, fn, dsem=None):
        self.eng = eng
        self.fn = fn
        self.deps = ()
        self.signal = False
        self.ev = None
        self.waits = ()
        self.dsem = dsem
        self.snap = None


class Prog:
    def __init__(self, nc):
        self.nc = nc
        self.all = []
        self._semctx = []
        self.last = {}
        self.dmas = []

    def new_sem(self, name):
        cm = self.nc.semaphore(name)
        h = cm.__enter__()
        self._semctx.append(cm)
        return h

    def dma_sem(self, name):
        return DmaSem(self.new_sem(name))

    def op(self, eng, fn, reads=(), writes=(), accs=(), dsem=None, extra=(), nobar=False):
        o = Op(eng, fn, dsem)
        deps = set(extra)
        for r in reads:
            deps.update(r.writers)
            r.readers.append(o)
        for w in writes:
            deps.update(w.writers)
            deps.update(w.readers)
            w.pw = w.writers
            w.pr = w.readers
            w.writers = [o]
            w.readers = []
        for a in accs:
            deps.update(a.pw)
            deps.update(a.pr)
            a.writers.append(o)
        deps.discard(o)
        if eng == "pe":
            deps = [d for d in deps if d.eng != "pe" or d.dsem is not None]
        o.deps = tuple(deps)
        if dsem is not None:
            dsem.count += 1
            o.ev = (dsem.handle, 16 * dsem.count)
            o.signal = True
            if not nobar:
                self.dmas.append(o)
        else:
            self.last[eng] = o
        self.all.append(o)
        return o

    def barrier(self):
        deps = list(self.last.values()) + list(self.dmas)
        self.dmas = []
        for e in ENGS:
            o = Op(e, lambda eng: eng.nop())
            o.deps = tuple(deps)
            self.all.append(o)

    def finalize(self):
        for o in self.all:
            for d in o.deps:
                d.signal = True
        counts = {e: 0 for e in ENGS}
        esems = {e: [] for e in ENGS}
        for o in self.all:
            if o.dsem is None and o.signal:
                c = counts[o.eng]
                k = c // EPOCH
                while len(esems[o.eng]) <= k:
                    esems[o.eng].append(self.new_sem(f"s_{o.eng}_{len(esems[o.eng])}"))
                o.ev = (esems[o.eng][k], c % EPOCH + 1)
                counts[o.eng] = c + 1
        known = {e: {} for e in ENGS}
        nwaits = 0
        for o in self.all:
            kn = known[o.eng]
            need = {}
            for d in o.deps:
                s, v = d.ev
                if kn.get(s, 0) < v and need.get(s, (0, None))[0] < v:
                    need[s] = (v, d)
            waits = []
            for s, (v, d) in need.items():
                if kn.get(s, 0) >= v:
                    continue
                waits.append((s, v))
                for s2, v2 in d.snap.items():
                    if kn.get(s2, 0) < v2:
                        kn[s2] = v2
                if kn.get(s, 0) < v:
                    kn[s] = v
            o.waits = tuple(waits)
            nwaits += len(waits)
            if o.signal:
                sn = dict(kn)
                s, v = o.ev
                if sn.get(s, 0) < v:
                    sn[s] = v
                o.snap = sn
        self.counts = counts
        self.nwaits = nwaits

    def emit(self):
        nc = self.nc
        per = {e: [o for o in self.all if o.eng == e] for e in ENGS}

        def run(eng, ops):
            for o in ops:
                for s, v in o.waits:
                    eng.wait_ge(s, v)
                ins = o.fn(eng)
                if o.signal:
                    ins.then_inc(o.ev[0], 16 if o.dsem is not None else 1)

        with nc.Block() as block:
            @block.tensor
            def _(e):
                run(e, per["pe"])

            @block.scalar
            def _(e):
                run(e, per["act"])

            @block.vector
            def _(e):
                run(e, per["dve"])

            @block.gpsimd
            def _(e):
                run(e, per["pool"])

            @block.sync
            def _(e):
                run(e, per["sp"])


class SbPool:
    def __init__(self, nc, lo=16640, hi=229376):
        self.nc = nc
        self.cur = lo
        self.hi = hi
        self.n = 0

    def alloc(self, name, shape, dt):
        esz = 4 if dt in (F32, I32) else 2
        per = esz
        for s in shape[1:]:
            per *= s
        per = (per + 63) // 64 * 64
        assert self.cur + per <= self.hi, f"SBUF overflow allocating {name}: {self.cur}+{per}"
        self.n += 1
        t = self.nc.alloc_sbuf_tensor_at(f"{name}_{self.n}", list(shape), dt, offset=self.cur)
        self.cur += per
        return t


def bc(ap, shape):
    return ap.to_broadcast(list(shape))


def build(NPAIR, debug=False):
    NB = NPAIR * 8
    S = NB * 128
    NOWN = NPAIR * 512
    nc = bass.Bass("TRN2", target_bir_lowering=False)
    P = Prog(nc)
    sb = SbPool(nc)

    def din(name, shape, dt=F32):
        return nc.dram_tensor(name, list(shape), dt, kind="ExternalInput").ap()

    xs = din("xs", [S, D])
    xh = din("xh", [NPAIR * HALO, D])
    hv_d = din("hv", [128, NPAIR])
    pos_d = din("posc", [128, NB], I32)
    invf_d = din("invf", [128, 64])
    c_d = din("cT", [128, 16])
    ident_d = din("ident", [128, 128])
    obias_d = din("obias", [128, 1])
    ada_w = din("ada_w", [D, 6 * D])
    ada_bT = din("ada_bT", [128, 96])
    ada_b_row = din("ada_b_row", [1, 6 * D])
    nmg_d = din("nmgT", [128, 16])
    nfg_d = din("nfgT", [128, 16])
    w_in = din("w_in", [D, 9216])
    conv_w_d = din("conv_w", [CK, CW])
    conv_b_d = din("conv_bT", [128, 8])
    conv_g_d = din("conv_gT", [128, 8])
    w_conv_out = din("w_conv_out", [CW, D])
    qg_d = din("qg_b", [128, 128])
    kg_d = din("kg_b", [128, 128])
    lam_d = din("lam_b", [128, 4, 128])
    subln_d = din("sublnT", [128, 2])
    w_attn_out = din("w_attn_out", [CW, D])
    gate_b_d = din("gate_bT", [128, 32])
    w_out = din("w_out", [D, D])
    w_mlp_in = din("w_mlp_in", [D, DFF])
    w_mlp_out = din("w_mlp_out", [DFF, D])
    out_d = nc.dram_tensor("out", [NOWN, D], F32, kind="ExternalOutput").ap()
    skind = "ExternalOutput" if debug else "Internal"
    KT_d = nc.dram_tensor("KT_s", [NH, 128, 2, S], BF16, kind=skind).ap()
    V_d = nc.dram_tensor("V_s", [S, NH * VD], BF16, kind=skind).ap()
    QT_d = nc.dram_tensor("QT_s", [NH, 128, 2, NOWN], BF16, kind=skind).ap()
    AT_d = nc.dram_tensor("AT_s", [8, 128, NOWN], BF16, kind=skind).ap()

    PS = [nc.alloc_psum_tensor(f"ps{i}", [128, 512], F32) for i in range(8)]
    PSR = [Res(f"ps{i}") for i in range(8)]

    def psb(i):
        return PS[i][:].bitcast(BF16)

    ident_bf = sb.alloc("ident_bf", [128, 128], BF16)
    ident_f = sb.alloc("ident_f", [128, 128], F32)
    ones_bf = sb.alloc("ones_bf", [128, 128], BF16)
    gmodm = sb.alloc("gmodm", [128, 16], F32)
    shiftm = sb.alloc("shiftm", [128, 16], F32)
    gmodf = sb.alloc("gmodf", [128, 16], F32)
    shiftf = sb.alloc("shiftf", [128, 16], F32)
    gate_m_b = sb.alloc("gate_m_b", [128, D], F32)
    gate_f_b = sb.alloc("gate_f_b", [128, D], F32)
    obias = sb.alloc("obias", [128, 1], F32)
    nlam = sb.alloc("nlam", [128, 1], F32)
    sublng = sb.alloc("sublng", [128, 2], F32)
    gate_bT = sb.alloc("gate_bT", [128, 32], F32)
    conv_bT = sb.alloc("conv_bT", [128, 8], F32)
    conv_gT = sb.alloc("conv_gT", [128, 8], F32)
    cwT = sb.alloc("cwT", [128, 8, CK], F32)
    hv = sb.alloc("hv", [128, NPAIR], F32)
    dummy = sb.alloc("dummy", [128, 16], F32)
    R_const = Res("const")
    R_mod = Res("mod")
    R_gates = Res("gatesb")
    R_lam = Res("lam")
    R_cw = Res("cw")
    persist_mark = sb.cur

    dq = [0]

    def dsem(name):
        dq[0] += 1
        return P.dma_sem(f"{name}{dq[0]}")

    def dma(eng, out, in_, sem, reads=(), writes=(), accs=(), nobar=False):
        return P.op(eng, lambda e: e.dma_start(out=out, in_=in_), reads=reads, writes=writes, accs=accs, dsem=sem, nobar=nobar)

    def mm(out, lhsT, rhs, start, stop, reads=(), writes=(), accs=()):
        return P.op("pe", lambda e: e.matmul(out, lhsT=lhsT, rhs=rhs, start=start, stop=stop),
                    reads=reads, writes=writes, accs=accs)

    def tr(out, in_, idn, reads=(), writes=(), accs=()):
        return P.op("pe", lambda e: e.transpose(out, in_, idn), reads=reads, writes=writes, accs=accs)

    def act(out, in_, func, bias=None, scale=None, accum=None, reads=(), writes=(), accs=()):
        kw = {}
        if bias is not None:
            kw["bias"] = bias
        if scale is not None:
            kw["scale"] = scale
        if accum is not None:
            kw["accum_out"] = accum
        return P.op("act", lambda e: e.activation(out=out, in_=in_, func=func, **kw), reads=reads, writes=writes, accs=accs)

    def tt(eng, out, in0, in1, op, reads=(), writes=(), accs=()):
        return P.op(eng, lambda e: e.tensor_tensor(out=out, in0=in0, in1=in1, op=op), reads=reads, writes=writes, accs=accs)

    def ts(eng, out, in0, s1, s2, op0, op1=None, reads=(), writes=(), accs=()):
        if op1 is None:
            return P.op(eng, lambda e: e.tensor_scalar(out=out, in0=in0, scalar1=s1, scalar2=None, op0=op0), reads=reads, writes=writes, accs=accs)
        return P.op(eng, lambda e: e.tensor_scalar(out=out, in0=in0, scalar1=s1, scalar2=s2, op0=op0, op1=op1), reads=reads, writes=writes, accs=accs)

    def stt(eng, out, in0, scalar, in1, op0, op1, reads=(), writes=(), accs=()):
        return P.op(eng, lambda e: e.scalar_tensor_tensor(out=out, in0=in0, scalar=scalar, in1=in1, op0=op0, op1=op1),
                    reads=reads, writes=writes, accs=accs)

    def cp(eng, out, in_, reads=(), writes=(), accs=()):
        return P.op(eng, lambda e: e.tensor_copy(out=out, in_=in_), reads=reads, writes=writes, accs=accs)

    def recip(out, in_, reads=(), writes=(), accs=()):
        return P.op("dve", lambda e: e.reciprocal(out=out, in_=in_), reads=reads, writes=writes, accs=accs)

    def mset(eng, ap, val, reads=(), writes=(), accs=()):
        return P.op(eng, lambda e: e.memset(ap, val), reads=reads, writes=writes, accs=accs)

    def rstd_from_ssq(rstd, ssq, inv_n, r_in, r_out):
        act(rstd, ssq, AF.Sqrt, bias=EPS, scale=inv_n, reads=[r_in], writes=[r_out])
        recip(rstd, rstd, reads=[r_out], writes=[r_out])

    cos_t = sb.alloc("cos_t", [128, NB, 64], F32)
    sin_t = sb.alloc("sin_t", [128, NB, 64], F32)
    qg_b = sb.alloc("qg_b", [128, 128], F32)
    kg_b = sb.alloc("kg_b", [128, 128], F32)
    qgsw = sb.alloc("qgsw", [128, 128], F32)
    kgsw = sb.alloc("kgsw", [128, 128], F32)
    mP1 = sb.cur
    s_c = dsem("c")
    dma("sp", ident_f[:], ident_d, s_c, writes=[R_const])
    dma("pool", ident_bf[:], ident_d, dsem("cq"), accs=[R_const])
    dma("sp", obias[:], obias_d, s_c, accs=[R_const])
    dma("sp", qg_b[:], qg_d, s_c, accs=[R_const])
    dma("sp", kg_b[:], kg_d, s_c, accs=[R_const])
    dma("sp", sublng[:], subln_d, s_c, accs=[R_const])
    dma("sp", gate_bT[:], gate_b_d, s_c, accs=[R_const])
    dma("sp", conv_bT[:], conv_b_d, s_c, accs=[R_const])
    dma("sp", conv_gT[:], conv_g_d, s_c, accs=[R_const])
    dma("sp", hv[:], hv_d, s_c, accs=[R_const])
    cT = sb.alloc("cT", [128, 16], F32)
    lam_t = sb.alloc("lam_t", [128, 4, 128], F32)
    cw_nat = sb.alloc("cw_nat", [CK, CW], F32)
    nmg = sb.alloc("nmg", [128, 16], F32)
    nfg = sb.alloc("nfg", [128, 16], F32)
    adabT = sb.alloc("adabT", [128, 96], F32)
    dma("sp", cT[:], c_d, s_c, accs=[R_const])
    dma("sp", lam_t[:], lam_d, s_c, accs=[R_const])
    dma("sp", cw_nat[:], conv_w_d, s_c, accs=[R_const])
    dma("sp", nmg[:], nmg_d, s_c, accs=[R_const])
    dma("sp", nfg[:], nfg_d, s_c, accs=[R_const])
    dma("sp", adabT[:], ada_bT, s_c, accs=[R_const])
    dma("sp", gate_m_b[:], ada_b_row[:, 2 * D:3 * D].partition_broadcast(128), s_c, accs=[R_const])
    dma("sp", gate_f_b[:], ada_b_row[:, 5 * D:6 * D].partition_broadcast(128), s_c, accs=[R_const])

    R_ones = Res("ones")
    P.op("pool", lambda e: e.memset(ones_bf[:], 1.0), writes=[R_ones])

    lsc = sb.alloc("lsc", [128, 2, 128], F32)
    lsum = sb.alloc("lsum", [128, 2], F32)
    tt("dve", lsc[:, 0, :], lam_t[:, 0, :], lam_t[:, 1, :], ALU.mult, reads=[R_const], writes=[R_lam])
    tt("dve", lsc[:, 1, :], lam_t[:, 2, :], lam_t[:, 3, :], ALU.mult, reads=[R_const, R_lam], writes=[R_lam])
    P.op("dve", lambda e: e.reduce_sum(out=lsum[:], in_=lsc[:], axis=AX.X), reads=[R_lam], writes=[R_lam])
    act(lsum[:], lsum[:], AF.Exp, reads=[R_lam], writes=[R_lam])
    tt("dve", nlam[:], lsum[:, 1:2], lsum[:, 0:1], ALU.subtract, reads=[R_lam], writes=[R_lam])
    ts("dve", nlam[:], nlam[:], -LAM_INIT, None, ALU.add, reads=[R_lam], writes=[R_lam])
    ts("dve", sublng[:], sublng[:], 1.0 - LAM_INIT, None, ALU.mult, reads=[R_const], writes=[R_const])
    R_gsw = Res("gsw")
    ts("dve", qgsw[:, 0:64], qg_b[:, 64:128], -1.0, None, ALU.mult, reads=[R_const], writes=[R_gsw])
    cp("dve", qgsw[:, 64:128], qg_b[:, 0:64], reads=[R_const], accs=[R_gsw])
    ts("dve", kgsw[:, 0:64], kg_b[:, 64:128], -1.0, None, ALU.mult, reads=[R_const], accs=[R_gsw])
    cp("dve", kgsw[:, 64:128], kg_b[:, 0:64], reads=[R_const], accs=[R_gsw])

    for c in range(8):
        P.op("pe", lambda e, c=c: e.transpose(PS[0][:, c * 32:c * 32 + CK], cw_nat[0:CK, c * 128:(c + 1) * 128], ident_f[0:CK, 0:CK]),
             reads=[R_const], writes=[PSR[0]] if c == 0 else (), accs=[PSR[0]] if c else ())
    cp("dve", cwT[:], PS[0][:, 0:256].rearrange("p (c j) -> p c j", j=32)[:, :, 0:CK], reads=[PSR[0]], writes=[R_cw])

    cact = sb.alloc("cact", [128, 16], BF16)
    cact_rep = sb.alloc("cact_rep", [128, 16, 128], BF16)
    R_cact = Res("cact")
    act(cact[:], cT[:], AF.Silu, reads=[R_const], writes=[R_cact])
    cp("dve", cact_rep[:], bc(cact[:].unsqueeze(2), [128, 16, 128]), reads=[R_cact], accs=[R_cact])

    NAP = 3
    apan = [sb.alloc(f"apan{i}", [128, 16, 512], BF16) for i in range(NAP)]
    apan_r = [Res(f"apan{i}") for i in range(NAP)]
    apan_s = [dsem("apan") for i in range(NAP)]
    adaT = sb.alloc("adaT", [128, 96], F32)
    R_adaT = Res("adaT")
    pi = 0
    for seg in range(2):
        for g in range(4):
            col0 = seg * D + g * 512
            slot = pi % NAP
            pi += 1
            dma("pool", apan[slot][:], ada_w[:, col0:col0 + 512].rearrange("(kc p) n -> p kc n", p=128), apan_s[slot], writes=[apan_r[slot]])
            if seg in (2, 5):
                bank = 1 + (pi % 2)
                for kc in range(16):
                    mm(PS[bank][:], cact_rep[:, kc, :], apan[slot][:, kc, :], kc == 0, kc == 15,
                       reads=[R_cact, apan_r[slot]], writes=[PSR[bank]] if kc == 0 else (), accs=[PSR[bank]] if kc else ())
                dst = gate_m_b if seg == 2 else gate_f_b
                tt("dve", dst[:, g * 512:(g + 1) * 512], PS[bank][:], dst[:, g * 512:(g + 1) * 512], ALU.add,
                   reads=[PSR[bank], R_const], accs=[R_gates])
            else:
                for ch in range(4):
                    j = (col0 + ch * 128) // 128
                    bank = 3
                    for kc in range(16):
                        first = (kc == 0 and ch == 0)
                        mm(PS[bank][:, ch:ch + 1], apan[slot][:, kc, ch * 128:(ch + 1) * 128], cact[:, kc:kc + 1], kc == 0, kc == 15,
                           reads=[R_cact, apan_r[slot]], writes=[PSR[bank]] if first else (), accs=() if first else [PSR[bank]])
                j0 = col0 // 128
                tt("dve", adaT[:, j0:j0 + 4], PS[3][:, 0:4], adabT[:, j0:j0 + 4], ALU.add, reads=[PSR[3], R_const], accs=[R_adaT])
    ts("dve", gmodm[:], adaT[:, 16:32], 1.0, None, ALU.add, reads=[R_adaT], writes=[R_mod])
    tt("dve", gmodm[:], gmodm[:], nmg[:], ALU.mult, reads=[R_mod, R_const], writes=[R_mod])
    cp("dve", shiftm[:], adaT[:, 0:16], reads=[R_adaT], accs=[R_mod])

    posi = sb.alloc("posi", [128, NB], I32)
    posf = sb.alloc("posf", [128, NB], F32)
    invf = sb.alloc("invf", [128, 64], F32)
    ang = sb.alloc("ang", [128, NB, 64], F32)
    yy = sb.alloc("yy", [128, NB, 64], F32)
    y0 = sb.alloc("y0", [128, NB, 64], F32)
    dd = sb.alloc("dd", [128, NB, 64], F32)
    R_tab = Res("tab")
    R_t = Res("tabtmp")
    s_p = dsem("pos")
    dma("sp", posi[:], pos_d, s_p, writes=[R_t])
    dma("sp", invf[:], invf_d, s_p, accs=[R_t])
    cp("dve", posf[:], posi[:], reads=[R_t], writes=[R_t])
    tt("dve", ang[:], bc(posf[:].unsqueeze(2), [128, NB, 64]), bc(invf[:].unsqueeze(1), [128, NB, 64]), ALU.mult, reads=[R_t], writes=[R_t])
    ts("dve", y0[:], ang[:], 1.0 / (2 * math.pi), None, ALU.mult, reads=[R_t], writes=[R_t])
    cp("dve", yy[:], y0[:], reads=[R_t], writes=[R_t])
    for jb in range(13, -1, -1):
        pw = float(2 ** jb)
        ts("dve", dd[:], yy[:], pw, pw, ALU.is_ge, ALU.mult, reads=[R_t], writes=[R_t])
        tt("dve", yy[:], yy[:], dd[:], ALU.subtract, reads=[R_t], writes=[R_t])
    tt("dve", y0[:], y0[:], yy[:], ALU.subtract, reads=[R_t], writes=[R_t])
    stt("dve", ang[:], y0[:], -C1, ang[:], ALU.mult, ALU.add, reads=[R_t], writes=[R_t])
    stt("dve", ang[:], y0[:], -C2, ang[:], ALU.mult, ALU.add, reads=[R_t], writes=[R_t])
    stt("dve", ang[:], y0[:], -C3, ang[:], ALU.mult, ALU.add, reads=[R_t], writes=[R_t])
    ts("dve", yy[:], ang[:], -1.0, math.pi, ALU.mult, ALU.add, reads=[R_t], writes=[R_t])
    ts("dve", yy[:], yy[:], math.pi, -math.pi, ALU.min, ALU.max, reads=[R_t], writes=[R_t])
    act(sin_t[:], yy[:], AF.Sin, reads=[R_t], writes=[R_tab])
    act(dd[:], ang[:], AF.Abs, bias=-math.pi, scale=1.0, reads=[R_t], writes=[R_t])
    ts("dve", dd[:], dd[:], -math.pi / 2, -math.pi / 2, ALU.add, ALU.max, reads=[R_t], writes=[R_t])
    ts("dve", dd[:], dd[:], math.pi / 2, None, ALU.min, reads=[R_t], writes=[R_t])
    act(cos_t[:], dd[:], AF.Sin, reads=[R_t], accs=[R_tab])

    P.barrier()
    sb.cur = mP1

    def norm_transpose(xsrc, ntok, r_x, xn, r_xn, ssq, rstd, r_st, gmod, shift, hT_dst, r_hT_list, psbank, first_write):
        act(xn[0:ntok, :], xsrc, AF.Square, accum=ssq[0:ntok, :], reads=[r_x], writes=[r_xn, r_st])
        rstd_from_ssq(rstd[0:ntok, :], ssq[0:ntok, :], 1.0 / D, r_st, r_st)
        act(xn[0:ntok, :], xsrc, AF.Copy, scale=rstd[0:ntok, 0:1], reads=[r_x, r_st], writes=[r_xn])
        banks = psbank if isinstance(psbank, (tuple, list)) else (psbank,)
        for q4 in range(4):
            bk = banks[q4 % len(banks)]
            pv = psb(bk)
            for i in range(4):
                kc = q4 * 4 + i
                tr(pv[:, i * 128:i * 128 + ntok], xn[0:ntok, kc * 128:(kc + 1) * 128], ident_bf[0:ntok, 0:ntok],
                   reads=[r_xn, R_const], writes=[PSR[bk]] if i == 0 else (), accs=[PSR[bk]] if i else ())
            for i in range(4):
                kc = q4 * 4 + i
                kw = {"writes": [r_hT_list[kc]]} if first_write else {"accs": [r_hT_list[kc]]}
                if i % 2 == 0 or not OPT_NORM_ACT:
                    ts("dve", hT_dst(kc), pv[:, i * 128:i * 128 + ntok], gmod[:, kc:kc + 1], shift[:, kc:kc + 1], ALU.mult, ALU.add,
                       reads=[PSR[bk], R_mod], **kw)
                else:
                    act(hT_dst(kc), pv[:, i * 128:i * 128 + ntok], AF.Identity, bias=shift[:, kc:kc + 1], scale=gmod[:, kc:kc + 1],
                        reads=[PSR[bk], R_mod], **kw)

    wqkv = sb.alloc("wqkv", [128, 16, 3072], BF16)
    R_w = Res("wqkv")
    s_w = dsem("wqkv")
    for g in range(6):
        dma("pool", wqkv[:, :, g * 512:(g + 1) * 512], w_in[:, 2048 + g * 512:2048 + (g + 1) * 512].rearrange("(kc p) n -> p kc n", p=128),
            s_w, **({"writes": [R_w]} if g == 0 else {"accs": [R_w]}))
    panels = []
    ADA_SEGS = [2, 3, 4, 5]
    for seg in ADA_SEGS:
        for g in range(4):
            panels.append([(ada_w[:, seg * D + g * 512:seg * D + (g + 1) * 512], 0, 16)])
    NADA = len(panels)
    for c4 in range(2):
        panels.append([(w_in[:, c4 * 512:(c4 + 1) * 512], 0, 16)])
        panels.append([(w_in[:, 1024 + c4 * 512:1024 + (c4 + 1) * 512], 0, 16)])
    for f4 in range(4):
        panels.append([(w_in[:, 5120 + f4 * 512:5120 + (f4 + 1) * 512], 0, 16)])
        panels.append([(w_in[:, 7168 + f4 * 512:7168 + (f4 + 1) * 512], 0, 16)])
        panels.append([(w_conv_out[:, f4 * 512:(f4 + 1) * 512], 0, 8), (w_attn_out[:, f4 * 512:(f4 + 1) * 512], 8, 8)])
    for cg in range(4):
        panels.append([(w_out[:, cg * 512:(cg + 1) * 512], 0, 16)])
    for half in range(2):
        for fg in range(8):
            panels.append([(w_mlp_in[:, half * 4096 + fg * 512:half * 4096 + (fg + 1) * 512], 0, 16)])
        for cg in range(4):
            for kg in range(2):
                panels.append([(w_mlp_out[half * 4096 + kg * 2048:half * 4096 + (kg + 1) * 2048, cg * 512:(cg + 1) * 512], 0, 16)])
    NPAN = len(panels)
    WS_d = nc.dram_tensor("WS_s", [NPAN, 128, 16 * 512], BF16).ap()
    R_WS = Res("WS")
    s_WS = dsem("WS")
    R_WSa = Res("WSa")
    s_WSa = dsem("WSa")
    p0_jobs = []
    for i, plist in enumerate(panels):
        for (src, kc0, kcs) in plist:
            p0_jobs.append((i, src, kc0, kcs))
    p0_pos = [0]

    def p0_issue(n):
        for _ in range(n):
            if p0_pos[0] >= len(p0_jobs):
                return
            i, src, kc0, kcs = p0_jobs[p0_pos[0]]
            p0_pos[0] += 1
            dma("pool", WS_d[i].rearrange("p (kc n) -> p kc n", n=512)[:, kc0:kc0 + kcs, :],
                src.rearrange("(kc p) n -> p kc n", p=128), s_WSa if i < NADA else s_WS, accs=[R_WSa if i < NADA else R_WS], nobar=True)

    p0_every = max(1, NB // NADA)
    p0_per_blk = (NADA + NB - 1) // NB

    xb = [sb.alloc(f"xb{i}", [128, D], F32) for i in range(2)]
    xb_r = [Res(f"xb{i}") for i in range(2)]
    xb_s = [dsem("xb") for i in range(2)]
    xn1 = sb.alloc("xn1", [128, D], BF16)
    r_xn1 = Res("xn1")
    ssq1 = sb.alloc("ssq1", [128, 1], F32)
    rstd1 = sb.alloc("rstd1", [128, 1], F32)
    r_st1 = Res("st1")
    hTb = [sb.alloc(f"hTb{i}", [128, 16, 128], BF16) for i in range(2)]
    hTb_r = [[Res(f"hTb{i}_{k}") for k in range(16)] for i in range(2)]
    NSET = 2
    ssqg = [sb.alloc(f"ssqg{i}", [128, 4], F32) for i in range(NSET)]
    rstdg = [sb.alloc(f"rstdg{i}", [128, 4], F32) for i in range(NSET)]
    r_g = [Res(f"ssqg{i}") for i in range(NSET)]
    tA = [sb.alloc(f"tA{i}", [128, 512], F32) for i in range(NSET)]
    tB = [sb.alloc(f"tB{i}", [128, 512], F32) for i in range(NSET)]
    r_tA = [Res(f"tA{i}") for i in range(NSET)]
    r_tB = [Res(f"tB{i}") for i in range(NSET)]
    kn = [sb.alloc(f"kn{i}", [128, 512], BF16) for i in range(NSET)]
    r_kn = [Res(f"kn{i}") for i in range(NSET)]
    CG = [sb.alloc(f"CG{i}", [128, 2, 128], F32) for i in range(2)]
    SG = [sb.alloc(f"SG{i}", [128, 2, 128], F32) for i in range(2)]
    r_tabs = [Res(f"tabs{i}") for i in range(2)]
    KTst = [sb.alloc(f"KTst{i}", [128, 8, 128], BF16) for i in range(2)]
    r_KTst = [Res(f"KTst{i}") for i in range(2)]
    s_KT = [dsem("KTst") for i in range(2)]
    QTst = [sb.alloc(f"QTst{i}", [128, 8, 128], BF16) for i in range(2)]
    r_QTst = [Res(f"QTst{i}") for i in range(2)]
    s_QT = [dsem("QTst") for i in range(2)]
    Vst = [sb.alloc(f"Vst{i}", [128, 1024], BF16) for i in range(2)]
    r_Vst = [Res(f"Vst{i}") for i in range(2)]
    s_V = [dsem("Vst") for i in range(2)]
    R_KTd = Res("KT_d")
    R_Vd = Res("V_d")
    R_QTd = Res("QT_d")
    setc = [0]

    def post1(bank, which, blk):
        st = setc[0] % NSET
        setc[0] += 1
        tp = blk % 2
        u = PS[bank]
        u3 = u[:].rearrange("p (g d) -> p g d", g=4)
        ti = 0 if which == "k" else 1
        tA3 = tA[st][:].rearrange("p (g d) -> p g d", g=4)
        tB3 = tB[st][:].rearrange("p (g d) -> p g d", g=4)
        for g in range(4):
            act(kn[st][:, g * 128:(g + 1) * 128], u[:, g * 128:(g + 1) * 128], AF.Square, accum=ssqg[st][:, g:g + 1], reads=[PSR[bank]],
                **({"writes": [r_kn[st], r_g[st]]} if g == 0 else {"accs": [r_kn[st], r_g[st]]}))
        rstd_from_ssq(rstdg[st][:], ssqg[st][:], 1.0 / HD, r_g[st], r_g[st])
        tt("dve", tA3, u3, bc(CG[tp][:, ti, :].unsqueeze(1), [128, 4, 128]), ALU.mult,
           reads=[PSR[bank], r_tabs[tp]], writes=[r_tA[st]])
        tt("dve", tB3[:, :, 0:64], u3[:, :, 64:128], bc(SG[tp][:, ti, 0:64].unsqueeze(1), [128, 4, 64]), ALU.mult,
           reads=[PSR[bank], r_tabs[tp]], writes=[r_tB[st]])
        tt("dve", tB3[:, :, 64:128], u3[:, :, 0:64], bc(SG[tp][:, ti, 64:128].unsqueeze(1), [128, 4, 64]), ALU.mult,
           reads=[PSR[bank], r_tabs[tp]], accs=[r_tB[st]])
        tt("pool", tA[st][:], tA[st][:], tB[st][:], ALU.add, reads=[r_tA[st], r_tB[st]], writes=[r_tA[st]])
        if OPT_P1_ACT:
            for g in range(4):
                act(kn[st][:, g * 128:(g + 1) * 128], tA[st][:, g * 128:(g + 1) * 128], AF.Copy, scale=rstdg[st][:, g:g + 1],
                    reads=[r_tA[st], r_g[st]], **({"writes": [r_kn[st]]} if g == 0 else {"accs": [r_kn[st]]}))
        else:
            tt("pool", kn[st][:].rearrange("p (g d) -> p g d", g=4), tA3,
               bc(rstdg[st][:].unsqueeze(2), [128, 4, 128]), ALU.mult, reads=[r_tA[st], r_g[st]], writes=[r_kn[st]])
        return st

    def post2(st, half, stage, r_stage):
        pv = psb(7)
        for g in range(4):
            tr(pv[:, g * 128:(g + 1) * 128], kn[st][:, g * 128:(g + 1) * 128], ident_bf[:],
               reads=[r_kn[st], R_const], writes=[PSR[7]] if g == 0 else (), accs=[PSR[7]] if g else ())
        act(stage[:, half * 4:half * 4 + 4, :], pv[:, 0:512].rearrange("p (g t) -> p g t", g=4), AF.Copy,
            reads=[PSR[7]], **({"writes": [r_stage]} if half == 0 else {"accs": [r_stage]}))

    def load_x(blk):
        slot = blk % 2
        dma("sp", xb[slot][:], xs[blk * 128:(blk + 1) * 128, :], xb_s[slot], writes=[xb_r[slot]])

    def norm1(blk):
        slot = blk % 2
        xsrc = xb[slot][:]
        act(xn1[:], xsrc, AF.Square, accum=ssq1[:], reads=[xb_r[slot]], writes=[r_xn1, r_st1])
        rstd_from_ssq(rstd1[:], ssq1[:], 1.0 / D, r_st1, r_st1)
        act(xn1[:], xsrc, AF.Copy, scale=rstd1[:, 0:1], reads=[xb_r[slot], r_st1], writes=[r_xn1])

    def norm2(blk):
        slot = blk % 2
        for q4 in range(4):
            pv = psb(6)
            for i in range(4):
                kc = q4 * 4 + i
                tr(pv[:, i * 128:(i + 1) * 128], xn1[:, kc * 128:(kc + 1) * 128], ident_bf[:],
                   reads=[r_xn1, R_const], writes=[PSR[6]] if i == 0 else (), accs=[PSR[6]] if i else ())
            for i in range(4):
                kc = q4 * 4 + i
                ts("dve", hTb[slot][:, kc, :], pv[:, i * 128:(i + 1) * 128], gmodm[:, kc:kc + 1], shiftm[:, kc:kc + 1], ALU.mult, ALU.add,
                   reads=[PSR[6], R_mod], writes=[hTb_r[slot][kc]])

    def mm_group(blk, g):
        slot = blk % 2
        for kc in range(16):
            mm(PS[g][:], hTb[slot][:, kc, :], wqkv[:, kc, g * 512:(g + 1) * 512], kc == 0, kc == 15,
               reads=[hTb_r[slot][kc], R_w], writes=[PSR[g]] if kc == 0 else (), accs=[PSR[g]] if kc else ())

    def store_q(blk):
        tp = blk % 2
        t0 = (blk // 8) * 512 + (blk % 8) * 128
        for h in range(NH):
            dma("sp", QT_d[h, :, :, t0:t0 + 128], QTst[tp][:, 2 * h:2 * h + 2, :], s_QT[tp], reads=[r_QTst[tp]], accs=[R_QTd])

    pending = []
    load_x(0)
    if NB > 1:
        load_x(1)
    norm1(0)
    norm2(0)
    for blk in range(NB):
        own = (blk % 8) < 4
        tp = blk % 2
        if blk % p0_every == 0 and p0_pos[0] < NADA:
            p0_issue(min(p0_per_blk, NADA - p0_pos[0]))
        tt("pool", CG[tp][:, 0, :].rearrange("p (h f) -> p h f", h=2), bc(cos_t[:, blk, :].unsqueeze(1), [128, 2, 64]),
           kg_b[:].rearrange("p (h f) -> p h f", h=2), ALU.mult, reads=[R_tab, R_const], writes=[r_tabs[tp]])
        tt("pool", SG[tp][:, 0, :].rearrange("p (h f) -> p h f", h=2), bc(sin_t[:, blk, :].unsqueeze(1), [128, 2, 64]),
           kgsw[:].rearrange("p (h f) -> p h f", h=2), ALU.mult, reads=[R_tab, R_gsw], accs=[r_tabs[tp]])
        if own:
            tt("pool", CG[tp][:, 1, :].rearrange("p (h f) -> p h f", h=2), bc(cos_t[:, blk, :].unsqueeze(1), [128, 2, 64]),
               qg_b[:].rearrange("p (h f) -> p h f", h=2), ALU.mult, reads=[R_tab, R_const], accs=[r_tabs[tp]])
            tt("pool", SG[tp][:, 1, :].rearrange("p (h f) -> p h f", h=2), bc(sin_t[:, blk, :].unsqueeze(1), [128, 2, 64]),
               qgsw[:].rearrange("p (h f) -> p h f", h=2), ALU.mult, reads=[R_tab, R_gsw], accs=[r_tabs[tp]])
        if blk + 1 < NB:
            norm1(blk + 1)
        mm_group(blk, 2)
        if pending:
            pending.pop(0)()
        st_k0 = post1(2, "k", blk)
        mm_group(blk, 3)
        if pending:
            pending.pop(0)()
        st_k1 = post1(3, "k", blk)
        mm_group(blk, 4)
        act(Vst[tp][:, 0:512], PS[4][:], AF.Copy, reads=[PSR[4]], writes=[r_Vst[tp]])
        post2(st_k0, 0, KTst[tp], r_KTst[tp])
        mm_group(blk, 5)
        act(Vst[tp][:, 512:1024], PS[5][:], AF.Copy, reads=[PSR[5]], accs=[r_Vst[tp]])
        post2(st_k1, 1, KTst[tp], r_KTst[tp])
        for h in range(NH):
            dma("sp", KT_d[h, :, :, blk * 128:(blk + 1) * 128], KTst[tp][:, 2 * h:2 * h + 2, :], s_KT[tp], reads=[r_KTst[tp]], accs=[R_KTd])
        dma("sp", V_d[blk * 128:(blk + 1) * 128, :], Vst[tp][:], s_V[tp], reads=[r_Vst[tp]], accs=[R_Vd])
        if blk + 1 < NB:
            norm2(blk + 1)
        if own:
            mm_group(blk, 0)
            st_q0 = post1(0, "q", blk)
            mm_group(blk, 1)
            st_q1 = post1(1, "q", blk)
            pending.append(lambda st=st_q0, tp=tp: post2(st, 0, QTst[tp], r_QTst[tp]))
            pending.append(lambda st=st_q1, tp=tp, blk=blk: (post2(st, 1, QTst[tp], r_QTst[tp]), store_q(blk)))
        if blk + 2 < NB:
            load_x(blk + 2)
    while pending:
        pending.pop(0)()
    p0_issue(NADA - p0_pos[0])

    P.barrier()
    sb.cur = persist_mark

    KTs = sb.alloc("KTs", [128, 2, S], BF16)
    Vaug = sb.alloc("Vaug", [128, NB, VD + 1], BF16)
    QTs = sb.alloc("QTs", [128, 2, NOWN], BF16)
    NCH = NPAIR
    r_Kc = [Res(f"Kc{i}") for i in range(NCH)]
    r_Vc = [Res(f"Vc{i}") for i in range(NCH)]
    s_Kc = [dsem("Kc") for i in range(NCH)]
    s_Vc = [dsem("Vc") for i in range(NCH)]
    r_Q = Res("QTs")
    s_Q = dsem("QTs")
    R_ones2 = Res("vones")
    mset("pool", Vaug[:, :, VD:VD + 1], 1.0, writes=[R_ones2])
    NPT = 6
    STB = [0, 1, 6]
    PT = [sb.alloc(f"PT{i}", [128, 2, 256], BF16) for i in range(NPT)]
    r_PT = [Res(f"PT{i}") for i in range(NPT)]
    rl = [sb.alloc(f"rl{i}", [128, 2], F32) for i in range(2)]
    r_rl = [Res(f"rl{i}") for i in range(2)]
    T1 = [sb.alloc(f"T1{i}", [128, VD], F32) for i in range(2)]
    r_T1 = [Res(f"T1{i}") for i in range(2)]
    Ot = [sb.alloc(f"Ot{i}", [128, VD], F32) for i in range(2)]
    r_Ot = [Res(f"Ot{i}") for i in range(2)]
    osq = [sb.alloc(f"osq{i}", [128, VD], BF16) for i in range(2)]
    r_osq = [Res(f"osq{i}") for i in range(2)]
    ossq = [sb.alloc(f"ossq{i}", [128, 1], F32) for i in range(2)]
    orstd = [sb.alloc(f"orstd{i}", [128, 1], F32) for i in range(2)]
    r_os = [Res(f"os{i}") for i in range(2)]
    On = [[sb.alloc(f"On{a_}{b_}", [128, VD], BF16) for b_ in range(2)] for a_ in range(2)]
    r_On = [[Res(f"On{a_}{b_}") for b_ in range(2)] for a_ in range(2)]
    Oa = [sb.alloc(f"Oa{i}", [128, 4, VD + 1], F32) for i in range(2)]
    r_Oa = [Res(f"Oa{i}") for i in range(2)]
    gcount = [0]
    pend2 = []
    ATst = [sb.alloc(f"ATst{i}", [128, 2, 512], BF16) for i in range(2)]
    r_ATst = [Res(f"ATst{i}") for i in range(2)]
    s_AT = [dsem("ATst") for i in range(2)]
    R_ATd = Res("AT_d")
    SCALE = 1.0 / math.sqrt(HD)
    ptc = [0]
    arp = [sb.alloc(f"arp{i}", [128, 16, 512], BF16) for i in range(2)]
    r_arp = [Res(f"arp{i}") for i in range(2)]
    s_arp = [dsem("arp") for i in range(2)]
    cT2 = sb.alloc("cT2", [128, 16], F32)
    adabT2 = sb.alloc("adabT2", [128, 96], F32)
    nfg2 = sb.alloc("nfg2", [128, 16], F32)
    adaT2 = sb.alloc("adaT2", [128, 96], F32)
    cact2 = sb.alloc("cact2", [128, 16], BF16)
    cact_rep2 = sb.alloc("cact_rep2", [128, 16, 128], BF16)
    R_c2 = Res("c2")
    R_adaT2 = Res("adaT2")
    s_c2 = dsem("c2")
    dma("pool", cT2[:], c_d, s_c2, writes=[R_c2])
    dma("pool", adabT2[:], ada_bT, s_c2, accs=[R_c2])
    dma("pool", nfg2[:], nfg_d, s_c2, accs=[R_c2])
    R_cact2 = Res("cact2")
    act(cact2[:], cT2[:], AF.Silu, reads=[R_c2], writes=[R_cact2])
    cp("dve", cact_rep2[:], bc(cact2[:].unsqueeze(2), [128, 16, 128]), reads=[R_cact2], accs=[R_cact2])
    ada_k = [0]

    def ada_load(k):
        if k < NADA:
            dma("pool", arp[k % 2][:].rearrange("p kc n -> p (kc n)"), WS_d[k], s_arp[k % 2], reads=[R_WSa], writes=[r_arp[k % 2]])

    def ada_panel():
        k = ada_k[0]
        if k >= NADA:
            return
        ada_k[0] += 1
        seg = ADA_SEGS[k // 4]
        g = k % 4
        slot = k % 2
        if seg in (2, 5):
            for kc in range(16):
                mm(PS[7][:], cact_rep2[:, kc, :], arp[slot][:, kc, :], kc == 0, kc == 15,
                   reads=[R_cact2, r_arp[slot]], writes=[PSR[7]] if kc == 0 else (), accs=[PSR[7]] if kc else ())
            dst = gate_m_b if seg == 2 else gate_f_b
            tt("dve", dst[:, g * 512:(g + 1) * 512], PS[7][:], dst[:, g * 512:(g + 1) * 512], ALU.add,
               reads=[PSR[7], R_const], accs=[R_gates])
        else:
            for ch in range(4):
                for kc in range(16):
                    first = (kc == 0 and ch == 0)
                    mm(PS[7][:, ch:ch + 1], arp[slot][:, kc, ch * 128:(ch + 1) * 128], cact2[:, kc:kc + 1], kc == 0, kc == 15,
                       reads=[R_cact2, r_arp[slot]], writes=[PSR[7]] if first else (), accs=() if first else [PSR[7]])
            j0 = (seg * D + g * 512) // 128
            tt("dve", adaT2[:, j0:j0 + 4], PS[7][:, 0:4], adabT2[:, j0:j0 + 4], ALU.add, reads=[PSR[7], R_c2], accs=[R_adaT2])
        ada_load(k + 2)
        if ada_k[0] == NADA:
            ts("dve", gmodf[:], adaT2[:, 64:80], 1.0, None, ALU.add, reads=[R_adaT2], accs=[R_mod])
            tt("dve", gmodf[:], gmodf[:], nfg2[:], ALU.mult, reads=[R_mod, R_c2], writes=[R_mod])
            cp("dve", shiftf[:], adaT2[:, 48:64], reads=[R_adaT2], accs=[R_mod])

    ada_load(0)
    ada_load(1)
    p0_per_grp = (len(p0_jobs) - NADA + NH * NPAIR * 2 - 1) // (NH * NPAIR * 2)
    for h in range(NH):
        for ch in range(NCH):
            dma("sp", KTs[:, :, ch * 1024:(ch + 1) * 1024], KT_d[h, :, :, ch * 1024:(ch + 1) * 1024], s_Kc[ch], reads=[R_KTd], writes=[r_Kc[ch]])
            dma("sp", Vaug[:, ch * 8:(ch + 1) * 8, 0:VD],
                V_d[ch * 1024:(ch + 1) * 1024, h * VD:(h + 1) * VD].rearrange("(b p) d -> p b d", p=128), s_Vc[ch],
                reads=[R_Vd, R_ones2], writes=[r_Vc[ch]])
        dma("sp", QTs[:], QT_d[h], s_Q, reads=[R_QTd], writes=[r_Q])
        for j in range(NPAIR):
            ast = j % 2
            for g in range(2):
                kbs = [(kb, 0, False, False) for kb in range(8 * j)]
                for l in range(4):
                    if l <= 2 * g + 1:
                        q0 = max(l - 2 * g, 0)
                        kbs.append((8 * j + l, q0, True if l >= 2 * g else False, False))
                for l in range(4, 8):
                    kbs.append((8 * j + l, 0, False, True))
                qcol = j * 512 + g * 256
                nk = len(kbs)
                seen = [False, False]
                last_for = [max(i for i, kbi in enumerate(kbs) if kbi[1] <= qb) for qb in range(2)]
                accb = [[2, 3], [4, 5]]

                def qk(i):
                    kb, q0, diag, ob = kbs[i]
                    bank = STB[i % 3]
                    nq = 256 - q0 * 128
                    for m in range(2):
                        mm(PS[bank][:, m * 256 + q0 * 128:m * 256 + 256], KTs[:, m, kb * 128:(kb + 1) * 128],
                           QTs[:, m, qcol + q0 * 128:qcol + 256], True, True,
                           reads=[r_Kc[kb // 8], r_Q], writes=[PSR[bank]] if m == 0 else (), accs=[PSR[bank]] if m else ())

                def ex_pv(i):
                    kb, q0, diag, ob = kbs[i]
                    bank = STB[i % 3]
                    pt = ptc[0] % NPT
                    ptc[0] += 1
                    src = PS[bank][:].rearrange("p (m q) -> p m q", m=2)[:, :, q0 * 128:256]
                    act(PT[pt][:, :, q0 * 128:256], src, AF.Exp, bias=obias[:, 0:1] if ob else 0.0, scale=SCALE,
                        reads=[PSR[bank], R_const], writes=[r_PT[pt]])
                    if diag:
                        mset("pool", PT[pt][64:128, :, q0 * 128:q0 * 128 + 64], 0.0, reads=[r_PT[pt]], writes=[r_PT[pt]])
                    for qb in range(q0, 2):
                        for m in range(2):
                            b_ = accb[qb][m]
                            first = not seen[qb]
                            mm(PS[b_][:, 0:VD + 1], PT[pt][:, m, qb * 128:(qb + 1) * 128], Vaug[:, kb, :], first, i == last_for[qb],
                               reads=[r_PT[pt], r_Vc[kb // 8]], writes=[PSR[b_]] if first else (), accs=() if first else [PSR[b_]])
                        seen[qb] = True

                qk(0)
                if nk > 1:
                    qk(1)
                for i in range(nk):
                    if i + 2 < nk:
                        qk(i + 2)
                    ex_pv(i)
                    if pend2 and i < len(pend2[0]):
                        pend2[0][i]()
                        if i == len(pend2[0]) - 1:
                            pend2.pop(0)
                gp = gcount[0] % 2
                gcount[0] += 1
                for qb in range(2):
                    for m in range(2):
                        b_ = accb[qb][m]
                        if m == 0:
                            act(Oa[gp][:, qb * 2 + m, :], PS[b_][:, 0:VD + 1], AF.Copy, reads=[PSR[b_]],
                                **({"writes": [r_Oa[gp]]} if (qb == 0) else {"accs": [r_Oa[gp]]}))
                        else:
                            cp("dve", Oa[gp][:, qb * 2 + m, :], PS[b_][:, 0:VD + 1], reads=[PSR[b_]], accs=[r_Oa[gp]])
                ada_panel()
                p0_issue(p0_per_grp)
                def st_a(gp=gp):
                    for qb in range(2):
                        O1 = Oa[gp][:, qb * 2, :]
                        O2 = Oa[gp][:, qb * 2 + 1, :]
                        cp("dve", rl[qb][:, 0:1], O1[:, VD:VD + 1], reads=[r_Oa[gp]], writes=[r_rl[qb]])
                        cp("dve", rl[qb][:, 1:2], O2[:, VD:VD + 1], reads=[r_Oa[gp]], accs=[r_rl[qb]])
                        recip(rl[qb][:], rl[qb][:], reads=[r_rl[qb]], writes=[r_rl[qb]])
                        tt("dve", rl[qb][:, 1:2], rl[qb][:, 1:2], nlam[:], ALU.mult, reads=[r_rl[qb], R_lam], writes=[r_rl[qb]])

                def st_b(gp=gp):
                    for qb in range(2):
                        act(T1[qb][:], Oa[gp][:, qb * 2, 0:VD], AF.Copy, scale=rl[qb][:, 0:1], reads=[r_Oa[gp], r_rl[qb]], writes=[r_T1[qb]])

                def st_c(gp=gp):
                    for qb in range(2):
                        stt("dve", Ot[qb][:], Oa[gp][:, qb * 2 + 1, 0:VD], rl[qb][:, 1:2], T1[qb][:], ALU.mult, ALU.add,
                            reads=[r_Oa[gp], r_rl[qb], r_T1[qb]], writes=[r_Ot[qb]])

                def st_d(gp=gp):
                    for qb in range(2):
                        act(osq[qb][:], Ot[qb][:], AF.Square, accum=ossq[qb][:], reads=[r_Ot[qb]], writes=[r_osq[qb], r_os[qb]])
                    for qb in range(2):
                        act(orstd[qb][:], ossq[qb][:], AF.Sqrt, bias=EPS, scale=1.0 / VD, reads=[r_os[qb]], writes=[r_os[qb]])

                def st_e(gp=gp):
                    for qb in range(2):
                        recip(orstd[qb][:], orstd[qb][:], reads=[r_os[qb]], writes=[r_os[qb]])
                        ts("dve", On[gp][qb][:], Ot[qb][:], orstd[qb][:, 0:1], None, ALU.mult, reads=[r_Ot[qb], r_os[qb]], writes=[r_On[gp][qb]])

                def fin2(h=h, j=j, g=g, gp=gp, ast=ast):
                    for qb in range(2):
                        tcol = g * 256 + qb * 128
                        pv = psb(7)
                        for c2 in range(2):
                            tr(pv[:, c2 * 128:(c2 + 1) * 128], On[gp][qb][:, c2 * 128:(c2 + 1) * 128], ident_bf[:],
                               reads=[r_On[gp][qb], R_const], writes=[PSR[7]] if c2 == 0 else (), accs=[PSR[7]] if c2 else ())
                        first_at = (g == 0 and qb == 0)
                        for c2 in range(2):
                            ts("dve", ATst[ast][:, c2, tcol:tcol + 128], pv[:, c2 * 128:(c2 + 1) * 128], sublng[:, c2:c2 + 1], None, ALU.mult,
                               reads=[PSR[7], R_const], **({"writes": [r_ATst[ast]]} if (first_at and c2 == 0) else {"accs": [r_ATst[ast]]}))
                    if g == 1:
                        for c2 in range(2):
                            dma("sp", AT_d[2 * h + c2, :, j * 512:(j + 1) * 512], ATst[ast][:, c2, :], s_AT[ast], reads=[r_ATst[ast]], accs=[R_ATd])

                st_a()
                pend2.append([st_b, st_c, st_d, st_e, fin2])
    while pend2:
        for f_ in pend2.pop(0):
            f_()
    while ada_k[0] < NADA:
        ada_panel()
    p0_issue(len(p0_jobs))

    P.barrier()
    sb.cur = persist_mark

    NT = 512
    xt = sb.alloc("xt", [128, 4, D], F32)
    r_xt = [[Res(f"xt{b}_{c}") for c in range(4)] for b in range(4)]
    s_xt = [dsem("xt") for _ in range(4)]
    s_out = [dsem("out") for _ in range(4)]
    xst = sb.alloc("xst", [128, D], F32)
    r_xst = Res("xst")
    s_xst = dsem("xst")
    hT = sb.alloc("hT", [128, 16, HALO + NT], BF16)
    r_hT = [Res(f"hT{k}") for k in range(16)]
    r_hTh = [Res(f"hTh{k}") for k in range(16)]
    xn3 = [sb.alloc(f"xn3_{i}", [128, D], BF16) for i in range(2)]
    r_xn3 = [Res(f"xn3_{i}") for i in range(2)]
    ssq3 = [sb.alloc(f"ssq3_{i}", [128, 1], F32) for i in range(2)]
    rstd3 = [sb.alloc(f"rstd3_{i}", [128, 1], F32) for i in range(2)]
    r_st3 = [Res(f"st3_{i}") for i in range(2)]
    ncnt = [0]
    NSLOT = 4
    wsl = [sb.alloc(f"wsl{i}", [128, 16, 512], BF16) for i in range(NSLOT)]
    r_wsl = [Res(f"wsl{i}") for i in range(NSLOT)]
    s_wsl = [dsem("wsl") for i in range(NSLOT)]
    wc = [0]
    pidx = [0]
    mScr = sb.cur
    ycv = sb.alloc("ycv", [128, 8, NT], F32)
    r_ycv = [Res(f"ycv{c}") for c in range(8)]
    mEnd1 = sb.cur
    sb.cur = mScr
    merged = sb.alloc("merged", [128, 16, NT], BF16)
    r_mg = [Res(f"mg{f}") for f in range(16)]
    sb.cur = mEnd1
    mGlu = sb.cur
    glu = sb.alloc("glu", [128, 8, HALO + NT], BF16)
    r_glu = [Res(f"glu{c}") for c in range(8)]
    mEnd2 = sb.cur
    sb.cur = mGlu
    zT = sb.alloc("zT", [128, 8, NT], BF16)
    r_zT = [Res(f"zT{c}") for c in range(8)]
    sb.cur = mEnd2
    atT = sb.alloc("atT", [128, 8, NT], BF16)
    r_atT = Res("atT")
    s_atT = dsem("atT")
    mScrEnd = sb.cur
    sb.cur = mScr
    actT = sb.alloc("actT", [128, 32, NT], BF16)
    r_actT = [Res(f"actT{f}") for f in range(32)]
    sb.cur = max(sb.cur, mScrEnd)
    R_alias = Res("alias")
    sg = [sb.alloc(f"sg{i}", [128, NT], F32) for i in range(2)]
    r_sg = [Res(f"sg{i}") for i in range(2)]
    sgh = [sb.alloc(f"sgh{i}", [128, HALO], F32) for i in range(2)]
    r_sgh = [Res(f"sgh{i}") for i in range(2)]
    PSR6h = [Res("ps6a"), Res("ps6b")]
    ysq = sb.alloc("ysq", [128, NT], BF16)
    r_ysq = Res("ysq")
    rstdb = sb.alloc("rstdb", [128, NT], F32)
    r_rstdb = Res("rstdb")
    tmp = [sb.alloc(f"tmp{i}", [128, NT], F32) for i in range(2)]
    r_tmp = [Res(f"tmp{i}") for i in range(2)]
    diag = [sb.alloc(f"diag{i}", [128, CK, 128], BF16) for i in range(2)]
    r_diag = [Res(f"diag{i}") for i in range(2)]
    print("P3 sbuf end", sb.cur, "of", sb.hi)

    def wload(*_a, **_k):
        slot = wc[0] % NSLOT
        wc[0] += 1
        i = NADA + pidx[0] % (NPAN - NADA)
        pidx[0] += 1
        dma("sp", wsl[slot][:].rearrange("p kc n -> p (kc n)"), WS_d[i], s_wsl[slot], reads=[R_WS], writes=[r_wsl[slot]])
        return slot

    wload2 = wload

    ccount = [0]

    def nrm(xsrc, ntok, r_x, gmod, shift, dst, r_list, first):
        i = ncnt[0] % 2
        ncnt[0] += 1
        norm_transpose(xsrc, ntok, r_x, xn3[i], r_xn3[i], ssq3[i], rstd3[i], r_st3[i], gmod, shift, dst, r_list, (6, 7), first)

    def prenorm_step(jn, step):
        if step == 0:
            dma("act", xst[0:HALO, :], xh[jn * HALO:(jn + 1) * HALO, :], s_xst, writes=[r_xst])
            nrm(xst[0:HALO, :], HALO, r_xst, gmodm, shiftm, lambda kc: hT[:, kc, 0:HALO], r_hTh, True)
        else:
            b = step - 1
            r0 = jn * 1024 + b * 128
            dma("act", xst[:], xs[r0:r0 + 128, :], s_xst, writes=[r_xst])
            nrm(xst[:], 128, r_xst, gmodm, shiftm, lambda kc, b=b: hT[:, kc, HALO + b * 128:HALO + (b + 1) * 128], r_hT, b == 0)

    for st_ in range(5):
        prenorm_step(0, st_)

    for j in range(NPAIR):
        row0 = j * 1024
        for b in range(4):
            dma("pool", xt[:, b, :], xs[row0 + b * 128:row0 + (b + 1) * 128, :], s_xt[b], writes=r_xt[b])
        dma("pool", atT[:], AT_d[:, :, j * 512:(j + 1) * 512].rearrange("c p t -> p c t"), s_atT, reads=[R_ATd, R_alias], writes=[r_atT])
        pslots = {}

        def proj(c):
            p = c % 2
            cl = c % 4
            if cl == 0:
                pslots["a"] = wload()
                pslots["g"] = wload()
            sa, sgs = pslots["a"], pslots["g"]
            bA, bG = 2 * p, 2 * p + 1
            H = PS[6][:, p * 64:(p + 1) * 64]
            for kc in range(16):
                mm(PS[bA][:], wsl[sa][:, kc, cl * 128:(cl + 1) * 128], hT[:, kc, HALO:HALO + NT], kc == 0, kc == 15,
                   reads=[r_wsl[sa], r_hT[kc]], writes=[PSR[bA]] if kc == 0 else (), accs=[PSR[bA]] if kc else ())
            for kc in range(16):
                mm(PS[bG][:], wsl[sgs][:, kc, cl * 128:(cl + 1) * 128], hT[:, kc, HALO:HALO + NT], kc == 0, kc == 15,
                   reads=[r_wsl[sgs], r_hT[kc]], writes=[PSR[bG]] if kc == 0 else (), accs=[PSR[bG]] if kc else ())
            for kc in range(16):
                mm(H[:, 0:HALO], wsl[sa][:, kc, cl * 128:(cl + 1) * 128], hT[:, kc, 0:HALO], kc == 0, kc == 15,
                   reads=[r_wsl[sa], r_hTh[kc]], writes=[PSR6h[p]] if kc == 0 else (), accs=[PSR6h[p]] if kc else ())
            for kc in range(16):
                mm(H[:, HALO:2 * HALO], wsl[sgs][:, kc, cl * 128:(cl + 1) * 128], hT[:, kc, 0:HALO], kc == 0, kc == 15,
                   reads=[r_wsl[sgs], r_hTh[kc]], accs=[PSR6h[p]])
            act(sg[p][:], PS[bG][:], AF.Sigmoid, reads=[PSR[bG]], writes=[r_sg[p]])
            tt("dve", glu[:, c, HALO:HALO + NT], PS[bA][:], sg[p][:], ALU.mult, reads=[PSR[bA], r_sg[p], R_alias], writes=[r_glu[c]])
            act(sgh[p][:], H[:, HALO:2 * HALO], AF.Sigmoid, reads=[PSR6h[p]], writes=[r_sgh[p]])
            stt("dve", glu[:, c, 0:HALO], H[:, 0:HALO], hv[:, j:j + 1], sgh[p][:], ALU.mult, ALU.mult,
                reads=[PSR6h[p], r_sgh[p], R_const], accs=[r_glu[c]])

        dsel = {}

        def diagb(c):
            di = ccount[0] % 2
            ccount[0] += 1
            dsel[c] = di
            tt("dve", diag[di][:], bc(ident_bf[:].unsqueeze(1), [128, CK, 128]), bc(cwT[:, c, :].unsqueeze(2), [128, CK, 128]), ALU.mult,
               reads=[R_const, R_cw], writes=[r_diag[di]])

        def convmm(c):
            di = dsel[c]
            bank = 4 + (c % 2)
            for t in range(CK):
                mm(PS[bank][:], diag[di][:, t, :], glu[:, c, 2 + t:2 + t + NT], t == 0, t == CK - 1,
                   reads=[r_diag[di], r_glu[c]], writes=[PSR[bank]] if t == 0 else (), accs=[PSR[bank]] if t else ())
            act(ycv[:, c, :], PS[bank][:], AF.Identity, bias=conv_bT[:, c:c + 1], scale=1.0, reads=[PSR[bank], R_const, R_alias], writes=[r_ycv[c]])
            tt("dve", ysq[:], ycv[:, c, :], ycv[:, c, :], ALU.mult, reads=[r_ycv[c]], writes=[r_ysq])

        def ssqmm(c):
            mm(PS[7][:], ones_bf[:], ysq[:], c == 0, c == 7, reads=[r_ysq, R_ones],
               writes=[PSR[7]] if c == 0 else (), accs=[PSR[7]] if c else ())

        diagb(0)
        proj(0)
        for c in range(1, 8):
            diagb(c)
            proj(c)
            if c >= 2:
                ssqmm(c - 2)
            convmm(c - 1)
        ssqmm(6)
        convmm(7)
        ssqmm(7)
        act(rstdb[:], PS[7][:], AF.Sqrt, bias=EPS, scale=1.0 / CW, reads=[PSR[7]], writes=[r_rstdb])
        recip(rstdb[:], rstdb[:], reads=[r_rstdb], writes=[r_rstdb])
        for c in range(8):
            ti = c % 2
            tt("dve", tmp[ti][:], ycv[:, c, :], rstdb[:], ALU.mult, reads=[r_ycv[c], r_rstdb], writes=[r_tmp[ti]])
            act(zT[:, c, :], tmp[ti][:], AF.Silu, scale=conv_gT[:, c:c + 1], reads=[r_tmp[ti], R_const, R_alias], writes=[r_zT[c]])
        for f4 in range(4):
            s0 = wload(w_in[:, 5120 + f4 * 512:5120 + (f4 + 1) * 512])
            s1 = wload(w_in[:, 7168 + f4 * 512:7168 + (f4 + 1) * 512])
            s2 = wload2(w_conv_out[:, f4 * 512:(f4 + 1) * 512], w_attn_out[:, f4 * 512:(f4 + 1) * 512])
            for fl in range(4):
                f = f4 * 4 + fl
                for kc in range(16):
                    mm(PS[0][:], wsl[s0][:, kc, fl * 128:(fl + 1) * 128], hT[:, kc, HALO:HALO + NT], kc == 0, kc == 15,
                       reads=[r_wsl[s0], r_hT[kc]], writes=[PSR[0]] if kc == 0 else (), accs=[PSR[0]] if kc else ())
                for kc in range(16):
                    mm(PS[1][:], wsl[s1][:, kc, fl * 128:(fl + 1) * 128], hT[:, kc, HALO:HALO + NT], kc == 0, kc == 15,
                       reads=[r_wsl[s1], r_hT[kc]], writes=[PSR[1]] if kc == 0 else (), accs=[PSR[1]] if kc else ())
                for kc in range(8):
                    mm(PS[2][:], wsl[s2][:, kc, fl * 128:(fl + 1) * 128], zT[:, kc, :], kc == 0, kc == 7,
                       reads=[r_wsl[s2], r_zT[kc]], writes=[PSR[2]] if kc == 0 else (), accs=[PSR[2]] if kc else ())
                for kc in range(8):
                    mm(PS[3][:], wsl[s2][:, 8 + kc, fl * 128:(fl + 1) * 128], atT[:, kc, :], kc == 0, kc == 7,
                       reads=[r_wsl[s2], r_atT], writes=[PSR[3]] if kc == 0 else (), accs=[PSR[3]] if kc else ())
                act(sg[0][:], PS[0][:], AF.Sigmoid, bias=gate_bT[:, f:f + 1], scale=1.0, reads=[PSR[0], R_const], writes=[r_sg[0]])
                act(sg[1][:], PS[1][:], AF.Sigmoid, bias=gate_bT[:, 16 + f:17 + f], scale=1.0, reads=[PSR[1], R_const], writes=[r_sg[1]])
                tt("dve", tmp[0][:], PS[2][:], sg[0][:], ALU.mult, reads=[PSR[2], r_sg[0]], writes=[r_tmp[0]])
                tt("dve", tmp[1][:], PS[3][:], sg[1][:], ALU.mult, reads=[PSR[3], r_sg[1]], writes=[r_tmp[1]])
                tt("pool", merged[:, f, :], tmp[0][:], tmp[1][:], ALU.add, reads=[r_tmp[0], r_tmp[1], R_alias], writes=[r_mg[f]])
        for cg in range(4):
            so = wload(w_out[:, cg * 512:(cg + 1) * 512])
            for tb in range(4):
                bank = 4 + (tb % 2)
                for kc in range(16):
                    mm(PS[bank][:], merged[:, kc, tb * 128:(tb + 1) * 128], wsl[so][:, kc, :], kc == 0, kc == 15,
                       reads=[r_wsl[so], r_mg[kc]], writes=[PSR[bank]] if kc == 0 else (), accs=[PSR[bank]] if kc else ())
                ti = tb % 2
                tt("dve", tmp[ti][:], PS[bank][:], gate_m_b[:, cg * 512:(cg + 1) * 512], ALU.mult, reads=[PSR[bank], R_gates], writes=[r_tmp[ti]])
                tt("pool", xt[:, tb, cg * 512:(cg + 1) * 512], xt[:, tb, cg * 512:(cg + 1) * 512], tmp[ti][:], ALU.add,
                   reads=[r_tmp[ti], r_xt[tb][cg]], writes=[r_xt[tb][cg]])
        fence_reads = r_glu + r_ycv + r_zT + [r_atT] + r_mg
        P.op("pool", lambda e: e.memset(dummy[:, 2:3], 0.0), writes=fence_reads + [R_alias])
        for b in range(4):
            jr = Res()
            P.op("pool", lambda e: e.memset(dummy[:, 3:4], 0.0), reads=r_xt[b], writes=[jr])
            nrm(xt[:, b, :], 128, jr, gmodf, shiftf, lambda kc, b=b: hT[:, kc, HALO + b * 128:HALO + (b + 1) * 128], r_hT, b == 0)
        for half in range(2):
            for fg in range(8):
                sm = wload(w_mlp_in[:, half * 4096 + fg * 512:half * 4096 + (fg + 1) * 512])
                for fl in range(4):
                    fc = fg * 4 + fl
                    bank = fc % 2
                    for kc in range(16):
                        mm(PS[bank][:], wsl[sm][:, kc, fl * 128:(fl + 1) * 128], hT[:, kc, HALO:HALO + NT], kc == 0, kc == 15,
                           reads=[r_wsl[sm], r_hT[kc]], writes=[PSR[bank]] if kc == 0 else (), accs=[PSR[bank]] if kc else ())
                    si = fc % 2
                    act(sg[si][:], PS[bank][:], AF.Relu, reads=[PSR[bank]], writes=[r_sg[si]])
                    tt("dve", actT[:, fc, :], PS[bank][:], sg[si][:], ALU.mult, reads=[PSR[bank], r_sg[si], R_alias], writes=[r_actT[fc]])
            for cg in range(4):
                slots = [wload(w_mlp_out[half * 4096 + kg * 2048:half * 4096 + (kg + 1) * 2048, cg * 512:(cg + 1) * 512]) for kg in range(2)]
                for tb in range(4):
                    bank = 2 + tb
                    for kg in range(2):
                        for kc in range(16):
                            k = kg * 16 + kc
                            mm(PS[bank][:], actT[:, k, tb * 128:(tb + 1) * 128], wsl[slots[kg]][:, kc, :], k == 0, k == 31,
                               reads=[r_wsl[slots[kg]], r_actT[k]], writes=[PSR[bank]] if k == 0 else (), accs=[PSR[bank]] if k else ())
                    ti = tb % 2
                    tt("dve", tmp[ti][:], PS[bank][:], gate_f_b[:, cg * 512:(cg + 1) * 512], ALU.mult, reads=[PSR[bank], R_gates], writes=[r_tmp[ti]])
                    tt("pool", xt[:, tb, cg * 512:(cg + 1) * 512], xt[:, tb, cg * 512:(cg + 1) * 512], tmp[ti][:], ALU.add,
                       reads=[r_tmp[ti], r_xt[tb][cg]], writes=[r_xt[tb][cg]])
                if half == 1 and j + 1 < NPAIR:
                    if cg == 0:
                        prenorm_step(j + 1, 0)
                    prenorm_step(j + 1, cg + 1)
        P.op("pool", lambda e: e.memset(dummy[:, 4:5], 0.0), writes=r_actT + [R_alias])
        for b in range(4):
            dma("pool", out_d[j * 512 + b * 128:j * 512 + (b + 1) * 128, :], xt[:, b, :], s_out[b], reads=r_xt[b])
    P.op("sp", lambda e: e.nop(), extra=list(P.dmas))

    P.finalize()
    P.emit()
    info = dict(counts=P.counts, nwaits=P.nwaits, nops=len(P.all))
    return nc, info


_CACHE = {}


def _fm(v, n):
    return np.ascontiguousarray(np.asarray(v, dtype=np.float32).reshape(n, 128).T)


def prep_inputs(inputs, NPAIR, cores):
    S = NPAIR * 1024
    D_ = D
    shared = dict(
        ident=np.eye(128, dtype=np.float32),
        invf=np.ascontiguousarray(np.broadcast_to(
            (10000.0 ** (-np.arange(0, 128, 2, dtype=np.float32) / np.float32(128))).astype(np.float32)[None, :], (128, 64))),
        ada_w=np.ascontiguousarray(inputs["ada_w"][0]),
        ada_bT=_fm(inputs["ada_b"][0], 96),
        ada_b_row=np.ascontiguousarray(inputs["ada_b"][0][None, :]),
        nmgT=_fm(inputs["norm_mix_g"][0], 16),
        nfgT=_fm(inputs["norm_mlp_g"][0], 16),
        w_in=np.ascontiguousarray(inputs["w_in"][0]),
        conv_w=np.ascontiguousarray(inputs["conv_w"][0]),
        conv_bT=_fm(inputs["conv_b"][0], 8),
        conv_gT=_fm(inputs["conv_norm_g"][0], 8),
        w_conv_out=np.ascontiguousarray(inputs["w_conv_out"][0]),
        qg_b=np.ascontiguousarray(np.broadcast_to(inputs["q_norm_g"][0][None, :], (128, 128))),
        kg_b=np.ascontiguousarray(np.broadcast_to(inputs["k_norm_g"][0][None, :], (128, 128))),
        lam_b=np.ascontiguousarray(np.broadcast_to(np.stack([inputs["lambda_q1"][0], inputs["lambda_k1"][0],
                                                             inputs["lambda_q2"][0], inputs["lambda_k2"][0]])[None], (128, 4, 128))),
        sublnT=_fm(inputs["subln_g"][0], 2),
        w_attn_out=np.ascontiguousarray(inputs["w_attn_out"][0]),
        gate_bT=_fm(inputs["gate_b"][0], 32),
        w_out=np.ascontiguousarray(inputs["w_out"][0]),
        w_mlp_in=np.ascontiguousarray(inputs["w_mlp_in"][0]),
        w_mlp_out=np.ascontiguousarray(inputs["w_mlp_out"][0]),
    )
    shared["invf"] = np.ascontiguousarray(np.broadcast_to(
        np.power(np.float32(10000.0), -(np.arange(0, 128, 2, dtype=np.float32) / np.float32(128))).astype(np.float32)[None, :], (128, 64)))
    maps = []
    for core in cores:
        b, par = core // 2, core % 2
        x = np.asarray(inputs["x"][b][:S], dtype=np.float32)
        pos = np.asarray(inputs["pos"][b][:S], dtype=np.int32)
        order = []
        for j in range(NPAIR):
            order += [2 * j + par, 2 * j + 1 - par]
        xs = np.ascontiguousarray(x.reshape(2 * NPAIR, 512, D_)[order].reshape(S, D_))
        posr = pos.reshape(2 * NPAIR, 512)[order].reshape(S)
        posc = np.ascontiguousarray(posr.reshape(S // 128, 128).T)
        xh = np.zeros((NPAIR * HALO, D_), np.float32)
        hv = np.zeros((128, NPAIR), np.float32)
        for j in range(NPAIR):
            st = (2 * j + par) * 512
            if st > 0:
                xh[j * HALO:(j + 1) * HALO] = x[st - HALO:st]
                hv[:, j] = 1.0
        m = dict(shared)
        m.update(xs=xs, xh=xh, hv=hv, posc=posc, cT=_fm(inputs["c"][b], 16),
                 obias=np.full((128, 1), 0.0 if par else -30000.0, np.float32))
        maps.append(m)
    return maps


def run(inputs, NPAIR, cores, debug=False):
    key = (NPAIR, debug)
    if key not in _CACHE:
        _CACHE[key] = build(NPAIR, debug)
    nc, info = _CACHE[key]
    maps = prep_inputs(inputs, NPAIR, cores)
    res = run_bass_kernel_spmd(nc, maps, core_ids=list(range(len(cores))))
    return res, info


def kernel(**inputs):
    inputs = {k: np.asarray(v) for k, v in inputs.items()}
    B, S, _ = inputs["x"].shape
    NPAIR = S // 1024
    cores = list(range(2 * B))
    res, _ = run(inputs, NPAIR, cores)
    out = np.empty((B, S, D), np.float32)
    for ci, core in enumerate(cores):
        b, par = core // 2, core % 2
        o = res.results[ci]["out"].reshape(NPAIR, 512, D)
        ov = out[b].reshape(2 * NPAIR, 512, D)
        for j in range(NPAIR):
            ov[2 * j + par] = o[j]
    return out
```

```python
import math
import numpy as np
import concourse.bass as bass
import concourse.mybir as mybir
from concourse.bass_utils import run_bass_kernel_spmd

F32 = mybir.dt.float32
BF16 = mybir.dt.bfloat16
I32 = mybir.dt.int32
AF = mybir.ActivationFunctionType
ALU = mybir.AluOpType
AX = mybir.AxisListType

D = 2048
HD = 128
NH = 4
VD = 256
CW = 1024
CK = 31
DFF = 8192
EPS = 1e-6
LAM_INIT = 0.8 - 0.6 * math.exp(0.0)
HALO = 32
C1 = 6.28125
C2 = 0.001934051513671875
C3 = float(2 * math.pi - C1 - C2)

ENGS = ("pe", "act", "dve", "pool", "sp")
EPOCH = 8000

OPT_NORM_ACT = False
OPT_P1_ACT = False


class Res:
    __slots__ = ("name", "writers", "readers", "pw", "pr")

    def __init__(self, name=""):
        self.name = name
        self.writers = []
        self.readers = []
        self.pw = []
        self.pr = []


class DmaSem:
    def __init__(self, handle):
        self.handle = handle
        self.count = 0


class Op:
    __slots__ = ("eng", "fn", "deps", "signal", "ev", "waits", "dsem", "snap")

    def __init__(self, eng, fn, dsem=None):
        self.eng = eng
        self.fn = fn
        self.deps = ()
        self.signal = False
        self.ev = None
        self.waits = ()
        self.dsem = dsem
        self.snap = None


class Prog:
    def __init__(self, nc):
        self.nc = nc
        self.all = []
        self._semctx = []
        self.last = {}
        self.dmas = []

    def new_sem(self, name):
        cm = self.nc.semaphore(name)
        h = cm.__enter__()
        self._semctx.append(cm)
        return h

    def dma_sem(self, name):
        return DmaSem(self.new_sem(name))

    def op(self, eng, fn, reads=(), writes=(), accs=(), dsem=None, extra=(), nobar=False):
        o = Op(eng, fn, dsem)
        deps = set(extra)
        for r in reads:
            deps.update(r.writers)
            r.readers.append(o)
        for w in writes:
            deps.update(w.writers)
            deps.update(w.readers)
            w.pw = w.writers
            w.pr = w.readers
            w.writers = [o]
            w.readers = []
        for a in accs:
            deps.update(a.pw)
            deps.update(a.pr)
            a.writers.append(o)
        deps.discard(o)
        if eng == "pe":
            deps = [d for d in deps if d.eng != "pe" or d.dsem is not None]
        o.deps = tuple(deps)
        if dsem is not None:
            dsem.count += 1
            o.ev = (dsem.handle, 16 * dsem.count)
            o.signal = True
            if not nobar:
                self.dmas.append(o)
        else:
            self.last[eng] = o
        self.all.append(o)
        return o

    def barrier(self):
        deps = list(self.last.values()) + list(self.dmas)
        self.dmas = []
        for e in ENGS:
            o = Op(e, lambda eng: eng.nop())
            o.deps = tuple(deps)
            self.all.append(o)

    def finalize(self):
        for o in self.all:
            for d in o.deps:
                d.signal = True
        counts = {e: 0 for e in ENGS}
        esems = {e: [] for e in ENGS}
        for o in self.all:
            if o.dsem is None and o.signal:
                c = counts[o.eng]
                k = c // EPOCH
                while len(esems[o.eng]) <= k:
                    esems[o.eng].append(self.new_sem(f"s_{o.eng}_{len(esems[o.eng])}"))
                o.ev = (esems[o.eng][k], c % EPOCH + 1)
                counts[o.eng] = c + 1
        known = {e: {} for e in ENGS}
        nwaits = 0
        for o in self.all:
            kn = known[o.eng]
            need = {}
            for d in o.deps:
                s, v = d.ev
                if kn.get(s, 0) < v and need.get(s, (0, None))[0] < v:
                    need[s] = (v, d)
            waits = []
            for s, (v, d) in need.items():
                if kn.get(s, 0) >= v:
                    continue
                waits.append((s, v))
                for s2, v2 in d.snap.items():
                    if kn.get(s2, 0) < v2:
                        kn[s2] = v2
                if kn.get(s, 0) < v:
                    kn[s] = v
            o.waits = tuple(waits)
            nwaits += len(waits)
            if o.signal:
                sn = dict(kn)
                s, v = o.ev
                if sn.get(s, 0) < v:
                    sn[s] = v
                o.snap = sn
        self.counts = counts
        self.nwaits = nwaits

    def emit(self):
        nc = self.nc
        per = {e: [o for o in self.all if o.eng == e] for e in ENGS}

        def run(eng, ops):
            for o in ops:
                for s, v in o.waits:
                    eng.wait_ge(s, v)
                ins = o.fn(eng)
                if o.signal:
                    ins.then_inc(o.ev[0], 16 if o.dsem is not None else 1)

        with nc.Block() as block:
            @block.tensor
            def _(e):
                run(e, per["pe"])

            @block.scalar
            def _(e):
                run(e, per["act"])

            @block.vector
            def _(e):
                run(e, per["dve"])

            @block.gpsimd
            def _(e):
                run(e, per["pool"])

            @block.sync
            def _(e):
                run(e, per["sp"])


class SbPool:
    def __init__(self, nc, lo=16640, hi=229376):
        self.nc = nc
        self.cur = lo
        self.hi = hi
        self.n = 0

    def alloc(self, name, shape, dt):
        esz = 4 if dt in (F32, I32) else 2
        per = esz
        for s in shape[1:]:
            per *= s
        per = (per + 63) // 64 * 64
        assert self.cur + per <= self.hi, f"SBUF overflow allocating {name}: {self.cur}+{per}"
        self.n += 1
        t = self.nc.alloc_sbuf_tensor_at(f"{name}_{self.n}", list(shape), dt, offset=self.cur)
        self.cur += per
        return t


def bc(ap, shape):
    return ap.to_broadcast(list(shape))


def build(NPAIR, debug=False):
    NB = NPAIR * 8
    S = NB * 128
    NOWN = NPAIR * 512
    nc = bass.Bass("TRN2", target_bir_lowering=False)
    P = Prog(nc)
    sb = SbPool(nc)

    def din(name, shape, dt=F32):
        return nc.dram_tensor(name, list(shape), dt, kind="ExternalInput").ap()

    xs = din("xs", [S, D])
    xh = din("xh", [NPAIR * HALO, D])
    hv_d = din("hv", [128, NPAIR])
    pos_d = din("posc", [128, NB], I32)
    invf_d = din("invf", [128, 64])
    c_d = din("cT", [128, 16])
    ident_d = din("ident", [128, 128])
    obias_d = din("obias", [128, 1])
    ada_w = din("ada_w", [D, 6 * D])
    ada_bT = din("ada_bT", [128, 96])
    ada_b_row = din("ada_b_row", [1, 6 * D])
    nmg_d = din("nmgT", [128, 16])
    nfg_d = din("nfgT", [128, 16])
    w_in = din("w_in", [D, 9216])
    conv_w_d = din("conv_w", [CK, CW])
    conv_b_d = din("conv_bT", [128, 8])
    conv_g_d = din("conv_gT", [128, 8])
    w_conv_out = din("w_conv_out", [CW, D])
    qg_d = din("qg_b", [128, 128])
    kg_d = din("kg_b", [128, 128])
    lam_d = din("lam_b", [128, 4, 128])
    subln_d = din("sublnT", [128, 2])
    w_attn_out = din("w_attn_out", [CW, D])
    gate_b_d = din("gate_bT", [128, 32])
    w_out = din("w_out", [D, D])
    w_mlp_in = din("w_mlp_in", [D, DFF])
    w_mlp_out = din("w_mlp_out", [DFF, D])
    out_d = nc.dram_tensor("out", [NOWN, D], F32, kind="ExternalOutput").ap()
    skind = "ExternalOutput" if debug else "Internal"
    KT_d = nc.dram_tensor("KT_s", [NH, 128, 2, S], BF16, kind=skind).ap()
    V_d = nc.dram_tensor("V_s", [S, NH * VD], BF16, kind=skind).ap()
    QT_d = nc.dram_tensor("QT_s", [NH, 128, 2, NOWN], BF16, kind=skind).ap()
    AT_d = nc.dram_tensor("AT_s", [8, 128, NOWN], BF16, kind=skind).ap()

    PS = [nc.alloc_psum_tensor(f"ps{i}", [128, 512], F32) for i in range(8)]
    PSR = [Res(f"ps{i}") for i in range(8)]

    def psb(i):
        return PS[i][:].bitcast(BF16)

    ident_bf = sb.alloc("ident_bf", [128, 128], BF16)
    ident_f = sb.alloc("ident_f", [128, 128], F32)
    ones_bf = sb.alloc("ones_bf", [128, 128], BF16)
    gmodm = sb.alloc("gmodm", [128, 16], F32)
    shiftm = sb.alloc("shiftm", [128, 16], F32)
    gmodf = sb.alloc("gmodf", [128, 16], F32)
    shiftf = sb.alloc("shiftf", [128, 16], F32)
    gate_m_b = sb.alloc("gate_m_b", [128, D], F32)
    gate_f_b = sb.alloc("gate_f_b", [128, D], F32)
    obias = sb.alloc("obias", [128, 1], F32)
    nlam = sb.alloc("nlam", [128, 1], F32)
    sublng = sb.alloc("sublng", [128, 2], F32)
    gate_bT = sb.alloc("gate_bT", [128, 32], F32)
    conv_bT = sb.alloc("conv_bT", [128, 8], F32)
    conv_gT = sb.alloc("conv_gT", [128, 8], F32)
    cwT = sb.alloc("cwT", [128, 8, CK], F32)
    hv = sb.alloc("hv", [128, NPAIR], F32)
    dummy = sb.alloc("dummy", [128, 16], F32)
    R_const = Res("const")
    R_mod = Res("mod")
    R_gates = Res("gatesb")
    R_lam = Res("lam")
    R_cw = Res("cw")
    persist_mark = sb.cur

    dq = [0]

    def dsem(name):
        dq[0] += 1
        return P.dma_sem(f"{name}{dq[0]}")

    def dma(eng, out, in_, sem, reads=(), writes=(), accs=(), nobar=False):
        return P.op(eng, lambda e: e.dma_start(out=out, in_=in_), reads=reads, writes=writes, accs=accs, dsem=sem, nobar=nobar)

    def mm(out, lhsT, rhs, start, stop, reads=(), writes=(), accs=()):
        return P.op("pe", lambda e: e.matmul(out, lhsT=lhsT, rhs=rhs, start=start, stop=stop),
                    reads=reads, writes=writes, accs=accs)

    def tr(out, in_, idn, reads=(), writes=(), accs=()):
        return P.op("pe", lambda e: e.transpose(out, in_, idn), reads=reads, writes=writes, accs=accs)

    def act(out, in_, func, bias=None, scale=None, accum=None, reads=(), writes=(), accs=()):
        kw = {}
        if bias is not None:
            kw["bias"] = bias
        if scale is not None:
            kw["scale"] = scale
        if accum is not None:
            kw["accum_out"] = accum
        return P.op("act", lambda e: e.activation(out=out, in_=in_, func=func, **kw), reads=reads, writes=writes, accs=accs)

    def tt(eng, out, in0, in1, op, reads=(), writes=(), accs=()):
        return P.op(eng, lambda e: e.tensor_tensor(out=out, in0=in0, in1=in1, op=op), reads=reads, writes=writes, accs=accs)

    def ts(eng, out, in0, s1, s2, op0, op1=None, reads=(), writes=(), accs=()):
        if op1 is None:
            return P.op(eng, lambda e: e.tensor_scalar(out=out, in0=in0, scalar1=s1, scalar2=None, op0=op0), reads=reads, writes=writes, accs=accs)
        return P.op(eng, lambda e: e.tensor_scalar(out=out, in0=in0, scalar1=s1, scalar2=s2, op0=op0, op1=op1), reads=reads, writes=writes, accs=accs)

    def stt(eng, out, in0, scalar, in1, op0, op1, reads=(), writes=(), accs=()):
        return P.op(eng, lambda e: e.scalar_tensor_tensor(out=out, in0=in0, scalar=scalar, in1=in1, op0=op0, op1=op1),
                    reads=reads, writes=writes, accs=accs)

    def cp(eng, out, in_, reads=(), writes=(), accs=()):
        return P.op(eng, lambda e: e.tensor_copy(out=out, in_=in_), reads=reads, writes=writes, accs=accs)

    def recip(out, in_, reads=(), writes=(), accs=()):
        return P.op("dve", lambda e: e.reciprocal(out=out, in_=in_), reads=reads, writes=writes, accs=accs)

    def mset(eng, ap, val, reads=(), writes=(), accs=()):
        return P.op(eng, lambda e: e.memset(ap, val), reads=reads, writes=writes, accs=accs)

    def rstd_from_ssq(rstd, ssq, inv_n, r_in, r_out):
        act(rstd, ssq, AF.Sqrt, bias=EPS, scale=inv_n, reads=[r_in], writes=[r_out])
        recip(rstd, rstd, reads=[r_out], writes=[r_out])

    cos_t = sb.alloc("cos_t", [128, NB, 64], F32)
    sin_t = sb.alloc("sin_t", [128, NB, 64], F32)
    qg_b = sb.alloc("qg_b", [128, 128], F32)
    kg_b = sb.alloc("kg_b", [128, 128], F32)
    qgsw = sb.alloc("qgsw", [128, 128], F32)
    kgsw = sb.alloc("kgsw", [128, 128], F32)
    mP1 = sb.cur
    s_c = dsem("c")
    dma("sp", ident_f[:], ident_d, s_c, writes=[R_const])
    dma("pool", ident_bf[:], ident_d, dsem("cq"), accs=[R_const])
    dma("sp", obias[:], obias_d, s_c, accs=[R_const])
    dma("sp", qg_b[:], qg_d, s_c, accs=[R_const])
    dma("sp", kg_b[:], kg_d, s_c, accs=[R_const])
    dma("sp", sublng[:], subln_d, s_c, accs=[R_const])
    dma("sp", gate_bT[:], gate_b_d, s_c, accs=[R_const])
    dma("sp", conv_bT[:], conv_b_d, s_c, accs=[R_const])
    dma("sp", conv_gT[:], conv_g_d, s_c, accs=[R_const])
    dma("sp", hv[:], hv_d, s_c, accs=[R_const])
    cT = sb.alloc("cT", [128, 16], F32)
    lam_t = sb.alloc("lam_t", [128, 4, 128], F32)
    cw_nat = sb.alloc("cw_nat", [CK, CW], F32)
    nmg = sb.alloc("nmg", [128, 16], F32)
    nfg = sb.alloc("nfg", [128, 16], F32)
    adabT = sb.alloc("adabT", [128, 96], F32)
    dma("sp", cT[:], c_d, s_c, accs=[R_const])
    dma("sp", lam_t[:], lam_d, s_c, accs=[R_const])
    dma("sp", cw_nat[:], conv_w_d, s_c, accs=[R_const])
    dma("sp", nmg[:], nmg_d, s_c, accs=[R_const])
    dma("sp", nfg[:], nfg_d, s_c, accs=[R_const])
    dma("sp", adabT[:], ada_bT, s_c, accs=[R_const])
    dma("sp", gate_m_b[:], ada_b_row[:, 2 * D:3 * D].partition_broadcast(128), s_c, accs=[R_const])
    dma("sp", gate_f_b[:], ada_b_row[:, 5 * D:6 * D].partition_broadcast(128), s_c, accs=[R_const])

    R_ones = Res("ones")
    P.op("pool", lambda e: e.memset(ones_bf[:], 1.0), writes=[R_ones])

    lsc = sb.alloc("lsc", [128, 2, 128], F32)
    lsum = sb.alloc("lsum", [128, 2], F32)
    tt("dve", lsc[:, 0, :], lam_t[:, 0, :], lam_t[:, 1, :], ALU.mult, reads=[R_const], writes=[R_lam])
    tt("dve", lsc[:, 1, :], lam_t[:, 2, :], lam_t[:, 3, :], ALU.mult, reads=[R_const, R_lam], writes=[R_lam])
    P.op("dve", lambda e: e.reduce_sum(out=lsum[:], in_=lsc[:], axis=AX.X), reads=[R_lam], writes=[R_lam])
    act(lsum[:], lsum[:], AF.Exp, reads=[R_lam], writes=[R_lam])
    tt("dve", nlam[:], lsum[:, 1:2], lsum[:, 0:1], ALU.subtract, reads=[R_lam], writes=[R_lam])
    ts("dve", nlam[:], nlam[:], -LAM_INIT, None, ALU.add, reads=[R_lam], writes=[R_lam])
    ts("dve", sublng[:], sublng[:], 1.0 - LAM_INIT, None, ALU.mult, reads=[R_const], writes=[R_const])
    R_gsw = Res("gsw")
    ts("dve", qgsw[:, 0:64], qg_b[:, 64:128], -1.0, None, ALU.mult, reads=[R_const], writes=[R_gsw])
    cp("dve", qgsw[:, 64:128], qg_b[:, 0:64], reads=[R_const], accs=[R_gsw])
    ts("dve", kgsw[:, 0:64], kg_b[:, 64:128], -1.0, None, ALU.mult, reads=[R_const], accs=[R_gsw])
    cp("dve", kgsw[:, 64:128], kg_b[:, 0:64], reads=[R_const], accs=[R_gsw])

    for c in range(8):
        P.op("pe", lambda e, c=c: e.transpose(PS[0][:, c * 32:c * 32 + CK], cw_nat[0:CK, c * 128:(c + 1) * 128], ident_f[0:CK, 0:CK]),
             reads=[R_const], writes=[PSR[0]] if c == 0 else (), accs=[PSR[0]] if c else ())
    cp("dve", cwT[:], PS[0][:, 0:256].rearrange("p (c j) -> p c j", j=32)[:, :, 0:CK], reads=[PSR[0]], writes=[R_cw])

    cact = sb.alloc("cact", [128, 16], BF16)
    cact_rep = sb.alloc("cact_rep", [128, 16, 128], BF16)
    R_cact = Res("cact")
    act(cact[:], cT[:], AF.Silu, reads=[R_const], writes=[R_cact])
    cp("dve", cact_rep[:], bc(cact[:].unsqueeze(2), [128, 16, 128]), reads=[R_cact], accs=[R_cact])

    NAP = 3
    apan = [sb.alloc(f"apan{i}", [128, 16, 512], BF16) for i in range(NAP)]
    apan_r = [Res(f"apan{i}") for i in range(NAP)]
    apan_s = [dsem("apan") for i in range(NAP)]
    adaT = sb.alloc("adaT", [128, 96], F32)
    R_adaT = Res("adaT")
    pi = 0
    for seg in range(2):
        for g in range(4):
            col0 = seg * D + g * 512
            slot = pi % NAP
            pi += 1
            dma("pool", apan[slot][:], ada_w[:, col0:col0 + 512].rearrange("(kc p) n -> p kc n", p=128), apan_s[slot], writes=[apan_r[slot]])
            if seg in (2, 5):
                bank = 1 + (pi % 2)
                for kc in range(16):
                    mm(PS[bank][:], cact_rep[:, kc, :], apan[slot][:, kc, :], kc == 0, kc == 15,
                       reads=[R_cact, apan_r[slot]], writes=[PSR[bank]] if kc == 0 else (), accs=[PSR[bank]] if kc else ())
                dst = gate_m_b if seg == 2 else gate_f_b
                tt("dve", dst[:, g * 512:(g + 1) * 512], PS[bank][:], dst[:, g * 512:(g + 1) * 512], ALU.add,
                   reads=[PSR[bank], R_const], accs=[R_gates])
            else:
                for ch in range(4):
                    j = (col0 + ch * 128) // 128
                    bank = 3
                    for kc in range(16):
                        first = (kc == 0 and ch == 0)
                        mm(PS[bank][:, ch:ch + 1], apan[slot][:, kc, ch * 128:(ch + 1) * 128], cact[:, kc:kc + 1], kc == 0, kc == 15,
                           reads=[R_cact, apan_r[slot]], writes=[PSR[bank]] if first else (), accs=() if first else [PSR[bank]])
                j0 = col0 // 128
                tt("dve", adaT[:, j0:j0 + 4], PS[3][:, 0:4], adabT[:, j0:j0 + 4], ALU.add, reads=[PSR[3], R_const], accs=[R_adaT])
    ts("dve", gmodm[:], adaT[:, 16:32], 1.0, None, ALU.add, reads=[R_adaT], writes=[R_mod])
    tt("dve", gmodm[:], gmodm[:], nmg[:], ALU.mult, reads=[R_mod, R_const], writes=[R_mod])
    cp("dve", shiftm[:], adaT[:, 0:16], reads=[R_adaT], accs=[R_mod])

    posi = sb.alloc("posi", [128, NB], I32)
    posf = sb.alloc("posf", [128, NB], F32)
    invf = sb.alloc("invf", [128, 64], F32)
    ang = sb.alloc("ang", [128, NB, 64], F32)
    yy = sb.alloc("yy", [128, NB, 64], F32)
    y0 = sb.alloc("y0", [128, NB, 64], F32)
    dd = sb.alloc("dd", [128, NB, 64], F32)
    R_tab = Res("tab")
    R_t = Res("tabtmp")
    s_p = dsem("pos")
    dma("sp", posi[:], pos_d, s_p, writes=[R_t])
    dma("sp", invf[:], invf_d, s_p, accs=[R_t])
    cp("dve", posf[:], posi[:], reads=[R_t], writes=[R_t])
    tt("dve", ang[:], bc(posf[:].unsqueeze(2), [128, NB, 64]), bc(invf[:].unsqueeze(1), [128, NB, 64]), ALU.mult, reads=[R_t], writes=[R_t])
    ts("dve", y0[:], ang[:], 1.0 / (2 * math.pi), None, ALU.mult, reads=[R_t], writes=[R_t])
    cp("dve", yy[:], y0[:], reads=[R_t], writes=[R_t])
    for jb in range(13, -1, -1):
        pw = float(2 ** jb)
        ts("dve", dd[:], yy[:], pw, pw, ALU.is_ge, ALU.mult, reads=[R_t], writes=[R_t])
        tt("dve", yy[:], yy[:], dd[:], ALU.subtract, reads=[R_t], writes=[R_t])
    tt("dve", y0[:], y0[:], yy[:], ALU.subtract, reads=[R_t], writes=[R_t])
    stt("dve", ang[:], y0[:], -C1, ang[:], ALU.mult, ALU.add, reads=[R_t], writes=[R_t])
    stt("dve", ang[:], y0[:], -C2, ang[:], ALU.mult, ALU.add, reads=[R_t], writes=[R_t])
    stt("dve", ang[:], y0[:], -C3, ang[:], ALU.mult, ALU.add, reads=[R_t], writes=[R_t])
    ts("dve", yy[:], ang[:], -1.0, math.pi, ALU.mult, ALU.add, reads=[R_t], writes=[R_t])
    ts("dve", yy[:], yy[:], math.pi, -math.pi, ALU.min, ALU.max, reads=[R_t], writes=[R_t])
    act(sin_t[:], yy[:], AF.Sin, reads=[R_t], writes=[R_tab])
    act(dd[:], ang[:], AF.Abs, bias=-math.pi, scale=1.0, reads=[R_t], writes=[R_t])
    ts("dve", dd[:], dd[:], -math.pi / 2, -math.pi / 2, ALU.add, ALU.max, reads=[R_t], writes=[R_t])
    ts("dve", dd[:], dd[:], math.pi / 2, None, ALU.min, reads=[R_t], writes=[R_t])
    act(cos_t[:], dd[:], AF.Sin, reads=[R_t], accs=[R_tab])

    P.barrier()
    sb.cur = mP1

    def norm_transpose(xsrc, ntok, r_x, xn, r_xn, ssq, rstd, r_st, gmod, shift, hT_dst, r_hT_list, psbank, first_write):
        act(xn[0:ntok, :], xsrc, AF.Square, accum=ssq[0:ntok, :], reads=[r_x], writes=[r_xn, r_st])
        rstd_from_ssq(rstd[0:ntok, :], ssq[0:ntok, :], 1.0 / D, r_st, r_st)
        act(xn[0:ntok, :], xsrc, AF.Copy, scale=rstd[0:ntok, 0:1], reads=[r_x, r_st], writes=[r_xn])
        banks = psbank if isinstance(psbank, (tuple, list)) else (psbank,)
        for q4 in range(4):
            bk = banks[q4 % len(banks)]
            pv = psb(bk)
            for i in range(4):
                kc = q4 * 4 + i
                tr(pv[:, i * 128:i * 128 + ntok], xn[0:ntok, kc * 128:(kc + 1) * 128], ident_bf[0:ntok, 0:ntok],
                   reads=[r_xn, R_const], writes=[PSR[bk]] if i == 0 else (), accs=[PSR[bk]] if i else ())
            for i in range(4):
                kc = q4 * 4 + i
                kw = {"writes": [r_hT_list[kc]]} if first_write else {"accs": [r_hT_list[kc]]}
                if i % 2 == 0 or not OPT_NORM_ACT:
                    ts("dve", hT_dst(kc), pv[:, i * 128:i * 128 + ntok], gmod[:, kc:kc + 1], shift[:, kc:kc + 1], ALU.mult, ALU.add,
                       reads=[PSR[bk], R_mod], **kw)
                else:
                    act(hT_dst(kc), pv[:, i * 128:i * 128 + ntok], AF.Identity, bias=shift[:, kc:kc + 1], scale=gmod[:, kc:kc + 1],
                        reads=[PSR[bk], R_mod], **kw)

    wqkv = sb.alloc("wqkv", [128, 16, 3072], BF16)
    R_w = Res("wqkv")
    s_w = dsem("wqkv")
    for g in range(6):
        dma("pool", wqkv[:, :, g * 512:(g + 1) * 512], w_in[:, 2048 + g * 512:2048 + (g + 1) * 512].rearrange("(kc p) n -> p kc n", p=128),
            s_w, **({"writes": [R_w]} if g == 0 else {"accs": [R_w]}))
    panels = []
    ADA_SEGS = [2, 3, 4, 5]
    for seg in ADA_SEGS:
        for g in range(4):
            panels.append([(ada_w[:, seg * D + g * 512:seg * D + (g + 1) * 512], 0, 16)])
    NADA = len(panels)
    for c4 in range(2):
        panels.append([(w_in[:, c4 * 512:(c4 + 1) * 512], 0, 16)])
        panels.append([(w_in[:, 1024 + c4 * 512:1024 + (c4 + 1) * 512], 0, 16)])
    for f4 in range(4):
        panels.append([(w_in[:, 5120 + f4 * 512:5120 + (f4 + 1) * 512], 0, 16)])
        panels.append([(w_in[:, 7168 + f4 * 512:7168 + (f4 + 1) * 512], 0, 16)])
        panels.append([(w_conv_out[:, f4 * 512:(f4 + 1) * 512], 0, 8), (w_attn_out[:, f4 * 512:(f4 + 1) * 512], 8, 8)])
    for cg in range(4):
        panels.append([(w_out[:, cg * 512:(cg + 1) * 512], 0, 16)])
    for half in range(2):
        for fg in range(8):
            panels.append([(w_mlp_in[:, half * 4096 + fg * 512:half * 4096 + (fg + 1) * 512], 0, 16)])
        for cg in range(4):
            for kg in range(2):
                panels.append([(w_mlp_out[half * 4096 + kg * 2048:half * 4096 + (kg + 1) * 2048, cg * 512:(cg + 1) * 512], 0, 16)])
    NPAN = len(panels)
    WS_d = nc.dram_tensor("WS_s", [NPAN, 128, 16 * 512], BF16).ap()
    R_WS = Res("WS")
    s_WS = dsem("WS")
    R_WSa = Res("WSa")
    s_WSa = dsem("WSa")
    p0_jobs = []
    for i, plist in enumerate(panels):
        for (src, kc0, kcs) in plist:
            p0_jobs.append((i, src, kc0, kcs))
    p0_pos = [0]

    def p0_issue(n):
        for _ in range(n):
            if p0_pos[0] >= len(p0_jobs):
                return
            i, src, kc0, kcs = p0_jobs[p0_pos[0]]
            p0_pos[0] += 1
            dma("pool", WS_d[i].rearrange("p (kc n) -> p kc n", n=512)[:, kc0:kc0 + kcs, :],
                src.rearrange("(kc p) n -> p kc n", p=128), s_WSa if i < NADA else s_WS, accs=[R_WSa if i < NADA else R_WS], nobar=True)

    p0_every = max(1, NB // NADA)
    p0_per_blk = (NADA + NB - 1) // NB

    xb = [sb.alloc(f"xb{i}", [128, D], F32) for i in range(2)]
    xb_r = [Res(f"xb{i}") for i in range(2)]
    xb_s = [dsem("xb") for i in range(2)]
    xn1 = sb.alloc("xn1", [128, D], BF16)
    r_xn1 = Res("xn1")
    ssq1 = sb.alloc("ssq1", [128, 1], F32)
    rstd1 = sb.alloc("rstd1", [128, 1], F32)
    r_st1 = Res("st1")
    hTb = [sb.alloc(f"hTb{i}", [128, 16, 128], BF16) for i in range(2)]
    hTb_r = [[Res(f"hTb{i}_{k}") for k in range(16)] for i in range(2)]
    NSET = 2
    ssqg = [sb.alloc(f"ssqg{i}", [128, 4], F32) for i in range(NSET)]
    rstdg = [sb.alloc(f"rstdg{i}", [128, 4], F32) for i in range(NSET)]
    r_g = [Res(f"ssqg{i}") for i in range(NSET)]
    tA = [sb.alloc(f"tA{i}", [128, 512], F32) for i in range(NSET)]
    tB = [sb.alloc(f"tB{i}", [128, 512], F32) for i in range(NSET)]
    r_tA = [Res(f"tA{i}") for i in range(NSET)]
    r_tB = [Res(f"tB{i}") for i in range(NSET)]
    kn = [sb.alloc(f"kn{i}", [128, 512], BF16) for i in range(NSET)]
    r_kn = [Res(f"kn{i}") for i in range(NSET)]
    CG = [sb.alloc(f"CG{i}", [128, 2, 128], F32) for i in range(2)]
    SG = [sb.alloc(f"SG{i}", [128, 2, 128], F32) for i in range(2)]
    r_tabs = [Res(f"tabs{i}") for i in range(2)]
    KTst = [sb.alloc(f"KTst{i}", [128, 8, 128], BF16) for i in range(2)]
    r_KTst = [Res(f"KTst{i}") for i in range(2)]
    s_KT = [dsem("KTst") for i in range(2)]
    QTst = [sb.alloc(f"QTst{i}", [128, 8, 128], BF16) for i in range(2)]
    r_QTst = [Res(f"QTst{i}") for i in range(2)]
    s_QT = [dsem("QTst") for i in range(2)]
    Vst = [sb.alloc(f"Vst{i}", [128, 1024], BF16) for i in range(2)]
    r_Vst = [Res(f"Vst{i}") for i in range(2)]
    s_V = [dsem("Vst") for i in range(2)]
    R_KTd = Res("KT_d")
    R_Vd = Res("V_d")
    R_QTd = Res("QT_d")
    setc = [0]

    def post1(bank, which, blk):
        st = setc[0] % NSET
        setc[0] += 1
        tp = blk % 2
        u = PS[bank]
        u3 = u[:].rearrange("p (g d) -> p g d", g=4)
        ti = 0 if which == "k" else 1
        tA3 = tA[st][:].rearrange("p (g d) -> p g d", g=4)
        tB3 = tB[st][:].rearrange("p (g d) -> p g d", g=4)
        for g in range(4):
            act(kn[st][:, g * 128:(g + 1) * 128], u[:, g * 128:(g + 1) * 128], AF.Square, accum=ssqg[st][:, g:g + 1], reads=[PSR[bank]],
                **({"writes": [r_kn[st], r_g[st]]} if g == 0 else {"accs": [r_kn[st], r_g[st]]}))
        rstd_from_ssq(rstdg[st][:], ssqg[st][:], 1.0 / HD, r_g[st], r_g[st])
        tt("dve", tA3, u3, bc(CG[tp][:, ti, :].unsqueeze(1), [128, 4, 128]), ALU.mult,
           reads=[PSR[bank], r_tabs[tp]], writes=[r_tA[st]])
        tt("dve", tB3[:, :, 0:64], u3[:, :, 64:128], bc(SG[tp][:, ti, 0:64].unsqueeze(1), [128, 4, 64]), ALU.mult,
           reads=[PSR[bank], r_tabs[tp]], writes=[r_tB[st]])
        tt("dve", tB3[:, :, 64:128], u3[:, :, 0:64], bc(SG[tp][:, ti, 64:128].unsqueeze(1), [128, 4, 64]), ALU.mult,
           reads=[PSR[bank], r_tabs[tp]], accs=[r_tB[st]])
        tt("pool", tA[st][:], tA[st][:], tB[st][:], ALU.add, reads=[r_tA[st], r_tB[st]], writes=[r_tA[st]])
        if OPT_P1_ACT:
            for g in range(4):
                act(kn[st][:, g * 128:(g + 1) * 128], tA[st][:, g * 128:(g + 1) * 128], AF.Copy, scale=rstdg[st][:, g:g + 1],
                    reads=[r_tA[st], r_g[st]], **({"writes": [r_kn[st]]} if g == 0 else {"accs": [r_kn[st]]}))
        else:
            tt("pool", kn[st][:].rearrange("p (g d) -> p g d", g=4), tA3,
               bc(rstdg[st][:].unsqueeze(2), [128, 4, 128]), ALU.mult, reads=[r_tA[st], r_g[st]], writes=[r_kn[st]])
        return st

    def post2(st, half, stage, r_stage):
        pv = psb(7)
        for g in range(4):
            tr(pv[:, g * 128:(g + 1) * 128], kn[st][:, g * 128:(g + 1) * 128], ident_bf[:],
               reads=[r_kn[st], R_const], writes=[PSR[7]] if g == 0 else (), accs=[PSR[7]] if g else ())
        act(stage[:, half * 4:half * 4 + 4, :], pv[:, 0:512].rearrange("p (g t) -> p g t", g=4), AF.Copy,
            reads=[PSR[7]], **({"writes": [r_stage]} if half == 0 else {"accs": [r_stage]}))

    def load_x(blk):
        slot = blk % 2
        dma("sp", xb[slot][:], xs[blk * 128:(blk + 1) * 128, :], xb_s[slot], writes=[xb_r[slot]])

    def norm1(blk):
        slot = blk % 2
        xsrc = xb[slot][:]
        act(xn1[:], xsrc, AF.Square, accum=ssq1[:], reads=[xb_r[slot]], writes=[r_xn1, r_st1])
        rstd_from_ssq(rstd1[:], ssq1[:], 1.0 / D, r_st1, r_st1)
        act(xn1[:], xsrc, AF.Copy, scale=rstd1[:, 0:1], reads=[xb_r[slot], r_st1], writes=[r_xn1])

    def norm2(blk):
        slot = blk % 2
        for q4 in range(4):
            pv = psb(6)
            for i in range(4):
                kc = q4 * 4 + i
                tr(pv[:, i * 128:(i + 1) * 128], xn1[:, kc * 128:(kc + 1) * 128], ident_bf[:],
                   reads=[r_xn1, R_const], writes=[PSR[6]] if i == 0 else (), accs=[PSR[6]] if i else ())
            for i in range(4):
                kc = q4 * 4 + i
                ts("dve", hTb[slot][:, kc, :], pv[:, i * 128:(i + 1) * 128], gmodm[:, kc:kc + 1], shiftm[:, kc:kc + 1], ALU.mult, ALU.add,
                   reads=[PSR[6], R_mod], writes=[hTb_r[slot][kc]])

    def mm_group(blk, g):
        slot = blk % 2
        for kc in range(16):
            mm(PS[g][:], hTb[slot][:, kc, :], wqkv[:, kc, g * 512:(g + 1) * 512], kc == 0, kc == 15,
               reads=[hTb_r[slot][kc], R_w], writes=[PSR[g]] if kc == 0 else (), accs=[PSR[g]] if kc else ())

    def store_q(blk):
        tp = blk % 2
        t0 = (blk // 8) * 512 + (blk % 8) * 128
        for h in range(NH):
            dma("sp", QT_d[h, :, :, t0:t0 + 128], QTst[tp][:, 2 * h:2 * h + 2, :], s_QT[tp], reads=[r_QTst[tp]], accs=[R_QTd])

    pending = []
    load_x(0)
    if NB > 1:
        load_x(1)
    norm1(0)
    norm2(0)
    for blk in range(NB):
        own = (blk % 8) < 4
        tp = blk % 2
        if blk % p0_every == 0 and p0_pos[0] < NADA:
            p0_issue(min(p0_per_blk, NADA - p0_pos[0]))
        tt("pool", CG[tp][:, 0, :].rearrange("p (h f) -> p h f", h=2), bc(cos_t[:, blk, :].unsqueeze(1), [128, 2, 64]),
           kg_b[:].rearrange("p (h f) -> p h f", h=2), ALU.mult, reads=[R_tab, R_const], writes=[r_tabs[tp]])
        tt("pool", SG[tp][:, 0, :].rearrange("p (h f) -> p h f", h=2), bc(sin_t[:, blk, :].unsqueeze(1), [128, 2, 64]),
           kgsw[:].rearrange("p (h f) -> p h f", h=2), ALU.mult, reads=[R_tab, R_gsw], accs=[r_tabs[tp]])
        if own:
            tt("pool", CG[tp][:, 1, :].rearrange("p (h f) -> p h f", h=2), bc(cos_t[:, blk, :].unsqueeze(1), [128, 2, 64]),
               qg_b[:].rearrange("p (h f) -> p h f", h=2), ALU.mult, reads=[R_tab, R_const], accs=[r_tabs[tp]])
            tt("pool", SG[tp][:, 1, :].rearrange("p (h f) -> p h f", h=2), bc(sin_t[:, blk, :].unsqueeze(1), [128, 2, 64]),
               qgsw[:].rearrange("p (h f) -> p h f", h=2), ALU.mult, reads=[R_tab, R_gsw], accs=[r_tabs[tp]])
        mm_group(blk, 2)
        if pending:
            pending.pop(0)()
        st_k0 = post1(2, "k", blk)
        mm_group(blk, 3)
        if pending:
            pending.pop(0)()
        st_k1 = post1(3, "k", blk)
        if blk + 1 < NB:
            norm1(blk + 1)
        mm_group(blk, 4)
        act(Vst[tp][:, 0:512], PS[4][:], AF.Copy, reads=[PSR[4]], writes=[r_Vst[tp]])
        post2(st_k0, 0, KTst[tp], r_KTst[tp])
        mm_group(blk, 5)
        act(Vst[tp][:, 512:1024], PS[5][:], AF.Copy, reads=[PSR[5]], accs=[r_Vst[tp]])
        post2(st_k1, 1, KTst[tp], r_KTst[tp])
        for h in range(NH):
            dma("sp", KT_d[h, :, :, blk * 128:(blk + 1) * 128], KTst[tp][:, 2 * h:2 * h + 2, :], s_KT[tp], reads=[r_KTst[tp]], accs=[R_KTd])
        dma("sp", V_d[blk * 128:(blk + 1) * 128, :], Vst[tp][:], s_V[tp], reads=[r_Vst[tp]], accs=[R_Vd])
        if blk + 1 < NB:
            norm2(blk + 1)
        if own:
            mm_group(blk, 0)
            st_q0 = post1(0, "q", blk)
            mm_group(blk, 1)
            st_q1 = post1(1, "q", blk)
            pending.append(lambda st=st_q0, tp=tp: post2(st, 0, QTst[tp], r_QTst[tp]))
            pending.append(lambda st=st_q1, tp=tp, blk=blk: (post2(st, 1, QTst[tp], r_QTst[tp]), store_q(blk)))
        if blk + 2 < NB:
            load_x(blk + 2)
    while pending:
        pending.pop(0)()
    p0_issue(NADA - p0_pos[0])

    P.barrier()
    sb.cur = persist_mark

    KTs = sb.alloc("KTs", [128, 2, S], BF16)
    Vaug = sb.alloc("Vaug", [128, NB, VD + 1], BF16)
    QTs = sb.alloc("QTs", [128, 2, NOWN], BF16)
    NCH = NPAIR
    r_Kc = [Res(f"Kc{i}") for i in range(NCH)]
    r_Vc = [Res(f"Vc{i}") for i in range(NCH)]
    s_Kc = [dsem("Kc") for i in range(NCH)]
    s_Vc = [dsem("Vc") for i in range(NCH)]
    r_Q = Res("QTs")
    s_Q = dsem("QTs")
    R_ones2 = Res("vones")
    mset("pool", Vaug[:, :, VD:VD + 1], 1.0, writes=[R_ones2])
    NPT = 6
    STB = [0, 1, 6]
    PT = [sb.alloc(f"PT{i}", [128, 2, 256], BF16) for i in range(NPT)]
    r_PT = [Res(f"PT{i}") for i in range(NPT)]
    rl = [sb.alloc(f"rl{i}", [128, 2], F32) for i in range(2)]
    r_rl = [Res(f"rl{i}") for i in range(2)]
    T1 = [sb.alloc(f"T1{i}", [128, VD], F32) for i in range(2)]
    r_T1 = [Res(f"T1{i}") for i in range(2)]
    Ot = [sb.alloc(f"Ot{i}", [128, VD], F32) for i in range(2)]
    r_Ot = [Res(f"Ot{i}") for i in range(2)]
    osq = [sb.alloc(f"osq{i}", [128, VD], BF16) for i in range(2)]
    r_osq = [Res(f"osq{i}") for i in range(2)]
    ossq = [sb.alloc(f"ossq{i}", [128, 1], F32) for i in range(2)]
    orstd = [sb.alloc(f"orstd{i}", [128, 1], F32) for i in range(2)]
    r_os = [Res(f"os{i}") for i in range(2)]
    On = [[sb.alloc(f"On{a_}{b_}", [128, VD], BF16) for b_ in range(2)] for a_ in range(2)]
    r_On = [[Res(f"On{a_}{b_}") for b_ in range(2)] for a_ in range(2)]
    Oa = [sb.alloc(f"Oa{i}", [128, 4, VD + 1], F32) for i in range(2)]
    r_Oa = [Res(f"Oa{i}") for i in range(2)]
    gcount = [0]
    pend2 = []
    ATst = [sb.alloc(f"ATst{i}", [128, 2, 512], BF16) for i in range(2)]
    r_ATst = [Res(f"ATst{i}") for i in range(2)]
    s_AT = [dsem("ATst") for i in range(2)]
    R_ATd = Res("AT_d")
    SCALE = 1.0 / math.sqrt(HD)
    ptc = [0]
    arp = [sb.alloc(f"arp{i}", [128, 16, 512], BF16) for i in range(2)]
    r_arp = [Res(f"arp{i}") for i in range(2)]
    s_arp = [dsem("arp") for i in range(2)]
    cT2 = sb.alloc("cT2", [128, 16], F32)
    adabT2 = sb.alloc("adabT2", [128, 96], F32)
    nfg2 = sb.alloc("nfg2", [128, 16], F32)
    adaT2 = sb.alloc("adaT2", [128, 96], F32)
    cact2 = sb.alloc("cact2", [128, 16], BF16)
    cact_rep2 = sb.alloc("cact_rep2", [128, 16, 128], BF16)
    R_c2 = Res("c2")
    R_adaT2 = Res("adaT2")
    s_c2 = dsem("c2")
    dma("pool", cT2[:], c_d, s_c2, writes=[R_c2])
    dma("pool", adabT2[:], ada_bT, s_c2, accs=[R_c2])
    dma("pool", nfg2[:], nfg_d, s_c2, accs=[R_c2])
    R_cact2 = Res("cact2")
    act(cact2[:], cT2[:], AF.Silu, reads=[R_c2], writes=[R_cact2])
    cp("dve", cact_rep2[:], bc(cact2[:].unsqueeze(2), [128, 16, 128]), reads=[R_cact2], accs=[R_cact2])
    ada_k = [0]

    def ada_load(k):
        if k < NADA:
            dma("pool", arp[k % 2][:].rearrange("p kc n -> p (kc n)"), WS_d[k], s_arp[k % 2], reads=[R_WSa], writes=[r_arp[k % 2]])

    def ada_panel():
        k = ada_k[0]
        if k >= NADA:
            return
        ada_k[0] += 1
        seg = ADA_SEGS[k // 4]
        g = k % 4
        slot = k % 2
        if seg in (2, 5):
            for kc in range(16):
                mm(PS[7][:], cact_rep2[:, kc, :], arp[slot][:, kc, :], kc == 0, kc == 15,
                   reads=[R_cact2, r_arp[slot]], writes=[PSR[7]] if kc == 0 else (), accs=[PSR[7]] if kc else ())
            dst = gate_m_b if seg == 2 else gate_f_b
            tt("dve", dst[:, g * 512:(g + 1) * 512], PS[7][:], dst[:, g * 512:(g + 1) * 512], ALU.add,
               reads=[PSR[7], R_const], accs=[R_gates])
        else:
            for ch in range(4):
                for kc in range(16):
                    first = (kc == 0 and ch == 0)
                    mm(PS[7][:, ch:ch + 1], arp[slot][:, kc, ch * 128:(ch + 1) * 128], cact2[:, kc:kc + 1], kc == 0, kc == 15,
                       reads=[R_cact2, r_arp[slot]], writes=[PSR[7]] if first else (), accs=() if first else [PSR[7]])
            j0 = (seg * D + g * 512) // 128
            tt("dve", adaT2[:, j0:j0 + 4], PS[7][:, 0:4], adabT2[:, j0:j0 + 4], ALU.add, reads=[PSR[7], R_c2], accs=[R_adaT2])
        ada_load(k + 2)
        if ada_k[0] == NADA:
            ts("dve", gmodf[:], adaT2[:, 64:80], 1.0, None, ALU.add, reads=[R_adaT2], accs=[R_mod])
            tt("dve", gmodf[:], gmodf[:], nfg2[:], ALU.mult, reads=[R_mod, R_c2], writes=[R_mod])
            cp("dve", shiftf[:], adaT2[:, 48:64], reads=[R_adaT2], accs=[R_mod])

    ada_load(0)
    ada_load(1)
    p0_per_grp = (len(p0_jobs) - NADA + NH * NPAIR * 2 - 1) // (NH * NPAIR * 2)
    for h in range(NH):
        for ch in range(NCH):
            dma("sp", KTs[:, :, ch * 1024:(ch + 1) * 1024], KT_d[h, :, :, ch * 1024:(ch + 1) * 1024], s_Kc[ch], reads=[R_KTd], writes=[r_Kc[ch]])
            dma("sp", Vaug[:, ch * 8:(ch + 1) * 8, 0:VD],
                V_d[ch * 1024:(ch + 1) * 1024, h * VD:(h + 1) * VD].rearrange("(b p) d -> p b d", p=128), s_Vc[ch],
                reads=[R_Vd, R_ones2], writes=[r_Vc[ch]])
        dma("sp", QTs[:], QT_d[h], s_Q, reads=[R_QTd], writes=[r_Q])
        for j in range(NPAIR):
            ast = j % 2
            for g in range(2):
                kbs = [(kb, 0, False, False) for kb in range(8 * j)]
                for l in range(4):
                    if l <= 2 * g + 1:
                        q0 = max(l - 2 * g, 0)
                        kbs.append((8 * j + l, q0, True if l >= 2 * g else False, False))
                for l in range(4, 8):
                    kbs.append((8 * j + l, 0, False, True))
                qcol = j * 512 + g * 256
                nk = len(kbs)
                seen = [False, False]
                last_for = [max(i for i, kbi in enumerate(kbs) if kbi[1] <= qb) for qb in range(2)]
                accb = [[2, 3], [4, 5]]

                def qk(i):
                    kb, q0, diag, ob = kbs[i]
                    bank = STB[i % 3]
                    nq = 256 - q0 * 128
                    for m in range(2):
                        mm(PS[bank][:, m * 256 + q0 * 128:m * 256 + 256], KTs[:, m, kb * 128:(kb + 1) * 128],
                           QTs[:, m, qcol + q0 * 128:qcol + 256], True, True,
                           reads=[r_Kc[kb // 8], r_Q], writes=[PSR[bank]] if m == 0 else (), accs=[PSR[bank]] if m else ())

                def ex_pv(i):
                    kb, q0, diag, ob = kbs[i]
                    bank = STB[i % 3]
                    pt = ptc[0] % NPT
                    ptc[0] += 1
                    src = PS[bank][:].rearrange("p (m q) -> p m q", m=2)[:, :, q0 * 128:256]
                    act(PT[pt][:, :, q0 * 128:256], src, AF.Exp, bias=obias[:, 0:1] if ob else 0.0, scale=SCALE,
                        reads=[PSR[bank], R_const], writes=[r_PT[pt]])
                    if diag:
                        mset("pool", PT[pt][64:128, :, q0 * 128:q0 * 128 + 64], 0.0, reads=[r_PT[pt]], writes=[r_PT[pt]])
                    for qb in range(q0, 2):
                        for m in range(2):
                            b_ = accb[qb][m]
                            first = not seen[qb]
                            mm(PS[b_][:, 0:VD + 1], PT[pt][:, m, qb * 128:(qb + 1) * 128], Vaug[:, kb, :], first, i == last_for[qb],
                               reads=[r_PT[pt], r_Vc[kb // 8]], writes=[PSR[b_]] if first else (), accs=() if first else [PSR[b_]])
                        seen[qb] = True

                qk(0)
                if nk > 1:
                    qk(1)
                for i in range(nk):
                    if i + 2 < nk:
                        qk(i + 2)
                    ex_pv(i)
                    if pend2 and i < len(pend2[0]):
                        pend2[0][i]()
                        if i == len(pend2[0]) - 1:
                            pend2.pop(0)
                gp = gcount[0] % 2
                gcount[0] += 1
                for qb in range(2):
                    for m in range(2):
                        b_ = accb[qb][m]
                        if m == 0:
                            act(Oa[gp][:, qb * 2 + m, :], PS[b_][:, 0:VD + 1], AF.Copy, reads=[PSR[b_]],
                                **({"writes": [r_Oa[gp]]} if (qb == 0) else {"accs": [r_Oa[gp]]}))
                        else:
                            cp("dve", Oa[gp][:, qb * 2 + m, :], PS[b_][:, 0:VD + 1], reads=[PSR[b_]], accs=[r_Oa[gp]])
                ada_panel()
                p0_issue(p0_per_grp)
                def st_a(gp=gp):
                    for qb in range(2):
                        O1 = Oa[gp][:, qb * 2, :]
                        O2 = Oa[gp][:, qb * 2 + 1, :]
                        cp("dve", rl[qb][:, 0:1], O1[:, VD:VD + 1], reads=[r_Oa[gp]], writes=[r_rl[qb]])
                        cp("dve", rl[qb][:, 1:2], O2[:, VD:VD + 1], reads=[r_Oa[gp]], accs=[r_rl[qb]])
                        recip(rl[qb][:], rl[qb][:], reads=[r_rl[qb]], writes=[r_rl[qb]])
                        tt("dve", rl[qb][:, 1:2], rl[qb][:, 1:2], nlam[:], ALU.mult, reads=[r_rl[qb], R_lam], writes=[r_rl[qb]])

                def st_b(gp=gp):
                    for qb in range(2):
                        act(T1[qb][:], Oa[gp][:, qb * 2, 0:VD], AF.Copy, scale=rl[qb][:, 0:1], reads=[r_Oa[gp], r_rl[qb]], writes=[r_T1[qb]])

                def st_c(gp=gp):
                    for qb in range(2):
                        stt("dve", Ot[qb][:], Oa[gp][:, qb * 2 + 1, 0:VD], rl[qb][:, 1:2], T1[qb][:], ALU.mult, ALU.add,
                            reads=[r_Oa[gp], r_rl[qb], r_T1[qb]], writes=[r_Ot[qb]])

                def st_d(gp=gp):
                    for qb in range(2):
                        act(osq[qb][:], Ot[qb][:], AF.Square, accum=ossq[qb][:], reads=[r_Ot[qb]], writes=[r_osq[qb], r_os[qb]])
                    for qb in range(2):
                        act(orstd[qb][:], ossq[qb][:], AF.Sqrt, bias=EPS, scale=1.0 / VD, reads=[r_os[qb]], writes=[r_os[qb]])

                def st_e(gp=gp):
                    for qb in range(2):
                        recip(orstd[qb][:], orstd[qb][:], reads=[r_os[qb]], writes=[r_os[qb]])
                        ts("dve", On[gp][qb][:], Ot[qb][:], orstd[qb][:, 0:1], None, ALU.mult, reads=[r_Ot[qb], r_os[qb]], writes=[r_On[gp][qb]])

                def fin2(h=h, j=j, g=g, gp=gp, ast=ast):
                    for qb in range(2):
                        tcol = g * 256 + qb * 128
                        pv = psb(7)
                        for c2 in range(2):
                            tr(pv[:, c2 * 128:(c2 + 1) * 128], On[gp][qb][:, c2 * 128:(c2 + 1) * 128], ident_bf[:],
                               reads=[r_On[gp][qb], R_const], writes=[PSR[7]] if c2 == 0 else (), accs=[PSR[7]] if c2 else ())
                        first_at = (g == 0 and qb == 0)
                        for c2 in range(2):
                            ts("dve", ATst[ast][:, c2, tcol:tcol + 128], pv[:, c2 * 128:(c2 + 1) * 128], sublng[:, c2:c2 + 1], None, ALU.mult,
                               reads=[PSR[7], R_const], **({"writes": [r_ATst[ast]]} if (first_at and c2 == 0) else {"accs": [r_ATst[ast]]}))
                    if g == 1:
                        for c2 in range(2):
                            dma("sp", AT_d[2 * h + c2, :, j * 512:(j + 1) * 512], ATst[ast][:, c2, :], s_AT[ast], reads=[r_ATst[ast]], accs=[R_ATd])

                st_a()
                pend2.append([st_b, st_c, st_d, st_e, fin2])
    while pend2:
        for f_ in pend2.pop(0):
            f_()
    while ada_k[0] < NADA:
        ada_panel()
    p0_issue(len(p0_jobs))

    P.barrier()
    sb.cur = persist_mark

    NT = 512
    xt = sb.alloc("xt", [128, 4, D], F32)
    r_xt = [[Res(f"xt{b}_{c}") for c in range(4)] for b in range(4)]
    s_xt = [dsem("xt") for _ in range(4)]
    s_out = [dsem("out") for _ in range(4)]
    xst = sb.alloc("xst", [128, D], F32)
    r_xst = Res("xst")
    s_xst = dsem("xst")
    hT = sb.alloc("hT", [128, 16, HALO + NT], BF16)
    r_hT = [Res(f"hT{k}") for k in range(16)]
    r_hTh = [Res(f"hTh{k}") for k in range(16)]
    xn3 = [sb.alloc(f"xn3_{i}", [128, D], BF16) for i in range(2)]
    r_xn3 = [Res(f"xn3_{i}") for i in range(2)]
    ssq3 = [sb.alloc(f"ssq3_{i}", [128, 1], F32) for i in range(2)]
    rstd3 = [sb.alloc(f"rstd3_{i}", [128, 1], F32) for i in range(2)]
    r_st3 = [Res(f"st3_{i}") for i in range(2)]
    ncnt = [0]
    NSLOT = 4
    wsl = [sb.alloc(f"wsl{i}", [128, 16, 512], BF16) for i in range(NSLOT)]
    r_wsl = [Res(f"wsl{i}") for i in range(NSLOT)]
    s_wsl = [dsem("wsl") for i in range(NSLOT)]
    wc = [0]
    pidx = [0]
    mScr = sb.cur
    ycv = sb.alloc("ycv", [128, 8, NT], F32)
    r_ycv = [Res(f"ycv{c}") for c in range(8)]
    mEnd1 = sb.cur
    sb.cur = mScr
    merged = sb.alloc("merged", [128, 16, NT], BF16)
    r_mg = [Res(f"mg{f}") for f in range(16)]
    sb.cur = mEnd1
    mGlu = sb.cur
    glu = sb.alloc("glu", [128, 8, HALO + NT], BF16)
    r_glu = [Res(f"glu{c}") for c in range(8)]
    mEnd2 = sb.cur
    sb.cur = mGlu
    zT = sb.alloc("zT", [128, 8, NT], BF16)
    r_zT = [Res(f"zT{c}") for c in range(8)]
    sb.cur = mEnd2
    atT = sb.alloc("atT", [128, 8, NT], BF16)
    r_atT = Res("atT")
    s_atT = dsem("atT")
    mScrEnd = sb.cur
    sb.cur = mScr
    actT = sb.alloc("actT", [128, 32, NT], BF16)
    r_actT = [Res(f"actT{f}") for f in range(32)]
    sb.cur = max(sb.cur, mScrEnd)
    R_alias = Res("alias")
    sg = [sb.alloc(f"sg{i}", [128, NT], F32) for i in range(2)]
    r_sg = [Res(f"sg{i}") for i in range(2)]
    sgh = [sb.alloc(f"sgh{i}", [128, HALO], F32) for i in range(2)]
    r_sgh = [Res(f"sgh{i}") for i in range(2)]
    PSR6h = [Res("ps6a"), Res("ps6b")]
    ysq = sb.alloc("ysq", [128, NT], BF16)
    r_ysq = Res("ysq")
    rstdb = sb.alloc("rstdb", [128, NT], F32)
    r_rstdb = Res("rstdb")
    tmp = [sb.alloc(f"tmp{i}", [128, NT], F32) for i in range(2)]
    r_tmp = [Res(f"tmp{i}") for i in range(2)]
    diag = [sb.alloc(f"diag{i}", [128, CK, 128], BF16) for i in range(2)]
    r_diag = [Res(f"diag{i}") for i in range(2)]
    print("P3 sbuf end", sb.cur, "of", sb.hi)

    def wload(*_a, **_k):
        slot = wc[0] % NSLOT
        wc[0] += 1
        i = NADA + pidx[0] % (NPAN - NADA)
        pidx[0] += 1
        dma("sp", wsl[slot][:].rearrange("p kc n -> p (kc n)"), WS_d[i], s_wsl[slot], reads=[R_WS], writes=[r_wsl[slot]])
        return slot

    wload2 = wload

    ccount = [0]

    def nrm(xsrc, ntok, r_x, gmod, shift, dst, r_list, first):
        i = ncnt[0] % 2
        ncnt[0] += 1
        norm_transpose(xsrc, ntok, r_x, xn3[i], r_xn3[i], ssq3[i], rstd3[i], r_st3[i], gmod, shift, dst, r_list, (6, 7), first)

    def prenorm_step(jn, step):
        if step == 0:
            dma("act", xst[0:HALO, :], xh[jn * HALO:(jn + 1) * HALO, :], s_xst, writes=[r_xst])
            nrm(xst[0:HALO, :], HALO, r_xst, gmodm, shiftm, lambda kc: hT[:, kc, 0:HALO], r_hTh, True)
        else:
            b = step - 1
            r0 = jn * 1024 + b * 128
            dma("act", xst[:], xs[r0:r0 + 128, :], s_xst, writes=[r_xst])
            nrm(xst[:], 128, r_xst, gmodm, shiftm, lambda kc, b=b: hT[:, kc, HALO + b * 128:HALO + (b + 1) * 128], r_hT, b == 0)

    for st_ in range(5):
        prenorm_step(0, st_)

    for j in range(NPAIR):
        row0 = j * 1024
        for b in range(4):
            dma("pool", xt[:, b, :], xs[row0 + b * 128:row0 + (b + 1) * 128, :], s_xt[b], writes=r_xt[b])
        dma("pool", atT[:], AT_d[:, :, j * 512:(j + 1) * 512].rearrange("c p t -> p c t"), s_atT, reads=[R_ATd, R_alias], writes=[r_atT])
        pslots = {}

        def proj(c):
            p = c % 2
            cl = c % 4
            if cl == 0:
                pslots["a"] = wload()
                pslots["g"] = wload()
            sa, sgs = pslots["a"], pslots["g"]
            bA, bG = 2 * p, 2 * p + 1
            H = PS[6][:, p * 64:(p + 1) * 64]
            for kc in range(16):
                mm(PS[bA][:], wsl[sa][:, kc, cl * 128:(cl + 1) * 128], hT[:, kc, HALO:HALO + NT], kc == 0, kc == 15,
                   reads=[r_wsl[sa], r_hT[kc]], writes=[PSR[bA]] if kc == 0 else (), accs=[PSR[bA]] if kc else ())
            for kc in range(16):
                mm(PS[bG][:], wsl[sgs][:, kc, cl * 128:(cl + 1) * 128], hT[:, kc, HALO:HALO + NT], kc == 0, kc == 15,
                   reads=[r_wsl[sgs], r_hT[kc]], writes=[PSR[bG]] if kc == 0 else (), accs=[PSR[bG]] if kc else ())
            for kc in range(16):
                mm(H[:, 0:HALO], wsl[sa][:, kc, cl * 128:(cl + 1) * 128], hT[:, kc, 0:HALO], kc == 0, kc == 15,
                   reads=[r_wsl[sa], r_hTh[kc]], writes=[PSR6h[p]] if kc == 0 else (), accs=[PSR6h[p]] if kc else ())
            for kc in range(16):
                mm(H[:, HALO:2 * HALO], wsl[sgs][:, kc, cl * 128:(cl + 1) * 128], hT[:, kc, 0:HALO], kc == 0, kc == 15,
                   reads=[r_wsl[sgs], r_hTh[kc]], accs=[PSR6h[p]])
            act(sg[p][:], PS[bG][:], AF.Sigmoid, reads=[PSR[bG]], writes=[r_sg[p]])
            tt("dve", glu[:, c, HALO:HALO + NT], PS[bA][:], sg[p][:], ALU.mult, reads=[PSR[bA], r_sg[p], R_alias], writes=[r_glu[c]])
            act(sgh[p][:], H[:, HALO:2 * HALO], AF.Sigmoid, reads=[PSR6h[p]], writes=[r_sgh[p]])
            stt("dve", glu[:, c, 0:HALO], H[:, 0:HALO], hv[:, j:j + 1], sgh[p][:], ALU.mult, ALU.mult,
                reads=[PSR6h[p], r_sgh[p], R_const], accs=[r_glu[c]])

        dsel = {}

        def diagb(c):
            di = ccount[0] % 2
            ccount[0] += 1
            dsel[c] = di
            tt("dve", diag[di][:], bc(ident_bf[:].unsqueeze(1), [128, CK, 128]), bc(cwT[:, c, :].unsqueeze(2), [128, CK, 128]), ALU.mult,
               reads=[R_const, R_cw], writes=[r_diag[di]])

        def convmm(c):
            di = dsel[c]
            bank = 4 + (c % 2)
            for t in range(CK):
                mm(PS[bank][:], diag[di][:, t, :], glu[:, c, 2 + t:2 + t + NT], t == 0, t == CK - 1,
                   reads=[r_diag[di], r_glu[c]], writes=[PSR[bank]] if t == 0 else (), accs=[PSR[bank]] if t else ())
            act(ycv[:, c, :], PS[bank][:], AF.Identity, bias=conv_bT[:, c:c + 1], scale=1.0, reads=[PSR[bank], R_const, R_alias], writes=[r_ycv[c]])
            tt("dve", ysq[:], ycv[:, c, :], ycv[:, c, :], ALU.mult, reads=[r_ycv[c]], writes=[r_ysq])

        def ssqmm(c):
            mm(PS[7][:], ones_bf[:], ysq[:], c == 0, c == 7, reads=[r_ysq, R_ones],
               writes=[PSR[7]] if c == 0 else (), accs=[PSR[7]] if c else ())

        diagb(0)
        proj(0)
        for c in range(1, 8):
            diagb(c)
            proj(c)
            if c >= 2:
                ssqmm(c - 2)
            convmm(c - 1)
        ssqmm(6)
        convmm(7)
        ssqmm(7)
        act(rstdb[:], PS[7][:], AF.Sqrt, bias=EPS, scale=1.0 / CW, reads=[PSR[7]], writes=[r_rstdb])
        recip(rstdb[:], rstdb[:], reads=[r_rstdb], writes=[r_rstdb])
        for c in range(8):
            ti = c % 2
            tt("dve", tmp[ti][:], ycv[:, c, :], rstdb[:], ALU.mult, reads=[r_ycv[c], r_rstdb], writes=[r_tmp[ti]])
            act(zT[:, c, :], tmp[ti][:], AF.Silu, scale=conv_gT[:, c:c + 1], reads=[r_tmp[ti], R_const, R_alias], writes=[r_zT[c]])
        for f4 in range(4):
            s0 = wload(w_in[:, 5120 + f4 * 512:5120 + (f4 + 1) * 512])
            s1 = wload(w_in[:, 7168 + f4 * 512:7168 + (f4 + 1) * 512])
            s2 = wload2(w_conv_out[:, f4 * 512:(f4 + 1) * 512], w_attn_out[:, f4 * 512:(f4 + 1) * 512])
            for fl in range(4):
                f = f4 * 4 + fl
                for kc in range(16):
                    mm(PS[0][:], wsl[s0][:, kc, fl * 128:(fl + 1) * 128], hT[:, kc, HALO:HALO + NT], kc == 0, kc == 15,
                       reads=[r_wsl[s0], r_hT[kc]], writes=[PSR[0]] if kc == 0 else (), accs=[PSR[0]] if kc else ())
                for kc in range(16):
                    mm(PS[1][:], wsl[s1][:, kc, fl * 128:(fl + 1) * 128], hT[:, kc, HALO:HALO + NT], kc == 0, kc == 15,
                       reads=[r_wsl[s1], r_hT[kc]], writes=[PSR[1]] if kc == 0 else (), accs=[PSR[1]] if kc else ())
                for kc in range(8):
                    mm(PS[2][:], wsl[s2][:, kc, fl * 128:(fl + 1) * 128], zT[:, kc, :], kc == 0, kc == 7,
                       reads=[r_wsl[s2], r_zT[kc]], writes=[PSR[2]] if kc == 0 else (), accs=[PSR[2]] if kc else ())
                for kc in range(8):
                    mm(PS[3][:], wsl[s2][:, 8 + kc, fl * 128:(fl + 1) * 128], atT[:, kc, :], kc == 0, kc == 7,
                       reads=[r_wsl[s2], r_atT], writes=[PSR[3]] if kc == 0 else (), accs=[PSR[3]] if kc else ())
                act(sg[0][:], PS[0][:], AF.Sigmoid, bias=gate_bT[:, f:f + 1], scale=1.0, reads=[PSR[0], R_const], writes=[r_sg[0]])
                act(sg[1][:], PS[1][:], AF.Sigmoid, bias=gate_bT[:, 16 + f:17 + f], scale=1.0, reads=[PSR[1], R_const], writes=[r_sg[1]])
                tt("dve", tmp[0][:], PS[2][:], sg[0][:], ALU.mult, reads=[PSR[2], r_sg[0]], writes=[r_tmp[0]])
                tt("dve", tmp[1][:], PS[3][:], sg[1][:], ALU.mult, reads=[PSR[3], r_sg[1]], writes=[r_tmp[1]])
                tt("pool", merged[:, f, :], tmp[0][:], tmp[1][:], ALU.add, reads=[r_tmp[0], r_tmp[1], R_alias], writes=[r_mg[f]])
        for cg in range(4):
            so = wload(w_out[:, cg * 512:(cg + 1) * 512])
            for tb in range(4):
                bank = 4 + (tb % 2)
                for kc in range(16):
                    mm(PS[bank][:], merged[:, kc, tb * 128:(tb + 1) * 128], wsl[so][:, kc, :], kc == 0, kc == 15,
                       reads=[r_wsl[so], r_mg[kc]], writes=[PSR[bank]] if kc == 0 else (), accs=[PSR[bank]] if kc else ())
                ti = tb % 2
                tt("dve", tmp[ti][:], PS[bank][:], gate_m_b[:, cg * 512:(cg + 1) * 512], ALU.mult, reads=[PSR[bank], R_gates], writes=[r_tmp[ti]])
                tt("pool", xt[:, tb, cg * 512:(cg + 1) * 512], xt[:, tb, cg * 512:(cg + 1) * 512], tmp[ti][:], ALU.add,
                   reads=[r_tmp[ti], r_xt[tb][cg]], writes=[r_xt[tb][cg]])
        fence_reads = r_glu + r_ycv + r_zT + [r_atT] + r_mg
        P.op("pool", lambda e: e.memset(dummy[:, 2:3], 0.0), writes=fence_reads + [R_alias])
        for b in range(4):
            jr = Res()
            P.op("pool", lambda e: e.memset(dummy[:, 3:4], 0.0), reads=r_xt[b], writes=[jr])
            nrm(xt[:, b, :], 128, jr, gmodf, shiftf, lambda kc, b=b: hT[:, kc, HALO + b * 128:HALO + (b + 1) * 128], r_hT, b == 0)
        for half in range(2):
            for fg in range(8):
                sm = wload(w_mlp_in[:, half * 4096 + fg * 512:half * 4096 + (fg + 1) * 512])
                for fl in range(4):
                    fc = fg * 4 + fl
                    bank = fc % 2
                    for kc in range(16):
                        mm(PS[bank][:], wsl[sm][:, kc, fl * 128:(fl + 1) * 128], hT[:, kc, HALO:HALO + NT], kc == 0, kc == 15,
                           reads=[r_wsl[sm], r_hT[kc]], writes=[PSR[bank]] if kc == 0 else (), accs=[PSR[bank]] if kc else ())
                    si = fc % 2
                    act(sg[si][:], PS[bank][:], AF.Relu, reads=[PSR[bank]], writes=[r_sg[si]])
                    tt("dve", actT[:, fc, :], PS[bank][:], sg[si][:], ALU.mult, reads=[PSR[bank], r_sg[si], R_alias], writes=[r_actT[fc]])
            for cg in range(4):
                slots = [wload(w_mlp_out[half * 4096 + kg * 2048:half * 4096 + (kg + 1) * 2048, cg * 512:(cg + 1) * 512]) for kg in range(2)]
                for tb in range(4):
                    bank = 2 + tb
                    for kg in range(2):
                        for kc in range(16):
                            k = kg * 16 + kc
                            mm(PS[bank][:], actT[:, k, tb * 128:(tb + 1) * 128], wsl[slots[kg]][:, kc, :], k == 0, k == 31,
                               reads=[r_wsl[slots[kg]], r_actT[k]], writes=[PSR[bank]] if k == 0 else (), accs=[PSR[bank]] if k else ())
                    ti = tb % 2
                    tt("dve", tmp[ti][:], PS[bank][:], gate_f_b[:, cg * 512:(cg + 1) * 512], ALU.mult, reads=[PSR[bank], R_gates], writes=[r_tmp[ti]])
                    tt("pool", xt[:, tb, cg * 512:(cg + 1) * 512], xt[:, tb, cg * 512:(cg + 1) * 512], tmp[ti][:], ALU.add,
                       reads=[r_tmp[ti], r_xt[tb][cg]], writes=[r_xt[tb][cg]])
                if half == 1 and j + 1 < NPAIR:
                    if cg == 0:
                        prenorm_step(j + 1, 0)
                    prenorm_step(j + 1, cg + 1)
        P.op("pool", lambda e: e.memset(dummy[:, 4:5], 0.0), writes=r_actT + [R_alias])
        for b in range(4):
            dma("pool", out_d[j * 512 + b * 128:j * 512 + (b + 1) * 128, :], xt[:, b, :], s_out[b], reads=r_xt[b])
    P.op("sp", lambda e: e.nop(), extra=list(P.dmas))

    P.finalize()
    P.emit()
    info = dict(counts=P.counts, nwaits=P.nwaits, nops=len(P.all))
    return nc, info


_CACHE = {}


def _fm(v, n):
    return np.ascontiguousarray(np.asarray(v, dtype=np.float32).reshape(n, 128).T)


def prep_inputs(inputs, NPAIR, cores):
    S = NPAIR * 1024
    D_ = D
    shared = dict(
        ident=np.eye(128, dtype=np.float32),
        invf=np.ascontiguousarray(np.broadcast_to(
            (10000.0 ** (-np.arange(0, 128, 2, dtype=np.float32) / np.float32(128))).astype(np.float32)[None, :], (128, 64))),
        ada_w=np.ascontiguousarray(inputs["ada_w"][0]),
        ada_bT=_fm(inputs["ada_b"][0], 96),
        ada_b_row=np.ascontiguousarray(inputs["ada_b"][0][None, :]),
        nmgT=_fm(inputs["norm_mix_g"][0], 16),
        nfgT=_fm(inputs["norm_mlp_g"][0], 16),
        w_in=np.ascontiguousarray(inputs["w_in"][0]),
        conv_w=np.ascontiguousarray(inputs["conv_w"][0]),
        conv_bT=_fm(inputs["conv_b"][0], 8),
        conv_gT=_fm(inputs["conv_norm_g"][0], 8),
        w_conv_out=np.ascontiguousarray(inputs["w_conv_out"][0]),
        qg_b=np.ascontiguousarray(np.broadcast_to(inputs["q_norm_g"][0][None, :], (128, 128))),
        kg_b=np.ascontiguousarray(np.broadcast_to(inputs["k_norm_g"][0][None, :], (128, 128))),
        lam_b=np.ascontiguousarray(np.broadcast_to(np.stack([inputs["lambda_q1"][0], inputs["lambda_k1"][0],
                                                             inputs["lambda_q2"][0], inputs["lambda_k2"][0]])[None], (128, 4, 128))),
        sublnT=_fm(inputs["subln_g"][0], 2),
        w_attn_out=np.ascontiguousarray(inputs["w_attn_out"][0]),
        gate_bT=_fm(inputs["gate_b"][0], 32),
        w_out=np.ascontiguousarray(inputs["w_out"][0]),
        w_mlp_in=np.ascontiguousarray(inputs["w_mlp_in"][0]),
        w_mlp_out=np.ascontiguousarray(inputs["w_mlp_out"][0]),
    )
    shared["invf"] = np.ascontiguousarray(np.broadcast_to(
        np.power(np.float32(10000.0), -(np.arange(0, 128, 2, dtype=np.float32) / np.float32(128))).astype(np.float32)[None, :], (128, 64)))
    maps = []
    for core in cores:
        b, par = core // 2, core % 2
        x = np.asarray(inputs["x"][b][:S], dtype=np.float32)
        pos = np.asarray(inputs["pos"][b][:S], dtype=np.int32)
        order = []
        for j in range(NPAIR):
            order += [2 * j + par, 2 * j + 1 - par]
        xs = np.ascontiguousarray(x.reshape(2 * NPAIR, 512, D_)[order].reshape(S, D_))
        posr = pos.reshape(2 * NPAIR, 512)[order].reshape(S)
        posc = np.ascontiguousarray(posr.reshape(S // 128, 128).T)
        xh = np.zeros((NPAIR * HALO, D_), np.float32)
        hv = np.zeros((128, NPAIR), np.float32)
        for j in range(NPAIR):
            st = (2 * j + par) * 512
            if st > 0:
                xh[j * HALO:(j + 1) * HALO] = x[st - HALO:st]
                hv[:, j] = 1.0
        m = dict(shared)
        m.update(xs=xs, xh=xh, hv=hv, posc=posc, cT=_fm(inputs["c"][b], 16),
                 obias=np.full((128, 1), 0.0 if par else -30000.0, np.float32))
        maps.append(m)
    return maps


def run(inputs, NPAIR, cores, debug=False):
    key = (NPAIR, debug)
    if key not in _CACHE:
        _CACHE[key] = build(NPAIR, debug)
    nc, info = _CACHE[key]
    maps = prep_inputs(inputs, NPAIR, cores)
    res = run_bass_kernel_spmd(nc, maps, core_ids=list(range(len(cores))))
    return res, info


def kernel(**inputs):
    inputs = {k: np.asarray(v) for k, v in inputs.items()}
    B, S, _ = inputs["x"].shape
    NPAIR = S // 1024
    cores = list(range(2 * B))
    res, _ = run(inputs, NPAIR, cores)
    out = np.empty((B, S, D), np.float32)
    for ci, core in enumerate(cores):
        b, par = core // 2, core % 2
        o = res.results[ci]["out"].reshape(NPAIR, 512, D)
        ov = out[b].reshape(2 * NPAIR, 512, D)
        for j in range(NPAIR):
            ov[2 * j + par] = o[j]
    return out
```

```python
import math
import numpy as np
import concourse.bass as bass
import concourse.mybir as mybir
from concourse.bass_utils import run_bass_kernel_spmd

F32 = mybir.dt.float32
BF16 = mybir.dt.bfloat16
I32 = mybir.dt.int32
AF = mybir.ActivationFunctionType
ALU = mybir.AluOpType
AX = mybir.AxisListType

D = 2048
HD = 128
NH = 4
VD = 256
CW = 1024
CK = 31
DFF = 8192
EPS = 1e-6
LAM_INIT = 0.8 - 0.6 * math.exp(0.0)
HALO = 32
C1 = 6.28125
C2 = 0.001934051513671875
C3 = float(2 * math.pi - C1 - C2)

ENGS = ("pe", "act", "dve", "pool", "sp")
EPOCH = 8000

OPT_NORM_ACT = False
OPT_P1_ACT = False


class Res:
    __slots__ = ("name", "writers", "readers", "pw", "pr")

    def __init__(self, name=""):
        self.name = name
        self.writers = []
        self.readers = []
        self.pw = []
        self.pr = []


class DmaSem:
    def __init__(self, handle):
        self.handle = handle
        self.count = 0


class Op:
    __slots__ = ("eng", "fn", "deps", "signal", "ev", "waits", "dsem", "snap")

    def __init__(self, eng, fn, dsem=None):
        self.eng = eng
        self.fn = fn
        self.deps = ()
        self.signal = False
        self.ev = None
        self.waits = ()
        self.dsem = dsem
        self.snap = None


class Prog:
    def __init__(self, nc):
        self.nc = nc
        self.all = []
        self._semctx = []
        self.last = {}
        self.dmas = []

    def new_sem(self, name):
        cm = self.nc.semaphore(name)
        h = cm.__enter__()
        self._semctx.append(cm)
        return h

    def dma_sem(self, name):
        return DmaSem(self.new_sem(name))

    def op(self, eng, fn, reads=(), writes=(), accs=(), dsem=None, extra=(), nobar=False):
        o = Op(eng, fn, dsem)
        deps = set(extra)
        for r in reads:
            deps.update(r.writers)
            r.readers.append(o)
        for w in writes:
            deps.update(w.writers)
            deps.update(w.readers)
            w.pw = w.writers
            w.pr = w.readers
            w.writers = [o]
            w.readers = []
        for a in accs:
            deps.update(a.pw)
            deps.update(a.pr)
            a.writers.append(o)
        deps.discard(o)
        if eng == "pe":
            deps = [d for d in deps if d.eng != "pe" or d.dsem is not None]
        o.deps = tuple(deps)
        if dsem is not None:
            dsem.count += 1
            o.ev = (dsem.handle, 16 * dsem.count)
            o.signal = True
            if not nobar:
                self.dmas.append(o)
        else:
            self.last[eng] = o
        self.all.append(o)
        return o

    def barrier(self):
        deps = list(self.last.values()) + list(self.dmas)
        self.dmas = []
        for e in ENGS:
            o = Op(e, lambda eng: eng.nop())
            o.deps = tuple(deps)
            self.all.append(o)

    def finalize(self):
        for o in self.all:
            for d in o.deps:
                d.signal = True
        counts = {e: 0 for e in ENGS}
        esems = {e: [] for e in ENGS}
        for o in self.all:
            if o.dsem is None and o.signal:
                c = counts[o.eng]
                k = c // EPOCH
                while len(esems[o.eng]) <= k:
                    esems[o.eng].append(self.new_sem(f"s_{o.eng}_{len(esems[o.eng])}"))
                o.ev = (esems[o.eng][k], c % EPOCH + 1)
                counts[o.eng] = c + 1
        known = {e: {} for e in ENGS}
        nwaits = 0
        for o in self.all:
            kn = known[o.eng]
            need = {}
            for d in o.deps:
                s, v = d.ev
                if kn.get(s, 0) < v and need.get(s, (0, None))[0] < v:
                    need[s] = (v, d)
            waits = []
            for s, (v, d) in need.items():
                if kn.get(s, 0) >= v:
                    continue
                waits.append((s, v))
                for s2, v2 in d.snap.items():
                    if kn.get(s2, 0) < v2:
                        kn[s2] = v2
                if kn.get(s, 0) < v:
                    kn[s] = v
            o.waits = tuple(waits)
            nwaits += len(waits)
            if o.signal:
                sn = dict(kn)
                s, v = o.ev
                if sn.get(s, 0) < v:
                    sn[s] = v
                o.snap = sn
        self.counts = counts
        self.nwaits = nwaits

    def emit(self):
        nc = self.nc
        per = {e: [o for o in self.all if o.eng == e] for e in ENGS}

        def run(eng, ops):
            for o in ops:
                for s, v in o.waits:
                    eng.wait_ge(s, v)
                ins = o.fn(eng)
                if o.signal:
                    ins.then_inc(o.ev[0], 16 if o.dsem is not None else 1)

        with nc.Block() as block:
            @block.tensor
            def _(e):
                run(e, per["pe"])

            @block.scalar
            def _(e):
                run(e, per["act"])

            @block.vector
            def _(e):
                run(e, per["dve"])

            @block.gpsimd
            def _(e):
                run(e, per["pool"])

            @block.sync
            def _(e):
                run(e, per["sp"])


class SbPool:
    def __init__(self, nc, lo=16640, hi=229376):
        self.nc = nc
        self.cur = lo
        self.hi = hi
        self.n = 0

    def alloc(self, name, shape, dt):
        esz = 4 if dt in (F32, I32) else 2
        per = esz
        for s in shape[1:]:
            per *= s
        per = (per + 63) // 64 * 64
        assert self.cur + per <= self.hi, f"SBUF overflow allocating {name}: {self.cur}+{per}"
        self.n += 1
        t = self.nc.alloc_sbuf_tensor_at(f"{name}_{self.n}", list(shape), dt, offset=self.cur)
        self.cur += per
        return t


def bc(ap, shape):
    return ap.to_broadcast(list(shape))


def build(NPAIR, debug=False):
    NB = NPAIR * 8
    S = NB * 128
    NOWN = NPAIR * 512
    nc = bass.Bass("TRN2", target_bir_lowering=False)
    P = Prog(nc)
    sb = SbPool(nc)

    def din(name, shape, dt=F32):
        return nc.dram_tensor(name, list(shape), dt, kind="ExternalInput").ap()

    xs = din("xs", [S, D])
    xh = din("xh", [NPAIR * HALO, D])
    hv_d = din("hv", [128, NPAIR])
    pos_d = din("posc", [128, NB], I32)
    invf_d = din("invf", [128, 64])
    c_d = din("cT", [128, 16])
    ident_d = din("ident", [128, 128])
    obias_d = din("obias", [128, 1])
    ada_w = din("ada_w", [D, 6 * D])
    ada_bT = din("ada_bT", [128, 96])
    ada_b_row = din("ada_b_row", [1, 6 * D])
    nmg_d = din("nmgT", [128, 16])
    nfg_d = din("nfgT", [128, 16])
    w_in = din("w_in", [D, 9216])
    conv_w_d = din("conv_w", [CK, CW])
    conv_b_d = din("conv_bT", [128, 8])
    conv_g_d = din("conv_gT", [128, 8])
    w_conv_out = din("w_conv_out", [CW, D])
    qg_d = din("qg_b", [128, 128])
    kg_d = din("kg_b", [128, 128])
    lam_d = din("lam_b", [128, 4, 128])
    subln_d = din("sublnT", [128, 2])
    w_attn_out = din("w_attn_out", [CW, D])
    gate_b_d = din("gate_bT", [128, 32])
    w_out = din("w_out", [D, D])
    w_mlp_in = din("w_mlp_in", [D, DFF])
    w_mlp_out = din("w_mlp_out", [DFF, D])
    out_d = nc.dram_tensor("out", [NOWN, D], F32, kind="ExternalOutput").ap()
    skind = "ExternalOutput" if debug else "Internal"
    KT_d = nc.dram_tensor("KT_s", [NH, 128, 2, S], BF16, kind=skind).ap()
    V_d = nc.dram_tensor("V_s", [S, NH * VD], BF16, kind=skind).ap()
    QT_d = nc.dram_tensor("QT_s", [NH, 128, 2, NOWN], BF16, kind=skind).ap()
    AT_d = nc.dram_tensor("AT_s", [8, 128, NOWN], BF16, kind=skind).ap()

    PS = [nc.alloc_psum_tensor(f"ps{i}", [128, 512], F32) for i in range(8)]
    PSR = [Res(f"ps{i}") for i in range(8)]

    def psb(i):
        return PS[i][:].bitcast(BF16)

    ident_bf = sb.alloc("ident_bf", [128, 128], BF16)
    ident_f = sb.alloc("ident_f", [128, 128], F32)
    ones_bf = sb.alloc("ones_bf", [128, 128], BF16)
    gmodm = sb.alloc("gmodm", [128, 16], F32)
    shiftm = sb.alloc("shiftm", [128, 16], F32)
    gmodf = sb.alloc("gmodf", [128, 16], F32)
    shiftf = sb.alloc("shiftf", [128, 16], F32)
    gate_m_b = sb.alloc("gate_m_b", [128, D], F32)
    gate_f_b = sb.alloc("gate_f_b", [128, D], F32)
    obias = sb.alloc("obias", [128, 1], F32)
    nlam = sb.alloc("nlam", [128, 1], F32)
    sublng = sb.alloc("sublng", [128, 2], F32)
    gate_bT = sb.alloc("gate_bT", [128, 32], F32)
    conv_bT = sb.alloc("conv_bT", [128, 8], F32)
    conv_gT = sb.alloc("conv_gT", [128, 8], F32)
    cwT = sb.alloc("cwT", [128, 8, CK], F32)
    hv = sb.alloc("hv", [128, NPAIR], F32)
    dummy = sb.alloc("dummy", [128, 16], F32)
    R_const = Res("const")
    R_mod = Res("mod")
    R_gates = Res("gatesb")
    R_lam = Res("lam")
    R_cw = Res("cw")
    persist_mark = sb.cur

    dq = [0]

    def dsem(name):
        dq[0] += 1
        return P.dma_sem(f"{name}{dq[0]}")

    def dma(eng, out, in_, sem, reads=(), writes=(), accs=(), nobar=False):
        return P.op(eng, lambda e: e.dma_start(out=out, in_=in_), reads=reads, writes=writes, accs=accs, dsem=sem, nobar=nobar)

    def mm(out, lhsT, rhs, start, stop, reads=(), writes=(), accs=()):
        return P.op("pe", lambda e: e.matmul(out, lhsT=lhsT, rhs=rhs, start=start, stop=stop),
                    reads=reads, writes=writes, accs=accs)

    def tr(out, in_, idn, reads=(), writes=(), accs=()):
        return P.op("pe", lambda e: e.transpose(out, in_, idn), reads=reads, writes=writes, accs=accs)

    def act(out, in_, func, bias=None, scale=None, accum=None, reads=(), writes=(), accs=()):
        kw = {}
        if bias is not None:
            kw["bias"] = bias
        if scale is not None:
            kw["scale"] = scale
        if accum is not None:
            kw["accum_out"] = accum
        return P.op("act", lambda e: e.activation(out=out, in_=in_, func=func, **kw), reads=reads, writes=writes, accs=accs)

    def tt(eng, out, in0, in1, op, reads=(), writes=(), accs=()):
        return P.op(eng, lambda e: e.tensor_tensor(out=out, in0=in0, in1=in1, op=op), reads=reads, writes=writes, accs=accs)

    def ts(eng, out, in0, s1, s2, op0, op1=None, reads=(), writes=(), accs=()):
        if op1 is None:
            return P.op(eng, lambda e: e.tensor_scalar(out=out, in0=in0, scalar1=s1, scalar2=None, op0=op0), reads=reads, writes=writes, accs=accs)
        return P.op(eng, lambda e: e.tensor_scalar(out=out, in0=in0, scalar1=s1, scalar2=s2, op0=op0, op1=op1), reads=reads, writes=writes, accs=accs)

    def stt(eng, out, in0, scalar, in1, op0, op1, reads=(), writes=(), accs=()):
        return P.op(eng, lambda e: e.scalar_tensor_tensor(out=out, in0=in0, scalar=scalar, in1=in1, op0=op0, op1=op1),
                    reads=reads, writes=writes, accs=accs)

    def cp(eng, out, in_, reads=(), writes=(), accs=()):
        return P.op(eng, lambda e: e.tensor_copy(out=out, in_=in_), reads=reads, writes=writes, accs=accs)

    def recip(out, in_, reads=(), writes=(), accs=()):
        return P.op("dve", lambda e: e.reciprocal(out=out, in_=in_), reads=reads, writes=writes, accs=accs)

    def mset(eng, ap, val, reads=(), writes=(), accs=()):
        return P.op(eng, lambda e: e.memset(ap, val), reads=reads, writes=writes, accs=accs)

    def rstd_from_ssq(rstd, ssq, inv_n, r_in, r_out):
        act(rstd, ssq, AF.Sqrt, bias=EPS, scale=inv_n, reads=[r_in], writes=[r_out])
        recip(rstd, rstd, reads=[r_out], writes=[r_out])

    cos_t = sb.alloc("cos_t", [128, NB, 64], F32)
    sin_t = sb.alloc("sin_t", [128, NB, 64], F32)
    qg_b = sb.alloc("qg_b", [128, 128], F32)
    kg_b = sb.alloc("kg_b", [128, 128], F32)
    qgsw = sb.alloc("qgsw", [128, 128], F32)
    kgsw = sb.alloc("kgsw", [128, 128], F32)
    mP1 = sb.cur
    s_c = dsem("c")
    dma("sp", ident_f[:], ident_d, s_c, writes=[R_const])
    dma("pool", ident_bf[:], ident_d, dsem("cq"), accs=[R_const])
    dma("sp", obias[:], obias_d, s_c, accs=[R_const])
    dma("sp", qg_b[:], qg_d, s_c, accs=[R_const])
    dma("sp", kg_b[:], kg_d, s_c, accs=[R_const])
    dma("sp", sublng[:], subln_d, s_c, accs=[R_const])
    dma("sp", gate_bT[:], gate_b_d, s_c, accs=[R_const])
    dma("sp", conv_bT[:], conv_b_d, s_c, accs=[R_const])
    dma("sp", conv_gT[:], conv_g_d, s_c, accs=[R_const])
    dma("sp", hv[:], hv_d, s_c, accs=[R_const])
    cT = sb.alloc("cT", [128, 16], F32)
    lam_t = sb.alloc("lam_t", [128, 4, 128], F32)
    cw_nat = sb.alloc("cw_nat", [CK, CW], F32)
    nmg = sb.alloc("nmg", [128, 16], F32)
    nfg = sb.alloc("nfg", [128, 16], F32)
    adabT = sb.alloc("adabT", [128, 96], F32)
    dma("sp", cT[:], c_d, s_c, accs=[R_const])
    dma("sp", lam_t[:], lam_d, s_c, accs=[R_const])
    dma("sp", cw_nat[:], conv_w_d, s_c, accs=[R_const])
    dma("sp", nmg[:], nmg_d, s_c, accs=[R_const])
    dma("sp", nfg[:], nfg_d, s_c, accs=[R_const])
    dma("sp", adabT[:], ada_bT, s_c, accs=[R_const])
    dma("sp", gate_m_b[:], ada_b_row[:, 2 * D:3 * D].partition_broadcast(128), s_c, accs=[R_const])
    dma("sp", gate_f_b[:], ada_b_row[:, 5 * D:6 * D].partition_broadcast(128), s_c, accs=[R_const])

    R_ones = Res("ones")
    P.op("pool", lambda e: e.memset(ones_bf[:], 1.0), writes=[R_ones])

    lsc = sb.alloc("lsc", [128, 2, 128], F32)
    lsum = sb.alloc("lsum", [128, 2], F32)
    tt("dve", lsc[:, 0, :], lam_t[:, 0, :], lam_t[:, 1, :], ALU.mult, reads=[R_const], writes=[R_lam])
    tt("dve", lsc[:, 1, :], lam_t[:, 2, :], lam_t[:, 3, :], ALU.mult, reads=[R_const, R_lam], writes=[R_lam])
    P.op("dve", lambda e: e.reduce_sum(out=lsum[:], in_=lsc[:], axis=AX.X), reads=[R_lam], writes=[R_lam])
    act(lsum[:], lsum[:], AF.Exp, reads=[R_lam], writes=[R_lam])
    tt("dve", nlam[:], lsum[:, 1:2], lsum[:, 0:1], ALU.subtract, reads=[R_lam], writes=[R_lam])
    ts("dve", nlam[:], nlam[:], -LAM_INIT, None, ALU.add, reads=[R_lam], writes=[R_lam])
    ts("dve", sublng[:], sublng[:], 1.0 - LAM_INIT, None, ALU.mult, reads=[R_const], writes=[R_const])
    R_gsw = Res("gsw")
    ts("dve", qgsw[:, 0:64], qg_b[:, 64:128], -1.0, None, ALU.mult, reads=[R_const], writes=[R_gsw])
    cp("dve", qgsw[:, 64:128], qg_b[:, 0:64], reads=[R_const], accs=[R_gsw])
    ts("dve", kgsw[:, 0:64], kg_b[:, 64:128], -1.0, None, ALU.mult, reads=[R_const], accs=[R_gsw])
    cp("dve", kgsw[:, 64:128], kg_b[:, 0:64], reads=[R_const], accs=[R_gsw])

    for c in range(8):
        P.op("pe", lambda e, c=c: e.transpose(PS[0][:, c * 32:c * 32 + CK], cw_nat[0:CK, c * 128:(c + 1) * 128], ident_f[0:CK, 0:CK]),
             reads=[R_const], writes=[PSR[0]] if c == 0 else (), accs=[PSR[0]] if c else ())
    cp("dve", cwT[:], PS[0][:, 0:256].rearrange("p (c j) -> p c j", j=32)[:, :, 0:CK], reads=[PSR[0]], writes=[R_cw])

    cact = sb.alloc("cact", [128, 16], BF16)
    cact_rep = sb.alloc("cact_rep", [128, 16, 128], BF16)
    R_cact = Res("cact")
    act(cact[:], cT[:], AF.Silu, reads=[R_const], writes=[R_cact])
    cp("dve", cact_rep[:], bc(cact[:].unsqueeze(2), [128, 16, 128]), reads=[R_cact], accs=[R_cact])

    NAP = 3
    apan = [sb.alloc(f"apan{i}", [128, 16, 512], BF16) for i in range(NAP)]
    apan_r = [Res(f"apan{i}") for i in range(NAP)]
    apan_s = [dsem("apan") for i in range(NAP)]
    adaT = sb.alloc("adaT", [128, 96], F32)
    R_adaT = Res("adaT")
    pi = 0
    for seg in range(2):
        for g in range(4):
            col0 = seg * D + g * 512
            slot = pi % NAP
            pi += 1
            dma("pool", apan[slot][:], ada_w[:, col0:col0 + 512].rearrange("(kc p) n -> p kc n", p=128), apan_s[slot], writes=[apan_r[slot]])
            if seg in (2, 5):
                bank = 1 + (pi % 2)
                for kc in range(16):
                    mm(PS[bank][:], cact_rep[:, kc, :], apan[slot][:, kc, :], kc == 0, kc == 15,
                       reads=[R_cact, apan_r[slot]], writes=[PSR[bank]] if kc == 0 else (), accs=[PSR[bank]] if kc else ())
                dst = gate_m_b if seg == 2 else gate_f_b
                tt("dve", dst[:, g * 512:(g + 1) * 512], PS[bank][:], dst[:, g * 512:(g + 1) * 512], ALU.add,
                   reads=[PSR[bank], R_const], accs=[R_gates])
            else:
                for ch in range(4):
                    j = (col0 + ch * 128) // 128
                    bank = 3
                    for kc in range(16):
                        first = (kc == 0 and ch == 0)
                        mm(PS[bank][:, ch:ch + 1], apan[slot][:, kc, ch * 128:(ch + 1) * 128], cact[:, kc:kc + 1], kc == 0, kc == 15,
                           reads=[R_cact, apan_r[slot]], writes=[PSR[bank]] if first else (), accs=() if first else [PSR[bank]])
                j0 = col0 // 128
                tt("dve", adaT[:, j0:j0 + 4], PS[3][:, 0:4], adabT[:, j0:j0 + 4], ALU.add, reads=[PSR[3], R_const], accs=[R_adaT])
    ts("dve", gmodm[:], adaT[:, 16:32], 1.0, None, ALU.add, reads=[R_adaT], writes=[R_mod])
    tt("dve", gmodm[:], gmodm[:], nmg[:], ALU.mult, reads=[R_mod, R_const], writes=[R_mod])
    cp("dve", shiftm[:], adaT[:, 0:16], reads=[R_adaT], accs=[R_mod])

    posi = sb.alloc("posi", [128, NB], I32)
    posf = sb.alloc("posf", [128, NB], F32)
    invf = sb.alloc("invf", [128, 64], F32)
    ang = sb.alloc("ang", [128, NB, 64], F32)
    yy = sb.alloc("yy", [128, NB, 64], F32)
    y0 = sb.alloc("y0", [128, NB, 64], F32)
    dd = sb.alloc("dd", [128, NB, 64], F32)
    R_tab = Res("tab")
    R_t = Res("tabtmp")
    s_p = dsem("pos")
    dma("sp", posi[:], pos_d, s_p, writes=[R_t])
    dma("sp", invf[:], invf_d, s_p, accs=[R_t])
    cp("dve", posf[:], posi[:], reads=[R_t], writes=[R_t])
    tt("dve", ang[:], bc(posf[:].unsqueeze(2), [128, NB, 64]), bc(invf[:].unsqueeze(1), [128, NB, 64]), ALU.mult, reads=[R_t], writes=[R_t])
    ts("dve", y0[:], ang[:], 1.0 / (2 * math.pi), None, ALU.mult, reads=[R_t], writes=[R_t])
    cp("dve", yy[:], y0[:], reads=[R_t], writes=[R_t])
    for jb in range(13, -1, -1):
        pw = float(2 ** jb)
        ts("dve", dd[:], yy[:], pw, pw, ALU.is_ge, ALU.mult, reads=[R_t], writes=[R_t])
        tt("dve", yy[:], yy[:], dd[:], ALU.subtract, reads=[R_t], writes=[R_t])
    tt("dve", y0[:], y0[:], yy[:], ALU.subtract, reads=[R_t], writes=[R_t])
    stt("dve", ang[:], y0[:], -C1, ang[:], ALU.mult, ALU.add, reads=[R_t], writes=[R_t])
    stt("dve", ang[:], y0[:], -C2, ang[:], ALU.mult, ALU.add, reads=[R_t], writes=[R_t])
    stt("dve", ang[:], y0[:], -C3, ang[:], ALU.mult, ALU.add, reads=[R_t], writes=[R_t])
    ts("dve", yy[:], ang[:], -1.0, math.pi, ALU.mult, ALU.add, reads=[R_t], writes=[R_t])
    ts("dve", yy[:], yy[:], math.pi, -math.pi, ALU.min, ALU.max, reads=[R_t], writes=[R_t])
    act(sin_t[:], yy[:], AF.Sin, reads=[R_t], writes=[R_tab])
    act(dd[:], ang[:], AF.Abs, bias=-math.pi, scale=1.0, reads=[R_t], writes=[R_t])
    ts("dve", dd[:], dd[:], -math.pi / 2, -math.pi / 2, ALU.add, ALU.max, reads=[R_t], writes=[R_t])
    ts("dve", dd[:], dd[:], math.pi / 2, None, ALU.min, reads=[R_t], writes=[R_t])
    act(cos_t[:], dd[:], AF.Sin, reads=[R_t], accs=[R_tab])

    P.barrier()
    sb.cur = mP1

    def norm_transpose(xsrc, ntok, r_x, xn, r_xn, ssq, rstd, r_st, gmod, shift, hT_dst, r_hT_list, psbank, first_write):
        act(xn[0:ntok, :], xsrc, AF.Square, accum=ssq[0:ntok, :], reads=[r_x], writes=[r_xn, r_st])
        rstd_from_ssq(rstd[0:ntok, :], ssq[0:ntok, :], 1.0 / D, r_st, r_st)
        act(xn[0:ntok, :], xsrc, AF.Copy, scale=rstd[0:ntok, 0:1], reads=[r_x, r_st], writes=[r_xn])
        banks = psbank if isinstance(psbank, (tuple, list)) else (psbank,)
        for q4 in range(4):
            bk = banks[q4 % len(banks)]
            pv = psb(bk)
            for i in range(4):
                kc = q4 * 4 + i
                tr(pv[:, i * 128:i * 128 + ntok], xn[0:ntok, kc * 128:(kc + 1) * 128], ident_bf[0:ntok, 0:ntok],
                   reads=[r_xn, R_const], writes=[PSR[bk]] if i == 0 else (), accs=[PSR[bk]] if i else ())
            for i in range(4):
                kc = q4 * 4 + i
                kw = {"writes": [r_hT_list[kc]]} if first_write else {"accs": [r_hT_list[kc]]}
                if i % 2 == 0 or not OPT_NORM_ACT:
                    ts("dve", hT_dst(kc), pv[:, i * 128:i * 128 + ntok], gmod[:, kc:kc + 1], shift[:, kc:kc + 1], ALU.mult, ALU.add,
                       reads=[PSR[bk], R_mod], **kw)
                else:
                    act(hT_dst(kc), pv[:, i * 128:i * 128 + ntok], AF.Identity, bias=shift[:, kc:kc + 1], scale=gmod[:, kc:kc + 1],
                        reads=[PSR[bk], R_mod], **kw)

    wqkv = sb.alloc("wqkv", [128, 16, 3072], BF16)
    R_w = Res("wqkv")
    s_w = dsem("wqkv")
    for g in range(6):
        dma("pool", wqkv[:, :, g * 512:(g + 1) * 512], w_in[:, 2048 + g * 512:2048 + (g + 1) * 512].rearrange("(kc p) n -> p kc n", p=128),
            s_w, **({"writes": [R_w]} if g == 0 else {"accs": [R_w]}))
    panels = []
    ADA_SEGS = [2, 3, 4, 5]
    for seg in ADA_SEGS:
        for g in range(4):
            panels.append([(ada_w[:, seg * D + g * 512:seg * D + (g + 1) * 512], 0, 16)])
    NADA = len(panels)
    for c4 in range(2):
        panels.append([(w_in[:, c4 * 512:(c4 + 1) * 512], 0, 16)])
        panels.append([(w_in[:, 1024 + c4 * 512:1024 + (c4 + 1) * 512], 0, 16)])
    for f4 in range(4):
        panels.append([(w_in[:, 5120 + f4 * 512:5120 + (f4 + 1) * 512], 0, 16)])
        panels.append([(w_in[:, 7168 + f4 * 512:7168 + (f4 + 1) * 512], 0, 16)])
        panels.append([(w_conv_out[:, f4 * 512:(f4 + 1) * 512], 0, 8), (w_attn_out[:, f4 * 512:(f4 + 1) * 512], 8, 8)])
    for cg in range(4):
        panels.append([(w_out[:, cg * 512:(cg + 1) * 512], 0, 16)])
    for half in range(2):
        for fg in range(8):
            panels.append([(w_mlp_in[:, half * 4096 + fg * 512:half * 4096 + (fg + 1) * 512], 0, 16)])
        for cg in range(4):
            for kg in range(2):
                panels.append([(w_mlp_out[half * 4096 + kg * 2048:half * 4096 + (kg + 1) * 2048, cg * 512:(cg + 1) * 512], 0, 16)])
    NPAN = len(panels)
    WS_d = nc.dram_tensor("WS_s", [NPAN, 128, 16 * 512], BF16).ap()
    R_WS = Res("WS")
    s_WS = dsem("WS")
    R_WSa = Res("WSa")
    s_WSa = dsem("WSa")
    p0_jobs = []
    for i, plist in enumerate(panels):
        for (src, kc0, kcs) in plist:
            p0_jobs.append((i, src, kc0, kcs))
    p0_pos = [0]

    def p0_issue(n):
        for _ in range(n):
            if p0_pos[0] >= len(p0_jobs):
                return
            i, src, kc0, kcs = p0_jobs[p0_pos[0]]
            p0_pos[0] += 1
            dma("pool", WS_d[i].rearrange("p (kc n) -> p kc n", n=512)[:, kc0:kc0 + kcs, :],
                src.rearrange("(kc p) n -> p kc n", p=128), s_WSa if i < NADA else s_WS, accs=[R_WSa if i < NADA else R_WS], nobar=True)

    p0_every = max(1, NB // NADA)
    p0_per_blk = (NADA + NB - 1) // NB

    xb = [sb.alloc(f"xb{i}", [128, D], F32) for i in range(2)]
    xb_r = [Res(f"xb{i}") for i in range(2)]
    xb_s = [dsem("xb") for i in range(2)]
    xn1 = sb.alloc("xn1", [128, D], BF16)
    r_xn1 = Res("xn1")
    ssq1 = sb.alloc("ssq1", [128, 1], F32)
    rstd1 = sb.alloc("rstd1", [128, 1], F32)
    r_st1 = Res("st1")
    hTb = [sb.alloc(f"hTb{i}", [128, 16, 128], BF16) for i in range(2)]
    hTb_r = [[Res(f"hTb{i}_{k}") for k in range(16)] for i in range(2)]
    NSET = 2
    ssqg = [sb.alloc(f"ssqg{i}", [128, 4], F32) for i in range(NSET)]
    rstdg = [sb.alloc(f"rstdg{i}", [128, 4], F32) for i in range(NSET)]
    r_g = [Res(f"ssqg{i}") for i in range(NSET)]
    tA = [sb.alloc(f"tA{i}", [128, 512], F32) for i in range(NSET)]
    tB = [sb.alloc(f"tB{i}", [128, 512], F32) for i in range(NSET)]
    r_tA = [Res(f"tA{i}") for i in range(NSET)]
    r_tB = [Res(f"tB{i}") for i in range(NSET)]
    kn = [sb.alloc(f"kn{i}", [128, 512], BF16) for i in range(NSET)]
    r_kn = [Res(f"kn{i}") for i in range(NSET)]
    CG = [sb.alloc(f"CG{i}", [128, 2, 128], F32) for i in range(2)]
    SG = [sb.alloc(f"SG{i}", [128, 2, 128], F32) for i in range(2)]
    r_tabs = [Res(f"tabs{i}") for i in range(2)]
    KTst = [sb.alloc(f"KTst{i}", [128, 8, 128], BF16) for i in range(2)]
    r_KTst = [Res(f"KTst{i}") for i in range(2)]
    s_KT = [dsem("KTst") for i in range(2)]
    QTst = [sb.alloc(f"QTst{i}", [128, 8, 128], BF16) for i in range(2)]
    r_QTst = [Res(f"QTst{i}") for i in range(2)]
    s_QT = [dsem("QTst") for i in range(2)]
    Vst = [sb.alloc(f"Vst{i}", [128, 1024], BF16) for i in range(2)]
    r_Vst = [Res(f"Vst{i}") for i in range(2)]
    s_V = [dsem("Vst") for i in range(2)]
    R_KTd = Res("KT_d")
    R_Vd = Res("V_d")
    R_QTd = Res("QT_d")
    setc = [0]

    def post1(bank, which, blk):
        st = setc[0] % NSET
        setc[0] += 1
        tp = blk % 2
        u = PS[bank]
        u3 = u[:].rearrange("p (g d) -> p g d", g=4)
        ti = 0 if which == "k" else 1
        tA3 = tA[st][:].rearrange("p (g d) -> p g d", g=4)
        tB3 = tB[st][:].rearrange("p (g d) -> p g d", g=4)
        for g in range(4):
            act(kn[st][:, g * 128:(g + 1) * 128], u[:, g * 128:(g + 1) * 128], AF.Square, accum=ssqg[st][:, g:g + 1], reads=[PSR[bank]],
                **({"writes": [r_kn[st], r_g[st]]} if g == 0 else {"accs": [r_kn[st], r_g[st]]}))
        rstd_from_ssq(rstdg[st][:], ssqg[st][:], 1.0 / HD, r_g[st], r_g[st])
        tt("dve", tA3, u3, bc(CG[tp][:, ti, :].unsqueeze(1), [128, 4, 128]), ALU.mult,
           reads=[PSR[bank], r_tabs[tp]], writes=[r_tA[st]])
        tt("dve", tB3[:, :, 0:64], u3[:, :, 64:128], bc(SG[tp][:, ti, 0:64].unsqueeze(1), [128, 4, 64]), ALU.mult,
           reads=[PSR[bank], r_tabs[tp]], writes=[r_tB[st]])
        tt("dve", tB3[:, :, 64:128], u3[:, :, 0:64], bc(SG[tp][:, ti, 64:128].unsqueeze(1), [128, 4, 64]), ALU.mult,
           reads=[PSR[bank], r_tabs[tp]], accs=[r_tB[st]])
        tt("pool", tA[st][:], tA[st][:], tB[st][:], ALU.add, reads=[r_tA[st], r_tB[st]], writes=[r_tA[st]])
        if OPT_P1_ACT:
            for g in range(4):
                act(kn[st][:, g * 128:(g + 1) * 128], tA[st][:, g * 128:(g + 1) * 128], AF.Copy, scale=rstdg[st][:, g:g + 1],
                    reads=[r_tA[st], r_g[st]], **({"writes": [r_kn[st]]} if g == 0 else {"accs": [r_kn[st]]}))
        else:
            tt("pool", kn[st][:].rearrange("p (g d) -> p g d", g=4), tA3,
               bc(rstdg[st][:].unsqueeze(2), [128, 4, 128]), ALU.mult, reads=[r_tA[st], r_g[st]], writes=[r_kn[st]])
        return st

    def post2(st, half, stage, r_stage):
        pv = psb(7)
        for g in range(4):
            tr(pv[:, g * 128:(g + 1) * 128], kn[st][:, g * 128:(g + 1) * 128], ident_bf[:],
               reads=[r_kn[st], R_const], writes=[PSR[7]] if g == 0 else (), accs=[PSR[7]] if g else ())
        act(stage[:, half * 4:half * 4 + 4, :], pv[:, 0:512].rearrange("p (g t) -> p g t", g=4), AF.Copy,
            reads=[PSR[7]], **({"writes": [r_stage]} if half == 0 else {"accs": [r_stage]}))

    def load_x(blk):
        slot = blk % 2
        dma("sp", xb[slot][:], xs[blk * 128:(blk + 1) * 128, :], xb_s[slot], writes=[xb_r[slot]])

    def norm1(blk):
        slot = blk % 2
        xsrc = xb[slot][:]
        act(xn1[:], xsrc, AF.Square, accum=ssq1[:], reads=[xb_r[slot]], writes=[r_xn1, r_st1])
        rstd_from_ssq(rstd1[:], ssq1[:], 1.0 / D, r_st1, r_st1)
        act(xn1[:], xsrc, AF.Copy, scale=rstd1[:, 0:1], reads=[xb_r[slot], r_st1], writes=[r_xn1])

    def norm2(blk):
        slot = blk % 2
        for q4 in range(4):
            pv = psb(6)
            for i in range(4):
                kc = q4 * 4 + i
                tr(pv[:, i * 128:(i + 1) * 128], xn1[:, kc * 128:(kc + 1) * 128], ident_bf[:],
                   reads=[r_xn1, R_const], writes=[PSR[6]] if i == 0 else (), accs=[PSR[6]] if i else ())
            for i in range(4):
                kc = q4 * 4 + i
                ts("dve", hTb[slot][:, kc, :], pv[:, i * 128:(i + 1) * 128], gmodm[:, kc:kc + 1], shiftm[:, kc:kc + 1], ALU.mult, ALU.add,
                   reads=[PSR[6], R_mod], writes=[hTb_r[slot][kc]])

    def mm_group(blk, g):
        slot = blk % 2
        for kc in range(16):
            mm(PS[g][:], hTb[slot][:, kc, :], wqkv[:, kc, g * 512:(g + 1) * 512], kc == 0, kc == 15,
               reads=[hTb_r[slot][kc], R_w], writes=[PSR[g]] if kc == 0 else (), accs=[PSR[g]] if kc else ())

    def store_q(blk):
        tp = blk % 2
        t0 = (blk // 8) * 512 + (blk % 8) * 128
        for h in range(NH):
            dma("sp", QT_d[h, :, :, t0:t0 + 128], QTst[tp][:, 2 * h:2 * h + 2, :], s_QT[tp], reads=[r_QTst[tp]], accs=[R_QTd])

    pending = []
    load_x(0)
    if NB > 1:
        load_x(1)
    norm1(0)
    norm2(0)
    for blk in range(NB):
        own = (blk % 8) < 4
        tp = blk % 2
        if blk % p0_every == 0 and p0_pos[0] < NADA:
            p0_issue(min(p0_per_blk, NADA - p0_pos[0]))
        tt("pool", CG[tp][:, 0, :].rearrange("p (h f) -> p h f", h=2), bc(cos_t[:, blk, :].unsqueeze(1), [128, 2, 64]),
           kg_b[:].rearrange("p (h f) -> p h f", h=2), ALU.mult, reads=[R_tab, R_const], writes=[r_tabs[tp]])
        tt("pool", SG[tp][:, 0, :].rearrange("p (h f) -> p h f", h=2), bc(sin_t[:, blk, :].unsqueeze(1), [128, 2, 64]),
           kgsw[:].rearrange("p (h f) -> p h f", h=2), ALU.mult, reads=[R_tab, R_gsw], accs=[r_tabs[tp]])
        if own:
            tt("pool", CG[tp][:, 1, :].rearrange("p (h f) -> p h f", h=2), bc(cos_t[:, blk, :].unsqueeze(1), [128, 2, 64]),
               qg_b[:].rearrange("p (h f) -> p h f", h=2), ALU.mult, reads=[R_tab, R_const], accs=[r_tabs[tp]])
            tt("pool", SG[tp][:, 1, :].rearrange("p (h f) -> p h f", h=2), bc(sin_t[:, blk, :].unsqueeze(1), [128, 2, 64]),
               qgsw[:].rearrange("p (h f) -> p h f", h=2), ALU.mult, reads=[R_tab, R_gsw], accs=[r_tabs[tp]])
        mm_group(blk, 2)
        if pending:
            pending.pop(0)()
        st_k0 = post1(2, "k", blk)
        mm_group(blk, 3)
        if pending:
            pending.pop(0)()
        st_k1 = post1(3, "k", blk)
        if blk + 1 < NB:
            norm1(blk + 1)
        mm_group(blk, 4)
        act(Vst[tp][:, 0:512], PS[4][:], AF.Copy, reads=[PSR[4]], writes=[r_Vst[tp]])
        post2(st_k0, 0, KTst[tp], r_KTst[tp])
        mm_group(blk, 5)
        act(Vst[tp][:, 512:1024], PS[5][:], AF.Copy, reads=[PSR[5]], accs=[r_Vst[tp]])
        post2(st_k1, 1, KTst[tp], r_KTst[tp])
        for h in range(NH):
            dma("sp", KT_d[h, :, :, blk * 128:(blk + 1) * 128], KTst[tp][:, 2 * h:2 * h + 2, :], s_KT[tp], reads=[r_KTst[tp]], accs=[R_KTd])
        dma("sp", V_d[blk * 128:(blk + 1) * 128, :], Vst[tp][:], s_V[tp], reads=[r_Vst[tp]], accs=[R_Vd])
        if blk + 1 < NB:
            norm2(blk + 1)
        if own:
            mm_group(blk, 0)
            st_q0 = post1(0, "q", blk)
            mm_group(blk, 1)
            st_q1 = post1(1, "q", blk)
            pending.append(lambda st=st_q0, tp=tp: post2(st, 0, QTst[tp], r_QTst[tp]))
            pending.append(lambda st=st_q1, tp=tp, blk=blk: (post2(st, 1, QTst[tp], r_QTst[tp]), store_q(blk)))
        if blk + 2 < NB:
            load_x(blk + 2)
    while pending:
        pending.pop(0)()
    p0_issue(NADA - p0_pos[0])

    P.barrier()
    sb.cur = persist_mark

    KTs = sb.alloc("KTs", [128, 2, S], BF16)
    Vaug = sb.alloc("Vaug", [128, NB, VD + 1], BF16)
    QTs = sb.alloc("QTs", [128, 2, NOWN], BF16)
    NCH = NPAIR
    r_Kc = [Res(f"Kc{i}") for i in range(NCH)]
    r_Vc = [Res(f"Vc{i}") for i in range(NCH)]
    s_Kc = [dsem("Kc") for i in range(NCH)]
    s_Vc = [dsem("Vc") for i in range(NCH)]
    r_Q = Res("QTs")
    s_Q = dsem("QTs")
    R_ones2 = Res("vones")
    mset("pool", Vaug[:, :, VD:VD + 1], 1.0, writes=[R_ones2])
    NPT = 6
    STB = [0, 1, 6]
    PT = [sb.alloc(f"PT{i}", [128, 2, 256], BF16) for i in range(NPT)]
    r_PT = [Res(f"PT{i}") for i in range(NPT)]
    rl = [sb.alloc(f"rl{i}", [128, 2], F32) for i in range(2)]
    r_rl = [Res(f"rl{i}") for i in range(2)]
    T1 = [sb.alloc(f"T1{i}", [128, VD], F32) for i in range(2)]
    r_T1 = [Res(f"T1{i}") for i in range(2)]
    Ot = [sb.alloc(f"Ot{i}", [128, VD], F32) for i in range(2)]
    r_Ot = [Res(f"Ot{i}") for i in range(2)]
    osq = [sb.alloc(f"osq{i}", [128, VD], BF16) for i in range(2)]
    r_osq = [Res(f"osq{i}") for i in range(2)]
    ossq = [sb.alloc(f"ossq{i}", [128, 1], F32) for i in range(2)]
    orstd = [sb.alloc(f"orstd{i}", [128, 1], F32) for i in range(2)]
    r_os = [Res(f"os{i}") for i in range(2)]
    On = [[sb.alloc(f"On{a_}{b_}", [128, VD], BF16) for b_ in range(2)] for a_ in range(2)]
    r_On = [[Res(f"On{a_}{b_}") for b_ in range(2)] for a_ in range(2)]
    Oa = [sb.alloc(f"Oa{i}", [128, 4, VD + 1], F32) for i in range(2)]
    r_Oa = [Res(f"Oa{i}") for i in range(2)]
    gcount = [0]
    pend2 = []
    ATst = [sb.alloc(f"ATst{i}", [128, 2, 512], BF16) for i in range(2)]
    r_ATst = [Res(f"ATst{i}") for i in range(2)]
    s_AT = [dsem("ATst") for i in range(2)]
    R_ATd = Res("AT_d")
    SCALE = 1.0 / math.sqrt(HD)
    ptc = [0]
    arp = [sb.alloc(f"arp{i}", [128, 16, 512], BF16) for i in range(2)]
    r_arp = [Res(f"arp{i}") for i in range(2)]
    s_arp = [dsem("arp") for i in range(2)]
    cT2 = sb.alloc("cT2", [128, 16], F32)
    adabT2 = sb.alloc("adabT2", [128, 96], F32)
    nfg2 = sb.alloc("nfg2", [128, 16], F32)
    adaT2 = sb.alloc("adaT2", [128, 96], F32)
    cact2 = sb.alloc("cact2", [128, 16], BF16)
    cact_rep2 = sb.alloc("cact_rep2", [128, 16, 128], BF16)
    R_c2 = Res("c2")
    R_adaT2 = Res("adaT2")
    s_c2 = dsem("c2")
    dma("pool", cT2[:], c_d, s_c2, writes=[R_c2])
    dma("pool", adabT2[:], ada_bT, s_c2, accs=[R_c2])
    dma("pool", nfg2[:], nfg_d, s_c2, accs=[R_c2])
    R_cact2 = Res("cact2")
    act(cact2[:], cT2[:], AF.Silu, reads=[R_c2], writes=[R_cact2])
    cp("dve", cact_rep2[:], bc(cact2[:].unsqueeze(2), [128, 16, 128]), reads=[R_cact2], accs=[R_cact2])
    ada_k = [0]
    ada_pend = [False]

    def ada_load(k):
        if k < NADA:
            dma("pool", arp[k % 2][:].rearrange("p kc n -> p (kc n)"), WS_d[k], s_arp[k % 2], reads=[R_WSa], writes=[r_arp[k % 2]])

    def ada_panel():
        k = ada_k[0]
        if k >= NADA:
            return
        ada_k[0] += 1
        seg = ADA_SEGS[k // 4]
        g = k % 4
        slot = k % 2
        if seg in (2, 5):
            for kc in range(16):
                mm(PS[7][:], cact_rep2[:, kc, :], arp[slot][:, kc, :], kc == 0, kc == 15,
                   reads=[R_cact2, r_arp[slot]], writes=[PSR[7]] if kc == 0 else (), accs=[PSR[7]] if kc else ())
            dst = gate_m_b if seg == 2 else gate_f_b
            tt("dve", dst[:, g * 512:(g + 1) * 512], PS[7][:], dst[:, g * 512:(g + 1) * 512], ALU.add,
               reads=[PSR[7], R_const], accs=[R_gates])
        else:
            for ch in range(4):
                for kc in range(16):
                    first = (kc == 0 and ch == 0)
                    mm(PS[7][:, ch:ch + 1], arp[slot][:, kc, ch * 128:(ch + 1) * 128], cact2[:, kc:kc + 1], kc == 0, kc == 15,
                       reads=[R_cact2, r_arp[slot]], writes=[PSR[7]] if first else (), accs=() if first else [PSR[7]])
            j0 = (seg * D + g * 512) // 128
            tt("dve", adaT2[:, j0:j0 + 4], PS[7][:, 0:4], adabT2[:, j0:j0 + 4], ALU.add, reads=[PSR[7], R_c2], accs=[R_adaT2])
        ada_load(k + 2)
        if ada_k[0] == NADA:
            ts("dve", gmodf[:], adaT2[:, 64:80], 1.0, None, ALU.add, reads=[R_adaT2], accs=[R_mod])
            tt("dve", gmodf[:], gmodf[:], nfg2[:], ALU.mult, reads=[R_mod, R_c2], writes=[R_mod])
            cp("dve", shiftf[:], adaT2[:, 48:64], reads=[R_adaT2], accs=[R_mod])

    ada_load(0)
    ada_load(1)
    p0_per_grp = (len(p0_jobs) - NADA + NH * NPAIR * 2 - 1) // (NH * NPAIR * 2)
    for h in range(NH):
        for ch in range(NCH):
            dma("sp", KTs[:, :, ch * 1024:(ch + 1) * 1024], KT_d[h, :, :, ch * 1024:(ch + 1) * 1024], s_Kc[ch], reads=[R_KTd], writes=[r_Kc[ch]])
            dma("sp", Vaug[:, ch * 8:(ch + 1) * 8, 0:VD],
                V_d[ch * 1024:(ch + 1) * 1024, h * VD:(h + 1) * VD].rearrange("(b p) d -> p b d", p=128), s_Vc[ch],
                reads=[R_Vd, R_ones2], writes=[r_Vc[ch]])
        dma("sp", QTs[:], QT_d[h], s_Q, reads=[R_QTd], writes=[r_Q])
        for j in range(NPAIR):
            ast = j % 2
            for g in range(2):
                kbs = [(kb, 0, False, False) for kb in range(8 * j)]
                for l in range(4):
                    if l <= 2 * g + 1:
                        q0 = max(l - 2 * g, 0)
                        kbs.append((8 * j + l, q0, True if l >= 2 * g else False, False))
                for l in range(4, 8):
                    kbs.append((8 * j + l, 0, False, True))
                qcol = j * 512 + g * 256
                nk = len(kbs)
                seen = [False, False]
                last_for = [max(i for i, kbi in enumerate(kbs) if kbi[1] <= qb) for qb in range(2)]
                accb = [[2, 3], [4, 5]]

                def qk(i):
                    kb, q0, diag, ob = kbs[i]
                    bank = STB[i % 3]
                    nq = 256 - q0 * 128
                    for m in range(2):
                        mm(PS[bank][:, m * 256 + q0 * 128:m * 256 + 256], KTs[:, m, kb * 128:(kb + 1) * 128],
                           QTs[:, m, qcol + q0 * 128:qcol + 256], True, True,
                           reads=[r_Kc[kb // 8], r_Q], writes=[PSR[bank]] if m == 0 else (), accs=[PSR[bank]] if m else ())

                def ex_pv(i):
                    kb, q0, diag, ob = kbs[i]
                    bank = STB[i % 3]
                    pt = ptc[0] % NPT
                    ptc[0] += 1
                    src = PS[bank][:].rearrange("p (m q) -> p m q", m=2)[:, :, q0 * 128:256]
                    act(PT[pt][:, :, q0 * 128:256], src, AF.Exp, bias=obias[:, 0:1] if ob else 0.0, scale=SCALE,
                        reads=[PSR[bank], R_const], writes=[r_PT[pt]])
                    if diag:
                        mset("pool", PT[pt][64:128, :, q0 * 128:q0 * 128 + 64], 0.0, reads=[r_PT[pt]], writes=[r_PT[pt]])
                    for qb in range(q0, 2):
                        for m in range(2):
                            b_ = accb[qb][m]
                            first = not seen[qb]
                            mm(PS[b_][:, 0:VD + 1], PT[pt][:, m, qb * 128:(qb + 1) * 128], Vaug[:, kb, :], first, i == last_for[qb],
                               reads=[r_PT[pt], r_Vc[kb // 8]], writes=[PSR[b_]] if first else (), accs=() if first else [PSR[b_]])
                        seen[qb] = True

                qk(0)
                if nk > 1:
                    qk(1)
                if ada_pend[0]:
                    ada_pend[0] = False
                    ada_panel()
                for i in range(nk):
                    if i + 2 < nk:
                        qk(i + 2)
                    ex_pv(i)
                    if pend2 and i < len(pend2[0]):
                        pend2[0][i]()
                        if i == len(pend2[0]) - 1:
                            pend2.pop(0)
                gp = gcount[0] % 2
                gcount[0] += 1
                for qb in range(2):
                    for m in range(2):
                        b_ = accb[qb][m]
                        cp("dve", Oa[gp][:, qb * 2 + m, :], PS[b_][:, 0:VD + 1], reads=[PSR[b_]],
                           **({"writes": [r_Oa[gp]]} if (qb == 0 and m == 0) else {"accs": [r_Oa[gp]]}))
                ada_pend[0] = True
                p0_issue(p0_per_grp)
                def st_a(gp=gp):
                    for qb in range(2):
                        O1 = Oa[gp][:, qb * 2, :]
                        O2 = Oa[gp][:, qb * 2 + 1, :]
                        cp("dve", rl[qb][:, 0:1], O1[:, VD:VD + 1], reads=[r_Oa[gp]], writes=[r_rl[qb]])
                        cp("dve", rl[qb][:, 1:2], O2[:, VD:VD + 1], reads=[r_Oa[gp]], accs=[r_rl[qb]])
                        recip(rl[qb][:], rl[qb][:], reads=[r_rl[qb]], writes=[r_rl[qb]])
                        tt("dve", rl[qb][:, 1:2], rl[qb][:, 1:2], nlam[:], ALU.mult, reads=[r_rl[qb], R_lam], writes=[r_rl[qb]])

                def st_b(gp=gp):
                    for qb in range(2):
                        act(T1[qb][:], Oa[gp][:, qb * 2, 0:VD], AF.Copy, scale=rl[qb][:, 0:1], reads=[r_Oa[gp], r_rl[qb]], writes=[r_T1[qb]])

                def st_c(gp=gp):
                    for qb in range(2):
                        stt("dve", Ot[qb][:], Oa[gp][:, qb * 2 + 1, 0:VD], rl[qb][:, 1:2], T1[qb][:], ALU.mult, ALU.add,
                            reads=[r_Oa[gp], r_rl[qb], r_T1[qb]], writes=[r_Ot[qb]])

                def st_d(gp=gp):
                    for qb in range(2):
                        act(osq[qb][:], Ot[qb][:], AF.Square, accum=ossq[qb][:], reads=[r_Ot[qb]], writes=[r_osq[qb], r_os[qb]])
                    for qb in range(2):
                        act(orstd[qb][:], ossq[qb][:], AF.Sqrt, bias=EPS, scale=1.0 / VD, reads=[r_os[qb]], writes=[r_os[qb]])

                def st_e(gp=gp):
                    for qb in range(2):
                        recip(orstd[qb][:], orstd[qb][:], reads=[r_os[qb]], writes=[r_os[qb]])
                        ts("dve", On[gp][qb][:], Ot[qb][:], orstd[qb][:, 0:1], None, ALU.mult, reads=[r_Ot[qb], r_os[qb]], writes=[r_On[gp][qb]])

                def fin2(h=h, j=j, g=g, gp=gp, ast=ast):
                    for qb in range(2):
                        tcol = g * 256 + qb * 128
                        pv = psb(7)
                        for c2 in range(2):
                            tr(pv[:, c2 * 128:(c2 + 1) * 128], On[gp][qb][:, c2 * 128:(c2 + 1) * 128], ident_bf[:],
                               reads=[r_On[gp][qb], R_const], writes=[PSR[7]] if c2 == 0 else (), accs=[PSR[7]] if c2 else ())
                        first_at = (g == 0 and qb == 0)
                        for c2 in range(2):
                            ts("dve", ATst[ast][:, c2, tcol:tcol + 128], pv[:, c2 * 128:(c2 + 1) * 128], sublng[:, c2:c2 + 1], None, ALU.mult,
                               reads=[PSR[7], R_const], **({"writes": [r_ATst[ast]]} if (first_at and c2 == 0) else {"accs": [r_ATst[ast]]}))
                    if g == 1:
                        for c2 in range(2):
                            dma("sp", AT_d[2 * h + c2, :, j * 512:(j + 1) * 512], ATst[ast][:, c2, :], s_AT[ast], reads=[r_ATst[ast]], accs=[R_ATd])

                st_a()
                pend2.append([st_b, st_c, st_d, st_e, fin2])
    while pend2:
        for f_ in pend2.pop(0):
            f_()
    while ada_k[0] < NADA:
        ada_panel()
    p0_issue(len(p0_jobs))

    P.barrier()
    sb.cur = persist_mark

    NT = 512
    xt = sb.alloc("xt", [128, 4, D], F32)
    r_xt = [[Res(f"xt{b}_{c}") for c in range(4)] for b in range(4)]
    s_xt = [dsem("xt") for _ in range(4)]
    s_out = [dsem("out") for _ in range(4)]
    xst = sb.alloc("xst", [128, D], F32)
    r_xst = Res("xst")
    s_xst = dsem("xst")
    hT = sb.alloc("hT", [128, 16, HALO + NT], BF16)
    r_hT = [Res(f"hT{k}") for k in range(16)]
    r_hTh = [Res(f"hTh{k}") for k in range(16)]
    xn3 = [sb.alloc(f"xn3_{i}", [128, D], BF16) for i in range(2)]
    r_xn3 = [Res(f"xn3_{i}") for i in range(2)]
    ssq3 = [sb.alloc(f"ssq3_{i}", [128, 1], F32) for i in range(2)]
    rstd3 = [sb.alloc(f"rstd3_{i}", [128, 1], F32) for i in range(2)]
    r_st3 = [Res(f"st3_{i}") for i in range(2)]
    ncnt = [0]
    NSLOT = 4
    wsl = [sb.alloc(f"wsl{i}", [128, 16, 512], BF16) for i in range(NSLOT)]
    r_wsl = [Res(f"wsl{i}") for i in range(NSLOT)]
    s_wsl = [dsem("wsl") for i in range(NSLOT)]
    wc = [0]
    pidx = [0]
    mScr = sb.cur
    ycv = sb.alloc("ycv", [128, 8, NT], F32)
    r_ycv = [Res(f"ycv{c}") for c in range(8)]
    mEnd1 = sb.cur
    sb.cur = mScr
    merged = sb.alloc("merged", [128, 16, NT], BF16)
    r_mg = [Res(f"mg{f}") for f in range(16)]
    sb.cur = mEnd1
    mGlu = sb.cur
    glu = sb.alloc("glu", [128, 8, HALO + NT], BF16)
    r_glu = [Res(f"glu{c}") for c in range(8)]
    mEnd2 = sb.cur
    sb.cur = mGlu
    zT = sb.alloc("zT", [128, 8, NT], BF16)
    r_zT = [Res(f"zT{c}") for c in range(8)]
    sb.cur = mEnd2
    atT = sb.alloc("atT", [128, 8, NT], BF16)
    r_atT = Res("atT")
    s_atT = dsem("atT")
    mScrEnd = sb.cur
    sb.cur = mScr
    actT = sb.alloc("actT", [128, 32, NT], BF16)
    r_actT = [Res(f"actT{f}") for f in range(32)]
    sb.cur = max(sb.cur, mScrEnd)
    R_alias = Res("alias")
    sg = [sb.alloc(f"sg{i}", [128, NT], F32) for i in range(2)]
    r_sg = [Res(f"sg{i}") for i in range(2)]
    sgh = [sb.alloc(f"sgh{i}", [128, HALO], F32) for i in range(2)]
    r_sgh = [Res(f"sgh{i}") for i in range(2)]
    PSR6h = [Res("ps6a"), Res("ps6b")]
    ysq = sb.alloc("ysq", [128, NT], BF16)
    r_ysq = Res("ysq")
    rstdb = sb.alloc("rstdb", [128, NT], F32)
    r_rstdb = Res("rstdb")
    tmp = [sb.alloc(f"tmp{i}", [128, NT], F32) for i in range(2)]
    r_tmp = [Res(f"tmp{i}") for i in range(2)]
    diag = [sb.alloc(f"diag{i}", [128, CK, 128], BF16) for i in range(2)]
    r_diag = [Res(f"diag{i}") for i in range(2)]
    print("P3 sbuf end", sb.cur, "of", sb.hi)

    def wload(*_a, **_k):
        slot = wc[0] % NSLOT
        wc[0] += 1
        i = NADA + pidx[0] % (NPAN - NADA)
        pidx[0] += 1
        dma("sp", wsl[slot][:].rearrange("p kc n -> p (kc n)"), WS_d[i], s_wsl[slot], reads=[R_WS], writes=[r_wsl[slot]])
        return slot

    wload2 = wload

    ccount = [0]

    def nrm(xsrc, ntok, r_x, gmod, shift, dst, r_list, first):
        i = ncnt[0] % 2
        ncnt[0] += 1
        norm_transpose(xsrc, ntok, r_x, xn3[i], r_xn3[i], ssq3[i], rstd3[i], r_st3[i], gmod, shift, dst, r_list, (6, 7), first)

    def prenorm_step(jn, step):
        if step == 0:
            dma("act", xst[0:HALO, :], xh[jn * HALO:(jn + 1) * HALO, :], s_xst, writes=[r_xst])
            nrm(xst[0:HALO, :], HALO, r_xst, gmodm, shiftm, lambda kc: hT[:, kc, 0:HALO], r_hTh, True)
        else:
            b = step - 1
            r0 = jn * 1024 + b * 128
            dma("act", xst[:], xs[r0:r0 + 128, :], s_xst, writes=[r_xst])
            nrm(xst[:], 128, r_xst, gmodm, shiftm, lambda kc, b=b: hT[:, kc, HALO + b * 128:HALO + (b + 1) * 128], r_hT, b == 0)

    for st_ in range(5):
        prenorm_step(0, st_)

    for j in range(NPAIR):
        row0 = j * 1024
        for b in range(4):
            dma("pool", xt[:, b, :], xs[row0 + b * 128:row0 + (b + 1) * 128, :], s_xt[b], writes=r_xt[b])
        dma("pool", atT[:], AT_d[:, :, j * 512:(j + 1) * 512].rearrange("c p t -> p c t"), s_atT, reads=[R_ATd, R_alias], writes=[r_atT])
        pslots = {}

        def proj(c):
            p = c % 2
            cl = c % 4
            if cl == 0:
                pslots["a"] = wload()
                pslots["g"] = wload()
            sa, sgs = pslots["a"], pslots["g"]
            bA, bG = 2 * p, 2 * p + 1
            H = PS[6][:, p * 64:(p + 1) * 64]
            for kc in range(16):
                mm(PS[bA][:], wsl[sa][:, kc, cl * 128:(cl + 1) * 128], hT[:, kc, HALO:HALO + NT], kc == 0, kc == 15,
                   reads=[r_wsl[sa], r_hT[kc]], writes=[PSR[bA]] if kc == 0 else (), accs=[PSR[bA]] if kc else ())
            for kc in range(16):
                mm(PS[bG][:], wsl[sgs][:, kc, cl * 128:(cl + 1) * 128], hT[:, kc, HALO:HALO + NT], kc == 0, kc == 15,
                   reads=[r_wsl[sgs], r_hT[kc]], writes=[PSR[bG]] if kc == 0 else (), accs=[PSR[bG]] if kc else ())
            for kc in range(16):
                mm(H[:, 0:HALO], wsl[sa][:, kc, cl * 128:(cl + 1) * 128], hT[:, kc, 0:HALO], kc == 0, kc == 15,
                   reads=[r_wsl[sa], r_hTh[kc]], writes=[PSR6h[p]] if kc == 0 else (), accs=[PSR6h[p]] if kc else ())
            for kc in range(16):
                mm(H[:, HALO:2 * HALO], wsl[sgs][:, kc, cl * 128:(cl + 1) * 128], hT[:, kc, 0:HALO], kc == 0, kc == 15,
                   reads=[r_wsl[sgs], r_hTh[kc]], accs=[PSR6h[p]])
            act(sg[p][:], PS[bG][:], AF.Sigmoid, reads=[PSR[bG]], writes=[r_sg[p]])
            tt("dve", glu[:, c, HALO:HALO + NT], PS[bA][:], sg[p][:], ALU.mult, reads=[PSR[bA], r_sg[p], R_alias], writes=[r_glu[c]])
            act(sgh[p][:], H[:, HALO:2 * HALO], AF.Sigmoid, reads=[PSR6h[p]], writes=[r_sgh[p]])
            stt("dve", glu[:, c, 0:HALO], H[:, 0:HALO], hv[:, j:j + 1], sgh[p][:], ALU.mult, ALU.mult,
                reads=[PSR6h[p], r_sgh[p], R_const], accs=[r_glu[c]])

        dsel = {}

        def diagb(c):
            di = ccount[0] % 2
            ccount[0] += 1
            dsel[c] = di
            tt("dve", diag[di][:], bc(ident_bf[:].unsqueeze(1), [128, CK, 128]), bc(cwT[:, c, :].unsqueeze(2), [128, CK, 128]), ALU.mult,
               reads=[R_const, R_cw], writes=[r_diag[di]])

        def convmm(c):
            di = dsel[c]
            bank = 4 + (c % 2)
            for t in range(CK):
                mm(PS[bank][:], diag[di][:, t, :], glu[:, c, 2 + t:2 + t + NT], t == 0, t == CK - 1,
                   reads=[r_diag[di], r_glu[c]], writes=[PSR[bank]] if t == 0 else (), accs=[PSR[bank]] if t else ())
            act(ycv[:, c, :], PS[bank][:], AF.Identity, bias=conv_bT[:, c:c + 1], scale=1.0, reads=[PSR[bank], R_const, R_alias], writes=[r_ycv[c]])
            tt("dve", ysq[:], ycv[:, c, :], ycv[:, c, :], ALU.mult, reads=[r_ycv[c]], writes=[r_ysq])

        def ssqmm(c):
            mm(PS[7][:], ones_bf[:], ysq[:], c == 0, c == 7, reads=[r_ysq, R_ones],
               writes=[PSR[7]] if c == 0 else (), accs=[PSR[7]] if c else ())

        diagb(0)
        proj(0)
        for c in range(1, 8):
            diagb(c)
            proj(c)
            if c >= 2:
                ssqmm(c - 2)
            convmm(c - 1)
        ssqmm(6)
        convmm(7)
        ssqmm(7)
        act(rstdb[:], PS[7][:], AF.Sqrt, bias=EPS, scale=1.0 / CW, reads=[PSR[7]], writes=[r_rstdb])
        recip(rstdb[:], rstdb[:], reads=[r_rstdb], writes=[r_rstdb])
        for c in range(8):
            ti = c % 2
            tt("dve", tmp[ti][:], ycv[:, c, :], rstdb[:], ALU.mult, reads=[r_ycv[c], r_rstdb], writes=[r_tmp[ti]])
            act(zT[:, c, :], tmp[ti][:], AF.Silu, scale=conv_gT[:, c:c + 1], reads=[r_tmp[ti], R_const, R_alias], writes=[r_zT[c]])
        for f4 in range(4):
            s0 = wload(w_in[:, 5120 + f4 * 512:5120 + (f4 + 1) * 512])
            s1 = wload(w_in[:, 7168 + f4 * 512:7168 + (f4 + 1) * 512])
            s2 = wload2(w_conv_out[:, f4 * 512:(f4 + 1) * 512], w_attn_out[:, f4 * 512:(f4 + 1) * 512])
            for fl in range(4):
                f = f4 * 4 + fl
                for kc in range(16):
                    mm(PS[0][:], wsl[s0][:, kc, fl * 128:(fl + 1) * 128], hT[:, kc, HALO:HALO + NT], kc == 0, kc == 15,
                       reads=[r_wsl[s0], r_hT[kc]], writes=[PSR[0]] if kc == 0 else (), accs=[PSR[0]] if kc else ())
                for kc in range(16):
                    mm(PS[1][:], wsl[s1][:, kc, fl * 128:(fl + 1) * 128], hT[:, kc, HALO:HALO + NT], kc == 0, kc == 15,
                       reads=[r_wsl[s1], r_hT[kc]], writes=[PSR[1]] if kc == 0 else (), accs=[PSR[1]] if kc else ())
                for kc in range(8):
                    mm(PS[2][:], wsl[s2][:, kc, fl * 128:(fl + 1) * 128], zT[:, kc, :], kc == 0, kc == 7,
                       reads=[r_wsl[s2], r_zT[kc]], writes=[PSR[2]] if kc == 0 else (), accs=[PSR[2]] if kc else ())
                for kc in range(8):
                    mm(PS[3][:], wsl[s2][:, 8 + kc, fl * 128:(fl + 1) * 128], atT[:, kc, :], kc == 0, kc == 7,
                       reads=[r_wsl[s2], r_atT], writes=[PSR[3]] if kc == 0 else (), accs=[PSR[3]] if kc else ())
                act(sg[0][:], PS[0][:], AF.Sigmoid, bias=gate_bT[:, f:f + 1], scale=1.0, reads=[PSR[0], R_const], writes=[r_sg[0]])
                act(sg[1][:], PS[1][:], AF.Sigmoid, bias=gate_bT[:, 16 + f:17 + f], scale=1.0, reads=[PSR[1], R_const], writes=[r_sg[1]])
                tt("dve", tmp[0][:], PS[2][:], sg[0][:], ALU.mult, reads=[PSR[2], r_sg[0]], writes=[r_tmp[0]])
                tt("dve", tmp[1][:], PS[3][:], sg[1][:], ALU.mult, reads=[PSR[3], r_sg[1]], writes=[r_tmp[1]])
                tt("pool", merged[:, f, :], tmp[0][:], tmp[1][:], ALU.add, reads=[r_tmp[0], r_tmp[1], R_alias], writes=[r_mg[f]])
        for cg in range(4):
            so = wload(w_out[:, cg * 512:(cg + 1) * 512])
            for tb in range(4):
                bank = 4 + (tb % 2)
                for kc in range(16):
                    mm(PS[bank][:], merged[:, kc, tb * 128:(tb + 1) * 128], wsl[so][:, kc, :], kc == 0, kc == 15,
                       reads=[r_wsl[so], r_mg[kc]], writes=[PSR[bank]] if kc == 0 else (), accs=[PSR[bank]] if kc else ())
                ti = tb % 2
                tt("dve", tmp[ti][:], PS[bank][:], gate_m_b[:, cg * 512:(cg + 1) * 512], ALU.mult, reads=[PSR[bank], R_gates], writes=[r_tmp[ti]])
                tt("pool", xt[:, tb, cg * 512:(cg + 1) * 512], xt[:, tb, cg * 512:(cg + 1) * 512], tmp[ti][:], ALU.add,
                   reads=[r_tmp[ti], r_xt[tb][cg]], writes=[r_xt[tb][cg]])
        fence_reads = r_glu + r_ycv + r_zT + [r_atT] + r_mg
        P.op("pool", lambda e: e.memset(dummy[:, 2:3], 0.0), writes=fence_reads + [R_alias])
        for b in range(4):
            jr = Res()
            P.op("pool", lambda e: e.memset(dummy[:, 3:4], 0.0), reads=r_xt[b], writes=[jr])
            nrm(xt[:, b, :], 128, jr, gmodf, shiftf, lambda kc, b=b: hT[:, kc, HALO + b * 128:HALO + (b + 1) * 128], r_hT, b == 0)
        for half in range(2):
            for fg in range(8):
                sm = wload(w_mlp_in[:, half * 4096 + fg * 512:half * 4096 + (fg + 1) * 512])
                for fl in range(4):
                    fc = fg * 4 + fl
                    bank = fc % 2
                    for kc in range(16):
                        mm(PS[bank][:], wsl[sm][:, kc, fl * 128:(fl + 1) * 128], hT[:, kc, HALO:HALO + NT], kc == 0, kc == 15,
                           reads=[r_wsl[sm], r_hT[kc]], writes=[PSR[bank]] if kc == 0 else (), accs=[PSR[bank]] if kc else ())
                    si = fc % 2
                    act(sg[si][:], PS[bank][:], AF.Relu, reads=[PSR[bank]], writes=[r_sg[si]])
                    tt("dve", actT[:, fc, :], PS[bank][:], sg[si][:], ALU.mult, reads=[PSR[bank], r_sg[si], R_alias], writes=[r_actT[fc]])
            for cg in range(4):
                slots = [wload(w_mlp_out[half * 4096 + kg * 2048:half * 4096 + (kg + 1) * 2048, cg * 512:(cg + 1) * 512]) for kg in range(2)]
                for tb in range(4):
                    bank = 2 + tb
                    for kg in range(2):
                        for kc in range(16):
                            k = kg * 16 + kc
                            mm(PS[bank][:], actT[:, k, tb * 128:(tb + 1) * 128], wsl[slots[kg]][:, kc, :], k == 0, k == 31,
                               reads=[r_wsl[slots[kg]], r_actT[k]], writes=[PSR[bank]] if k == 0 else (), accs=[PSR[bank]] if k else ())
                    ti = tb % 2
                    tt("dve", tmp[ti][:], PS[bank][:], gate_f_b[:, cg * 512:(cg + 1) * 512], ALU.mult, reads=[PSR[bank], R_gates], writes=[r_tmp[ti]])
                    tt("pool", xt[:, tb, cg * 512:(cg + 1) * 512], xt[:, tb, cg * 512:(cg + 1) * 512], tmp[ti][:], ALU.add,
                       reads=[r_tmp[ti], r_xt[tb][cg]], writes=[r_xt[tb][cg]])
                if half == 1 and j + 1 < NPAIR:
                    if cg == 0:
                        prenorm_step(j + 1, 0)
                    prenorm_step(j + 1, cg + 1)
        P.op("pool", lambda e: e.memset(dummy[:, 4:5], 0.0), writes=r_actT + [R_alias])
        for b in range(4):
            dma("pool", out_d[j * 512 + b * 128:j * 512 + (b + 1) * 128, :], xt[:, b, :], s_out[b], reads=r_xt[b])
    P.op("sp", lambda e: e.nop(), extra=list(P.dmas))

    P.finalize()
    P.emit()
    info = dict(counts=P.counts, nwaits=P.nwaits, nops=len(P.all))
    return nc, info


_CACHE = {}


def _fm(v, n):
    return np.ascontiguousarray(np.asarray(v, dtype=np.float32).reshape(n, 128).T)


def prep_inputs(inputs, NPAIR, cores):
    S = NPAIR * 1024
    D_ = D
    shared = dict(
        ident=np.eye(128, dtype=np.float32),
        invf=np.ascontiguousarray(np.broadcast_to(
            (10000.0 ** (-np.arange(0, 128, 2, dtype=np.float32) / np.float32(128))).astype(np.float32)[None, :], (128, 64))),
        ada_w=np.ascontiguousarray(inputs["ada_w"][0]),
        ada_bT=_fm(inputs["ada_b"][0], 96),
        ada_b_row=np.ascontiguousarray(inputs["ada_b"][0][None, :]),
        nmgT=_fm(inputs["norm_mix_g"][0], 16),
        nfgT=_fm(inputs["norm_mlp_g"][0], 16),
        w_in=np.ascontiguousarray(inputs["w_in"][0]),
        conv_w=np.ascontiguousarray(inputs["conv_w"][0]),
        conv_bT=_fm(inputs["conv_b"][0], 8),
        conv_gT=_fm(inputs["conv_norm_g"][0], 8),
        w_conv_out=np.ascontiguousarray(inputs["w_conv_out"][0]),
        qg_b=np.ascontiguousarray(np.broadcast_to(inputs["q_norm_g"][0][None, :], (128, 128))),
        kg_b=np.ascontiguousarray(np.broadcast_to(inputs["k_norm_g"][0][None, :], (128, 128))),
        lam_b=np.ascontiguousarray(np.broadcast_to(np.stack([inputs["lambda_q1"][0], inputs["lambda_k1"][0],
                                                             inputs["lambda_q2"][0], inputs["lambda_k2"][0]])[None], (128, 4, 128))),
        sublnT=_fm(inputs["subln_g"][0], 2),
        w_attn_out=np.ascontiguousarray(inputs["w_attn_out"][0]),
        gate_bT=_fm(inputs["gate_b"][0], 32),
        w_out=np.ascontiguousarray(inputs["w_out"][0]),
        w_mlp_in=np.ascontiguousarray(inputs["w_mlp_in"][0]),
        w_mlp_out=np.ascontiguousarray(inputs["w_mlp_out"][0]),
    )
    shared["invf"] = np.ascontiguousarray(np.broadcast_to(
        np.power(np.float32(10000.0), -(np.arange(0, 128, 2, dtype=np.float32) / np.float32(128))).astype(np.float32)[None, :], (128, 64)))
    maps = []
    for core in cores:
        b, par = core // 2, core % 2
        x = np.asarray(inputs["x"][b][:S], dtype=np.float32)
        pos = np.asarray(inputs["pos"][b][:S], dtype=np.int32)
        order = []
        for j in range(NPAIR):
            order += [2 * j + par, 2 * j + 1 - par]
        xs = np.ascontiguousarray(x.reshape(2 * NPAIR, 512, D_)[order].reshape(S, D_))
        posr = pos.reshape(2 * NPAIR, 512)[order].reshape(S)
        posc = np.ascontiguousarray(posr.reshape(S // 128, 128).T)
        xh = np.zeros((NPAIR * HALO, D_), np.float32)
        hv = np.zeros((128, NPAIR), np.float32)
        for j in range(NPAIR):
            st = (2 * j + par) * 512
            if st > 0:
                xh[j * HALO:(j + 1) * HALO] = x[st - HALO:st]
                hv[:, j] = 1.0
        m = dict(shared)
        m.update(xs=xs, xh=xh, hv=hv, posc=posc, cT=_fm(inputs["c"][b], 16),
                 obias=np.full((128, 1), 0.0 if par else -30000.0, np.float32))
        maps.append(m)
    return maps


def run(inputs, NPAIR, cores, debug=False):
    key = (NPAIR, debug)
    if key not in _CACHE:
        _CACHE[key] = build(NPAIR, debug)
    nc, info = _CACHE[key]
    maps = prep_inputs(inputs, NPAIR, cores)
    res = run_bass_kernel_spmd(nc, maps, core_ids=list(range(len(cores))))
    return res, info


def kernel(**inputs):
    inputs = {k: np.asarray(v) for k, v in inputs.items()}
    B, S, _ = inputs["x"].shape
    NPAIR = S // 1024
    cores = list(range(2 * B))
    res, _ = run(inputs, NPAIR, cores)
    out = np.empty((B, S, D), np.float32)
    for ci, core in enumerate(cores):
        b, par = core // 2, core % 2
        o = res.results[ci]["out"].reshape(NPAIR, 512, D)
        ov = out[b].reshape(2 * NPAIR, 512, D)
        for j in range(NPAIR):
            ov[2 * j + par] = o[j]
    return out
```
